# Optimizing a Trainium2 kernel written in Bass

```python
import math
import jax
import jax.numpy as jnp
from jax import lax
import numpy as np

D_MODEL = 1024
BATCH = 2
SEQ = 8192
DEPTH = 4

S5_WIDTH = D_MODEL // 2
S5_GROUP = 16
S5_GROUPS = S5_WIDTH // S5_GROUP
S5_STATE = 64
DT_MIN = 1e-3
DT_MAX = 1e-1
ML_WIDTH = D_MODEL
ML_HEADS = 4
ML_HEAD_DIM = ML_WIDTH // ML_HEADS
ML_CONV = 4
ML_CHUNK = 128
D_FF = 11 * D_MODEL // 4
NORM_EPS = 1e-6
IN_COLS = S5_WIDTH + 3 * ML_WIDTH + 2 * ML_HEADS + 2 * D_MODEL

kernel_name = 'hybrid_s5_mlstm_macaron'


def rmsnorm(x, g):
    xf = x.astype(jnp.float32)
    y = xf * lax.rsqrt(jnp.mean(xf * xf, axis=-1, keepdims=True) + NORM_EPS)
    return (y * g.astype(jnp.float32)).astype(x.dtype)


def swiglu(h, wg, wu, wd):
    return (jax.nn.silu(h @ wg) * (h @ wu)) @ wd


def _complex_affine_combine(e1, e2):
    a1r, a1i, b1r, b1i = e1
    a2r, a2i, b2r, b2i = e2
    return (a2r * a1r - a2i * a1i,
            a2r * a1i + a2i * a1r,
            a2r * b1r - a2i * b1i + b2r,
            a2r * b1i + a2i * b1r + b2i)


def s5_branch(u, lam_re, lam_im, log_dt, b_re, b_im, c_re, c_im, d_skip, glu_v, glu_g):
    f32 = jnp.float32
    bsz, L, _ = u.shape
    uf = u.astype(f32).reshape(bsz, L, S5_GROUPS, S5_GROUP)
    dt = jnp.exp(log_dt.astype(f32))[:, None]
    lr = jnp.minimum(lam_re.astype(f32), -1e-4)
    li = lam_im.astype(f32)
    mag = jnp.exp(lr * dt)
    ab_re = mag * jnp.cos(li * dt)
    ab_im = mag * jnp.sin(li * dt)
    den = lr * lr + li * li
    q_re = ((ab_re - 1.0) * lr + ab_im * li) / den
    q_im = (ab_im * lr - (ab_re - 1.0) * li) / den
    br = b_re.astype(f32)
    bi = b_im.astype(f32)
    bb_re = q_re[..., None] * br - q_im[..., None] * bi
    bb_im = q_re[..., None] * bi + q_im[..., None] * br
    bu_re = jnp.einsum('blgp,gnp->blgn', uf, bb_re)
    bu_im = jnp.einsum('blgp,gnp->blgn', uf, bb_im)
    a_re = jnp.broadcast_to(ab_re, bu_re.shape)
    a_im = jnp.broadcast_to(ab_im, bu_im.shape)
    _, _, s_re, s_im = lax.associative_scan(_complex_affine_combine, (a_re, a_im, bu_re, bu_im), axis=1)
    y = (jnp.einsum('blgn,gpn->blgp', s_re, c_re.astype(f32))
         - jnp.einsum('blgn,gpn->blgp', s_im, c_im.astype(f32)))
    y = y.reshape(bsz, L, S5_WIDTH) + d_skip.astype(f32) * u.astype(f32)
    z = jax.nn.gelu(y).astype(u.dtype)
    return (z @ glu_v) * jax.nn.sigmoid(z @ glu_g)


def causal_conv(x, w, b):
    K = w.shape[0]
    L = x.shape[1]
    xp = jnp.pad(x, ((0, 0), (K - 1, 0), (0, 0)))
    y = b
    for j in range(K):
        y = y + w[j] * xp[:, j:j + L]
    return y


def mlstm_chunkwise(q, k, v, i_pre, f_pre):
    bsz, H, L, d = q.shape
    nc = L // ML_CHUNK

    def to_chunks(t):
        return jnp.moveaxis(t.reshape((bsz, H, nc, ML_CHUNK) + t.shape[3:]), 2, 0)

    qc, kc, vc = to_chunks(q), to_chunks(k), to_chunks(v)
    ic = to_chunks(i_pre)
    lfc = to_chunks(jax.nn.log_sigmoid(f_pre))
    causal = jnp.tril(jnp.ones((ML_CHUNK, ML_CHUNK), dtype=bool))

    def step(carry, inp):
        C, n, m = carry
        qb, kb, vb, ib, lfb = inp
        b = jnp.cumsum(lfb, axis=-1)
        dmat = b[..., :, None] - b[..., None, :] + ib[..., None, :]
        dmat = jnp.where(causal, dmat, -jnp.inf)
        inter = b + m[..., None]
        m_t = jnp.maximum(inter, jnp.max(dmat, axis=-1))
        w_inter = jnp.exp(inter - m_t)
        s = jnp.einsum('bhtd,bhsd->bhts', qb, kb) * jnp.exp(dmat - m_t[..., None])
        num = (w_inter[..., None] * jnp.einsum('bhtd,bhde->bhte', qb, C)
               + jnp.einsum('bhts,bhse->bhte', s, vb))
        den = w_inter * jnp.einsum('bhtd,bhd->bht', qb, n) + jnp.sum(s, axis=-1)
        h = num / jnp.maximum(jnp.abs(den), jnp.exp(-m_t))[..., None]
        g = b[..., -1]
        decay = g[..., None] - b + ib
        m_new = jnp.maximum(g + m, jnp.max(decay, axis=-1))
        w_old = jnp.exp(g + m - m_new)
        kw = kb * jnp.exp(decay - m_new[..., None])[..., None]
        C_new = w_old[..., None, None] * C + jnp.einsum('bhsd,bhse->bhde', kw, vb)
        n_new = w_old[..., None] * n + jnp.sum(kw, axis=2)
        return (C_new, n_new, m_new), h

    init = (jnp.zeros((bsz, H, d, d), jnp.float32),
            jnp.zeros((bsz, H, d), jnp.float32),
            jnp.zeros((bsz, H), jnp.float32))
    _, hs = lax.scan(step, init, (qc, kc, vc, ic, lfc))
    return jnp.moveaxis(hs, 0, 2).reshape(bsz, H, L, d)


def mlstm_branch(x_ml, v, o, if_pre, conv_w, conv_b, w_q, w_k, norm_g, skip):
    f32 = jnp.float32
    bsz, L, _ = x_ml.shape
    xc = jax.nn.silu(causal_conv(x_ml, conv_w, conv_b))
    xh = xc.reshape(bsz, L, ML_HEADS, ML_HEAD_DIM)
    q = jnp.einsum('blhd,hde->bhle', xh, w_q).astype(f32) * (ML_HEAD_DIM ** -0.5)
    k = jnp.einsum('blhd,hde->bhle', xh, w_k).astype(f32)
    vh = v.reshape(bsz, L, ML_HEADS, ML_HEAD_DIM).transpose(0, 2, 1, 3).astype(f32)
    gif = if_pre.astype(f32)
    i_pre = gif[..., :ML_HEADS].transpose(0, 2, 1)
    f_pre = gif[..., ML_HEADS:].transpose(0, 2, 1)
    h_tilde = mlstm_chunkwise(q, k, vh, i_pre, f_pre).transpose(0, 2, 1, 3)
    og = jax.nn.sigmoid(o.astype(f32)).reshape(bsz, L, ML_HEADS, ML_HEAD_DIM)
    hc = og * h_tilde
    hn = hc * lax.rsqrt(jnp.mean(hc * hc, axis=-1, keepdims=True) + NORM_EPS)
    out = hn.reshape(bsz, L, ML_WIDTH) * norm_g.astype(f32) + skip.astype(f32) * xc.astype(f32)
    return out.astype(x_ml.dtype)


def hybrid_mixer(h, w_in, b_if, lam_re, lam_im, log_dt, b_re, b_im, c_re, c_im, d_skip, glu_v, glu_g,
                 conv_w, conv_b, w_q, w_k, ml_norm, ml_skip, w_br_s5, w_br_ml, w_out):
    o0 = S5_WIDTH
    o1 = o0 + ML_WIDTH
    o2 = o1 + ML_WIDTH
    o3 = o2 + ML_WIDTH
    o4 = o3 + 2 * ML_HEADS
    proj = h @ w_in
    y_s5 = s5_branch(proj[..., :o0], lam_re, lam_im, log_dt, b_re, b_im, c_re, c_im, d_skip, glu_v, glu_g)
    y_ml = mlstm_branch(proj[..., o0:o1], proj[..., o1:o2], proj[..., o2:o3], proj[..., o3:o4] + b_if,
                        conv_w, conv_b, w_q, w_k, ml_norm, ml_skip)
    gates = jax.nn.sigmoid(proj[..., o4:].astype(jnp.float32))
    mix = (gates[..., :D_MODEL] * (y_s5 @ w_br_s5).astype(jnp.float32)
           + gates[..., D_MODEL:] * (y_ml @ w_br_ml).astype(jnp.float32))
    return mix.astype(h.dtype) @ w_out


def setup_inputs(seed: int = 0) -> dict:
    key = jax.random.key(seed)
    ks = iter(jax.random.split(key, 48))
    f32 = jnp.float32

    def nrm(shape, scale):
        return scale * jax.random.normal(next(ks), shape, f32)

    def gain(shape):
        return 1.0 + nrm(shape, 0.02)

    res_scale = (2.0 * DEPTH) ** -0.5
    n_idx = jnp.arange(S5_STATE, dtype=f32)
    x = nrm((BATCH, SEQ, D_MODEL), 1.0)
    ffn1_norm = gain((DEPTH, D_MODEL))
    ffn1_wg = nrm((DEPTH, D_MODEL, D_FF), D_MODEL ** -0.5)
    ffn1_wu = nrm((DEPTH, D_MODEL, D_FF), D_MODEL ** -0.5)
    ffn1_wd = nrm((DEPTH, D_FF, D_MODEL), res_scale * D_FF ** -0.5)
    mix_norm = gain((DEPTH, D_MODEL))
    w_in = nrm((DEPTH, D_MODEL, IN_COLS), D_MODEL ** -0.5)
    b_if = jnp.concatenate([nrm((DEPTH, ML_HEADS), 0.1),
                            jnp.linspace(3.0, 6.0, ML_HEADS, dtype=f32)[None, :] + nrm((DEPTH, ML_HEADS), 0.1)],
                           axis=-1)
    s5_lam_re = -0.5 + nrm((DEPTH, S5_GROUPS, S5_STATE), 0.01)
    s5_lam_im = jnp.pi * n_idx + nrm((DEPTH, S5_GROUPS, S5_STATE), 0.01)
    s5_log_dt = jax.random.uniform(next(ks), (DEPTH, S5_GROUPS), f32,
                                   minval=math.log(DT_MIN), maxval=math.log(DT_MAX))
    s5_b_re = nrm((DEPTH, S5_GROUPS, S5_STATE, S5_GROUP), (2.0 * S5_GROUP) ** -0.5)
    s5_b_im = nrm((DEPTH, S5_GROUPS, S5_STATE, S5_GROUP), (2.0 * S5_GROUP) ** -0.5)
    s5_c_re = nrm((DEPTH, S5_GROUPS, S5_GROUP, S5_STATE), S5_STATE ** -0.5)
    s5_c_im = nrm((DEPTH, S5_GROUPS, S5_GROUP, S5_STATE), S5_STATE ** -0.5)
    s5_d = nrm((DEPTH, S5_WIDTH), 1.0)
    s5_glu_v = nrm((DEPTH, S5_WIDTH, S5_WIDTH), S5_WIDTH ** -0.5)
    s5_glu_g = nrm((DEPTH, S5_WIDTH, S5_WIDTH), S5_WIDTH ** -0.5)
    ml_conv_w = nrm((DEPTH, ML_CONV, ML_WIDTH), ML_CONV ** -0.5)
    ml_conv_b = nrm((DEPTH, ML_WIDTH), 0.02)
    ml_wq = nrm((DEPTH, ML_HEADS, ML_HEAD_DIM, ML_HEAD_DIM), ML_HEAD_DIM ** -0.5)
    ml_wk = nrm((DEPTH, ML_HEADS, ML_HEAD_DIM, ML_HEAD_DIM), ML_HEAD_DIM ** -0.5)
    ml_norm = gain((DEPTH, ML_WIDTH))
    ml_skip = gain((DEPTH, ML_WIDTH))
    w_br_s5 = nrm((DEPTH, S5_WIDTH, D_MODEL), S5_WIDTH ** -0.5)
    w_br_ml = nrm((DEPTH, ML_WIDTH, D_MODEL), ML_WIDTH ** -0.5)
    w_out = nrm((DEPTH, D_MODEL, D_MODEL), res_scale * D_MODEL ** -0.5)
    ffn2_norm = gain((DEPTH, D_MODEL))
    ffn2_wg = nrm((DEPTH, D_MODEL, D_FF), D_MODEL ** -0.5)
    ffn2_wu = nrm((DEPTH, D_MODEL, D_FF), D_MODEL ** -0.5)
    ffn2_wd = nrm((DEPTH, D_FF, D_MODEL), res_scale * D_FF ** -0.5)
    final_norm = gain((D_MODEL,))
    return {'x': x, 'ffn1_norm': ffn1_norm, 'ffn1_wg': ffn1_wg, 'ffn1_wu': ffn1_wu, 'ffn1_wd': ffn1_wd,
            'mix_norm': mix_norm, 'w_in': w_in, 'b_if': b_if,
            's5_lam_re': s5_lam_re, 's5_lam_im': s5_lam_im, 's5_log_dt': s5_log_dt,
            's5_b_re': s5_b_re, 's5_b_im': s5_b_im, 's5_c_re': s5_c_re, 's5_c_im': s5_c_im,
            's5_d': s5_d, 's5_glu_v': s5_glu_v, 's5_glu_g': s5_glu_g,
            'ml_conv_w': ml_conv_w, 'ml_conv_b': ml_conv_b, 'ml_wq': ml_wq, 'ml_wk': ml_wk,
            'ml_norm': ml_norm, 'ml_skip': ml_skip,
            'w_br_s5': w_br_s5, 'w_br_ml': w_br_ml, 'w_out': w_out,
            'ffn2_norm': ffn2_norm, 'ffn2_wg': ffn2_wg, 'ffn2_wu': ffn2_wu, 'ffn2_wd': ffn2_wd,
            'final_norm': final_norm}


def reference(x, ffn1_norm, ffn1_wg, ffn1_wu, ffn1_wd, mix_norm, w_in, b_if,
              s5_lam_re, s5_lam_im, s5_log_dt, s5_b_re, s5_b_im, s5_c_re, s5_c_im,
              s5_d, s5_glu_v, s5_glu_g, ml_conv_w, ml_conv_b, ml_wq, ml_wk, ml_norm, ml_skip,
              w_br_s5, w_br_ml, w_out, ffn2_norm, ffn2_wg, ffn2_wu, ffn2_wd, final_norm):
    for l in range(DEPTH):
        x = x + 0.5 * swiglu(rmsnorm(x, ffn1_norm[l]), ffn1_wg[l], ffn1_wu[l], ffn1_wd[l])
        x = x + hybrid_mixer(rmsnorm(x, mix_norm[l]), w_in[l], b_if[l],
                             s5_lam_re[l], s5_lam_im[l], s5_log_dt[l], s5_b_re[l], s5_b_im[l],
                             s5_c_re[l], s5_c_im[l], s5_d[l], s5_glu_v[l], s5_glu_g[l],
                             ml_conv_w[l], ml_conv_b[l], ml_wq[l], ml_wk[l], ml_norm[l], ml_skip[l],
                             w_br_s5[l], w_br_ml[l], w_out[l])
        x = x + 0.5 * swiglu(rmsnorm(x, ffn2_norm[l]), ffn2_wg[l], ffn2_wu[l], ffn2_wd[l])
    return rmsnorm(x, final_norm)
```

```python
import contextlib
import math
import numpy as np
import ml_dtypes
import concourse.bass as bass
import concourse.mybir as mybir
from concourse.bass_utils import run_bass_kernel_spmd

F32 = mybir.dt.float32
BF16 = mybir.dt.bfloat16
AF = mybir.ActivationFunctionType
ALU = mybir.AluOpType
AX = mybir.AxisListType

D = 1024
DT = 8
SEQ = 8192
NTOK = 2048
DFF = 2816
FT = 22
DEPTH = 4
EPS = 1e-6
SBT = 1024
NSB = NTOK // SBT
TBS = 512
NTB = SBT // TBS
ENGS = ("tensor", "vector", "scalar", "gpsimd", "sync")
TWO_PI = 2.0 * math.pi


class Op:
    def __init__(self, eng, fn):
        self.eng = eng
        self.fn = fn
        self.dma_sem = None
        self.dma_val = 0
        self.has_dep = False
        self.deps = []
        self.cval = 0


class Prog:
    def __init__(self, nc):
        self.nc = nc
        self.ops = {e: [] for e in ENGS}
        self.last_writer = {}
        self.readers = {}
        self.dma_counts = {}
        self.dma_last = {}
        self.pending_barrier = {e: [] for e in ENGS}

    def add(self, eng, fn, reads=(), writes=(), extra_deps=()):
        rd, wr = [], []
        for t in reads:
            if isinstance(t, tuple) and t[0] in ("bank", "bankT"):
                wr.append(("bank", t[1]) if t[0] == "bank" else ("bankT", 0))
            else:
                rd.append(t)
        for t in writes:
            if isinstance(t, tuple) and t[0] in ("bank", "bankT"):
                wr.append(("bank", t[1]) if t[0] == "bank" else ("bankT", 0))
            else:
                wr.append(t)
        reads, writes = rd, wr
        op = Op(eng, fn)
        deps = []
        for t in reads:
            w = self.last_writer.get(t)
            if w is not None:
                deps.append(w)
        for t in writes:
            w = self.last_writer.get(t)
            if w is not None:
                deps.append(w)
            deps.extend(self.readers.get(t, ()))
        for t in reads:
            self.readers.setdefault(t, []).append(op)
        for t in writes:
            self.last_writer[t] = op
            self.readers[t] = []
        deps.extend(extra_deps)
        if self.pending_barrier[eng]:
            deps.extend(self.pending_barrier[eng])
            self.pending_barrier[eng] = []
        op.deps = [d for d in deps if d is not op]
        for d in op.deps:
            d.has_dep = True
        self.ops[eng].append(op)
        return op

    def dma(self, eng, fn, semkey, reads=(), writes=(), extra_deps=()):
        op = self.add(eng, fn, reads, writes, extra_deps)
        op.dma_sem = semkey
        self.dma_counts[semkey] = self.dma_counts.get(semkey, 0) + 1
        op.dma_val = 16 * self.dma_counts[semkey]
        self.dma_last[semkey] = op
        return op

    def barrier(self):
        lasts = []
        for e in ENGS:
            for op in reversed(self.ops[e]):
                if op.dma_sem is None:
                    lasts.append(op)
                    break
        lasts.extend(self.dma_last.values())
        for e in ENGS:
            self.pending_barrier[e] = list(lasts)

    def emit(self, final_waits=()):
        nc = self.nc
        cnt = {e: 0 for e in ENGS}
        for e in ENGS:
            for op in self.ops[e]:
                if op.dma_sem is None and op.has_dep:
                    cnt[e] += 1
                    op.cval = cnt[e]
        with contextlib.ExitStack() as st:
            sems = {}
            for e in ENGS:
                if cnt[e] > 0:
                    sems[("eng", e)] = st.enter_context(nc.semaphore("s_" + e))
            for i, k in enumerate(sorted(self.dma_counts.keys(), key=str)):
                sems[("dma", k)] = st.enter_context(nc.semaphore("d%d" % i))
            block = st.enter_context(nc.Block())

            def body_for(e):
                def body(engine):
                    waited = {}
                    for op in self.ops[e]:
                        need = {}
                        for d in op.deps:
                            if d.dma_sem is not None:
                                key, val = ("dma", d.dma_sem), d.dma_val
                            else:
                                key, val = ("eng", d.eng), d.cval
                            if need.get(key, 0) < val:
                                need[key] = val
                        for key, val in need.items():
                            if waited.get(key, 0) >= val:
                                continue
                            engine.wait_ge(sems[key], val)
                            waited[key] = val
                        ins = op.fn(engine)
                        if op.dma_sem is not None:
                            ins.then_inc(sems[("dma", op.dma_sem)], 16)
                        elif op.has_dep:
                            ins.then_inc(sems[("eng", e)], 1)
                    if e == "sync":
                        for op in final_waits:
                            engine.wait_ge(sems[("dma", op.dma_sem)], op.dma_val)
                return body

            for e in ENGS:
                if self.ops[e] or (e == "sync" and final_waits):
                    getattr(block, e)(body_for(e))


class Ctx:
    pass


def emit_norm(K, sb, gcol, tag):
    P = K.P
    for tb in range(NTB):
        c0 = sb * SBT + tb * TBS
        lc = tb * TBS
        psn = K.bank[6]
        for dt in range(DT):
            sq = K.sq[:, dt % 2, :]
            P.add("scalar", lambda e, sq=sq, dt=dt, c0=c0: e.activation(out=sq, in_=K.x[:, dt, c0:c0 + TBS], func=AF.Square),
                  reads=[("x", dt, sb, tb)], writes=[("sq", dt % 2)])
            P.add("tensor", lambda e, sq=sq, dt=dt, psn=psn: e.matmul(psn[:, :], lhsT=K.ones32[:, :], rhs=sq, start=(dt == 0), stop=(dt == DT - 1)),
                  reads=[("sq", dt % 2), "consts"], writes=[("bank", 6)])
        P.add("vector", lambda e, psn=psn: e.tensor_scalar(out=K.rstd[:, :], in0=psn[:, :], scalar1=1.0 / D, scalar2=EPS, op0=ALU.mult, op1=ALU.add),
              reads=[("bank", 6)], writes=["rstd"])
        P.add("scalar", lambda e: e.activation(out=K.rstd[:, :], in_=K.rstd[:, :], func=AF.Sqrt), reads=["rstd"], writes=["rstd"])
        P.add("vector", lambda e: e.reciprocal(out=K.rstd[:, :], in_=K.rstd[:, :]), reads=["rstd"], writes=["rstd"])
        for dt in range(DT):
            eng = "vector"
            P.add(eng, lambda e, dt=dt, c0=c0, lc=lc: e.scalar_tensor_tensor(out=K.h[:, dt, lc:lc + TBS], in0=K.x[:, dt, c0:c0 + TBS], scalar=gcol(dt),
                                                                           in1=K.rstd[:, :], op0=ALU.mult, op1=ALU.mult),
                  reads=[("x", dt, sb, tb), "rstd", "vecs"], writes=[("h", dt, tb)])


def emit_ffn(K, sb, wg, wu, wd, gcol, tag):
    P = K.P
    emit_norm(K, sb, gcol, tag)
    wg_v = wg.rearrange("(dt p) f -> p dt f", p=128)
    wu_v = wu.rearrange("(dt p) f -> p dt f", p=128)
    wd_v = wd.rearrange("(ft p) d -> p ft d", p=128)
    for ft in range(FT):
        s = K.cnt_w % 2
        K.cnt_w += 1
        P.dma("gpsimd", lambda e, s=s, ft=ft: e.dma_start(out=K.wgu[:, s, 0, :, :], in_=wg_v[:, :, ft * 128:(ft + 1) * 128]),
              ("wgu", s), writes=[("wg", s)])
        P.dma("gpsimd", lambda e, s=s, ft=ft: e.dma_start(out=K.wgu[:, s, 1, :, :], in_=wu_v[:, :, ft * 128:(ft + 1) * 128]),
              ("wgu", s), writes=[("wu", s)])
        for tb in range(NTB):
            lc = tb * TBS
            pb = K.cnt_p % 2
            K.cnt_p += 1
            pg, pu = K.bank[pb], K.bank[2 + pb]
            for dt in range(DT):
                P.add("tensor", lambda e, dt=dt, s=s, lc=lc, pg=pg: e.matmul(pg[:, :], lhsT=K.wgu[:, s, 0, dt, :], rhs=K.h[:, dt, lc:lc + TBS],
                                                                           start=(dt == 0), stop=(dt == DT - 1)),
                      reads=[("wg", s), ("h", dt, tb)], writes=[("bank", pb)])
            for dt in range(DT):
                P.add("tensor", lambda e, dt=dt, s=s, lc=lc, pu=pu: e.matmul(pu[:, :], lhsT=K.wgu[:, s, 1, dt, :], rhs=K.h[:, dt, lc:lc + TBS],
                                                                           start=(dt == 0), stop=(dt == DT - 1)),
                      reads=[("wu", s), ("h", dt, tb)], writes=[("bank", 2 + pb)])
            sg = K.tmp[:, pb, :]
            P.add("scalar", lambda e, sg=sg, pg=pg: e.activation(out=sg, in_=pg[:, :], func=AF.Silu),
                  reads=[("bank", pb)], writes=[("tmp", pb)])
            P.add("vector", lambda e, sg=sg, pu=pu, ft=ft, lc=lc: e.tensor_tensor(out=K.act[:, ft, lc:lc + TBS], in0=pu[:, :], in1=sg, op=ALU.mult),
                  reads=[("bank", 2 + pb), ("tmp", pb)], writes=[("act", ft, tb)])
    for dt in range(DT):
        s = K.cnt_d % 2
        K.cnt_d += 1
        P.dma("gpsimd", lambda e, s=s, dt=dt: e.dma_start(out=K.wd[:, s, :, :], in_=wd_v[:, :, dt * 128:(dt + 1) * 128]),
              ("wd", s), writes=[("wd", s)])
        for tb in range(NTB):
            lc = tb * TBS
            c0 = sb * SBT + lc
            pb = K.cnt_q % 2
            K.cnt_q += 1
            pd = K.bank[4 + pb]
            for ft in range(FT):
                P.add("tensor", lambda e, ft=ft, s=s, lc=lc, pd=pd: e.matmul(pd[:, :], lhsT=K.wd[:, s, ft, :], rhs=K.act[:, ft, lc:lc + TBS],
                                                                           start=(ft == 0), stop=(ft == FT - 1)),
                      reads=[("wd", s), ("act", ft, tb)], writes=[("bank", 4 + pb)])
            P.add("vector", lambda e, dt=dt, c0=c0, pd=pd: e.scalar_tensor_tensor(out=K.x[:, dt, c0:c0 + TBS], in0=pd[:, :], scalar=0.5,
                                                                                 in1=K.x[:, dt, c0:c0 + TBS], op0=ALU.mult, op1=ALU.add),
                  reads=[("bank", 4 + pb), ("x", dt, sb, tb)], writes=[("x", dt, sb, tb)])


def emit_ta_tail(K, sb, gcol, w_if, hT_out, ifp_out):
    P = K.P
    emit_norm(K, sb, gcol, "mix")
    hv = hT_out.rearrange("(dt p) t -> p dt t", p=128)
    st_ops = []
    st_ops.append(P.dma("sync", lambda e: e.dma_start(out=hv[:, :, sb * SBT:(sb + 1) * SBT], in_=K.h[:, :, :]), ("hst", 0),
                        reads=[("h", dt, tb) for dt in range(DT) for tb in range(NTB)]))
    P.dma("gpsimd", lambda e: e.dma_start(out=K.wif[:, :, :], in_=w_if.rearrange("(dt p) c -> p dt c", p=128)), ("wif", 0), writes=["wif"])
    pif = K.bank[7]
    ntt = SBT // 128
    for tt in range(ntt):
        for dt in range(DT):
            P.add("tensor", lambda e, tt=tt, dt=dt: e.matmul(pif[:, tt * 8:(tt + 1) * 8], lhsT=K.h[:, dt, tt * 128:(tt + 1) * 128], rhs=K.wif[:, dt, :],
                                                             start=(dt == 0), stop=(dt == DT - 1)),
                  reads=[("h", dt, tt // 4), "wif"], writes=[("bank", 7)])
    P.add("vector", lambda e: e.tensor_copy(out=K.ifs[:, :], in_=pif[:, 0:ntt * 8]), reads=[("bank", 7)], writes=["ifs"])
    iv = ifp_out.rearrange("(tt p) c -> p tt c", p=128)
    st_ops.append(P.dma("sync", lambda e: e.dma_start(out=iv[:, sb * ntt:(sb + 1) * ntt, :], in_=K.ifs[:, :].rearrange("p (tt c) -> p tt c", c=8)),
                        ("ifst", 0), reads=["ifs"]))
    return st_ops


def emit_tb(K, sb, W):
    P = K.P
    c0s = sb * SBT
    P.dma("sync", lambda e: e.dma_start(out=K.h[:, :, :], in_=W.hT.rearrange("(dt p) t -> p dt t", p=128)[:, :, c0s:c0s + SBT]), ("hld", 0),
          writes=[("h", dt, tb) for dt in range(DT) for tb in range(NTB)])
    P.dma("sync", lambda e: e.dma_start(out=K.z[:, :, :], in_=W.zT.rearrange("(ct p) t -> p ct t", p=128)[:, :, c0s:c0s + SBT]), ("zld", 0),
          writes=["z"])
    P.dma("sync", lambda e: e.dma_start(out=K.yml[:, :, :], in_=W.ymlT.rearrange("(ct p) t -> p ct t", p=128)[:, :, c0s:c0s + SBT]), ("yld", 0),
          writes=["yml"])
    P.dma("gpsimd", lambda e: e.dma_start(out=K.wglu[:, 0, :, :], in_=W.glu_v.rearrange("(ct p) c -> p ct c", p=128)), ("wglu", 0), writes=["wglu0"])
    P.dma("gpsimd", lambda e: e.dma_start(out=K.wglu[:, 1, :, :], in_=W.glu_g.rearrange("(ct p) c -> p ct c", p=128)), ("wglu", 0), writes=["wglu1"])
    for co in range(4):
        for tb in range(NTB):
            lc = tb * TBS
            pb = K.cnt_p % 2
            K.cnt_p += 1
            pv, pg = K.bank[pb], K.bank[2 + pb]
            for ci in range(4):
                P.add("tensor", lambda e, ci=ci, co=co, lc=lc, pv=pv: e.matmul(pv[:, :], lhsT=K.wglu[:, 0, ci, co * 128:(co + 1) * 128], rhs=K.z[:, ci, lc:lc + TBS],
                                                                             start=(ci == 0), stop=(ci == 3)),
                      reads=["wglu0", "z"], writes=[("bank", pb)])
            for ci in range(4):
                P.add("tensor", lambda e, ci=ci, co=co, lc=lc, pg=pg: e.matmul(pg[:, :], lhsT=K.wglu[:, 1, ci, co * 128:(co + 1) * 128], rhs=K.z[:, ci, lc:lc + TBS],
                                                                             start=(ci == 0), stop=(ci == 3)),
                      reads=["wglu1", "z"], writes=[("bank", 2 + pb)])
            sg = K.tmp[:, pb, :]
            P.add("scalar", lambda e, sg=sg, pg=pg: e.activation(out=sg, in_=pg[:, :], func=AF.Sigmoid), reads=[("bank", 2 + pb)], writes=[("tmp", pb)])
            P.add("vector", lambda e, sg=sg, pv=pv, co=co, lc=lc: e.tensor_tensor(out=K.ys5[:, co, lc:lc + TBS], in0=pv[:, :], in1=sg, op=ALU.mult),
                  reads=[("bank", pb), ("tmp", pb)], writes=[("ys5", co, tb)])
    wbs_v = W.w_br_s5.rearrange("(ct p) d -> p ct d", p=128)
    wbm_v = W.w_br_ml.rearrange("(ct p) d -> p ct d", p=128)
    wg_v = W.w_g.rearrange("(ct p) d -> p ct d", p=128)
    for dt in range(DT):
        s = K.cnt_b % 2
        K.cnt_b += 1
        P.dma("gpsimd", lambda e, s=s, dt=dt: e.dma_start(out=K.wbr[:, s, 0:4, :], in_=wbs_v[:, :, dt * 128:(dt + 1) * 128]), ("wbr", s), writes=[("wbr0", s)])
        P.dma("gpsimd", lambda e, s=s, dt=dt: e.dma_start(out=K.wbr[:, s, 4:12, :], in_=wbm_v[:, :, dt * 128:(dt + 1) * 128]), ("wbr", s), writes=[("wbr1", s)])
        P.dma("gpsimd", lambda e, s=s, dt=dt: e.dma_start(out=K.wbr[:, s, 12:20, :], in_=wg_v[:, :, dt * 128:(dt + 1) * 128]), ("wbr", s), writes=[("wbr2", s)])
        P.dma("gpsimd", lambda e, s=s, dt=dt: e.dma_start(out=K.wbr[:, s, 20:28, :], in_=wg_v[:, :, D + dt * 128:D + (dt + 1) * 128]), ("wbr", s), writes=[("wbr3", s)])
        for tb in range(NTB):
            lc = tb * TBS
            b0, b1, b2, b3 = K.bank[0], K.bank[1], K.bank[2], K.bank[3]
            for ci in range(4):
                P.add("tensor", lambda e, ci=ci, s=s, lc=lc: e.matmul(b0[:, :], lhsT=K.wbr[:, s, ci, :], rhs=K.ys5[:, ci, lc:lc + TBS], start=(ci == 0), stop=(ci == 3)),
                      reads=[("wbr0", s), ("ys5", ci, tb)], writes=[("bank", 0)])
            for ci in range(8):
                P.add("tensor", lambda e, ci=ci, s=s, lc=lc: e.matmul(b1[:, :], lhsT=K.wbr[:, s, 4 + ci, :], rhs=K.yml[:, ci, lc:lc + TBS], start=(ci == 0), stop=(ci == 7)),
                      reads=[("wbr1", s), "yml"], writes=[("bank", 1)])
            for ci in range(8):
                P.add("tensor", lambda e, ci=ci, s=s, lc=lc: e.matmul(b2[:, :], lhsT=K.wbr[:, s, 12 + ci, :], rhs=K.h[:, ci, lc:lc + TBS], start=(ci == 0), stop=(ci == 7)),
                      reads=[("wbr2", s), ("h", ci, tb)], writes=[("bank", 2)])
            for ci in range(8):
                P.add("tensor", lambda e, ci=ci, s=s, lc=lc: e.matmul(b3[:, :], lhsT=K.wbr[:, s, 20 + ci, :], rhs=K.h[:, ci, lc:lc + TBS], start=(ci == 0), stop=(ci == 7)),
                      reads=[("wbr3", s), ("h", ci, tb)], writes=[("bank", 3)])
            P.add("scalar", lambda e: e.activation(out=K.tmp[:, 0, :], in_=b2[:, :], func=AF.Sigmoid), reads=[("bank", 2)], writes=[("tmp", 0)])
            P.add("scalar", lambda e: e.activation(out=K.tmp[:, 1, :], in_=b3[:, :], func=AF.Sigmoid), reads=[("bank", 3)], writes=[("tmp", 1)])
            P.add("vector", lambda e: e.tensor_tensor(out=K.tmp[:, 0, :], in0=b0[:, :], in1=K.tmp[:, 0, :], op=ALU.mult), reads=[("bank", 0), ("tmp", 0)], writes=[("tmp", 0)])
            P.add("vector", lambda e: e.tensor_tensor(out=K.tmp[:, 1, :], in0=b1[:, :], in1=K.tmp[:, 1, :], op=ALU.mult), reads=[("bank", 1), ("tmp", 1)], writes=[("tmp", 1)])
            P.add("vector", lambda e, dt=dt, lc=lc: e.tensor_tensor(out=K.mix[:, dt, lc:lc + TBS], in0=K.tmp[:, 0, :], in1=K.tmp[:, 1, :], op=ALU.add),
                  reads=[("tmp", 0), ("tmp", 1)], writes=[("mix", dt, tb)])
    wo_v = W.w_out.rearrange("(ct p) d -> p ct d", p=128)
    for dt in range(DT):
        s = K.cnt_b % 2
        K.cnt_b += 1
        P.dma("gpsimd", lambda e, s=s, dt=dt: e.dma_start(out=K.wbr[:, s, 0:8, :], in_=wo_v[:, :, dt * 128:(dt + 1) * 128]), ("wbr", s),
              writes=[("wbr0", s), ("wbr1", s)])
        for tb in range(NTB):
            lc = tb * TBS
            c0 = c0s + lc
            pb = K.cnt_q % 2
            K.cnt_q += 1
            pd = K.bank[4 + pb]
            for ci in range(8):
                P.add("tensor", lambda e, ci=ci, s=s, lc=lc, pd=pd: e.matmul(pd[:, :], lhsT=K.wbr[:, s, ci, :], rhs=K.mix[:, ci, lc:lc + TBS], start=(ci == 0), stop=(ci == 7)),
                      reads=[("wbr0", s), ("wbr1", s), ("mix", ci, tb)], writes=[("bank", 4 + pb)])
            P.add("vector", lambda e, dt=dt, c0=c0, pd=pd: e.tensor_tensor(out=K.x[:, dt, c0:c0 + TBS], in0=pd[:, :], in1=K.x[:, dt, c0:c0 + TBS], op=ALU.add),
                  reads=[("bank", 4 + pb), ("x", dt, sb, tb)], writes=[("x", dt, sb, tb)])


def emit_final(K, sb, gcol, outT):
    P = K.P
    emit_norm_f32(K, sb, gcol)
    ov = outT.rearrange("(dt p) t -> p dt t", p=128)
    return [P.dma("sync", lambda e: e.dma_start(out=ov[:, :, sb * SBT:(sb + 1) * SBT], in_=K.x[:, :, sb * SBT:(sb + 1) * SBT]), ("ost", 0),
                  reads=[("x", dt, sb, tb) for dt in range(DT) for tb in range(NTB)])]


def emit_norm_f32(K, sb, gcol):
    P = K.P
    for tb in range(NTB):
        c0 = sb * SBT + tb * TBS
        psn = K.bank[6]
        for dt in range(DT):
            sq = K.sq[:, dt % 2, :]
            P.add("scalar", lambda e, sq=sq, dt=dt, c0=c0: e.activation(out=sq, in_=K.x[:, dt, c0:c0 + TBS], func=AF.Square),
                  reads=[("x", dt, sb, tb)], writes=[("sq", dt % 2)])
            P.add("tensor", lambda e, sq=sq, dt=dt: e.matmul(psn[:, :], lhsT=K.ones32[:, :], rhs=sq, start=(dt == 0), stop=(dt == DT - 1)),
                  reads=[("sq", dt % 2), "consts"], writes=[("bank", 6)])
        P.add("vector", lambda e: e.tensor_scalar(out=K.rstd[:, :], in0=psn[:, :], scalar1=1.0 / D, scalar2=EPS, op0=ALU.mult, op1=ALU.add),
              reads=[("bank", 6)], writes=["rstd"])
        P.add("scalar", lambda e: e.activation(out=K.rstd[:, :], in_=K.rstd[:, :], func=AF.Sqrt), reads=["rstd"], writes=["rstd"])
        P.add("vector", lambda e: e.reciprocal(out=K.rstd[:, :], in_=K.rstd[:, :]), reads=["rstd"], writes=["rstd"])
        for dt in range(DT):
            eng = "vector"
            P.add(eng, lambda e, dt=dt, c0=c0: e.scalar_tensor_tensor(out=K.x[:, dt, c0:c0 + TBS], in0=K.x[:, dt, c0:c0 + TBS], scalar=gcol(dt),
                                                                     in1=K.rstd[:, :], op0=ALU.mult, op1=ALU.mult),
                  reads=[("x", dt, sb, tb), "rstd", "vecs"], writes=[("x", dt, sb, tb)])


def build_T(do_tb, do_ta, do_final):
    nc = bass.Bass("TRN2", target_bir_lowering=False)
    dr = lambda name, shape, dt, kind="ExternalInput": nc.dram_tensor(name, shape, dt, kind=kind).ap()
    W = Ctx()
    xin = dr("xin", [D, NTOK], F32)
    vecs_d = dr("vecs", [128, 40], F32)
    ones_d = dr("ones32", [128, 128], F32)
    if do_tb:
        W.hT = dr("hT_in", [D, NTOK], BF16)
        W.zT = dr("zT_in", [512, NTOK], BF16)
        W.ymlT = dr("ymlT_in", [D, NTOK], BF16)
        W.glu_v = dr("glu_v", [512, 512], F32)
        W.glu_g = dr("glu_g", [512, 512], F32)
        W.w_br_s5 = dr("w_br_s5", [512, D], F32)
        W.w_br_ml = dr("w_br_ml", [D, D], F32)
        W.w_g = dr("w_g", [D, 2 * D], F32)
        W.w_out = dr("w_out", [D, D], F32)
        f2 = (dr("f2_wg", [D, DFF], F32), dr("f2_wu", [D, DFF], F32), dr("f2_wd", [DFF, D], F32))
    if do_ta:
        f1 = (dr("f1_wg", [D, DFF], F32), dr("f1_wu", [D, DFF], F32), dr("f1_wd", [DFF, D], F32))
        w_if = dr("w_if", [D, 8], F32)
        hT_out = dr("hT_out", [D, NTOK], BF16, "ExternalOutput")
        ifp_out = dr("ifp_out", [NTOK, 8], F32, "ExternalOutput")
    xout = dr("xout", [D, NTOK], F32, "ExternalOutput")

    with contextlib.ExitStack() as st:
        K = Ctx()
        K.nc = nc
        K.P = P = Prog(nc)
        sb_t = lambda name, shape, dt: st.enter_context(nc.sbuf_tensor(name, shape, dt))
        K.x = sb_t("x", [128, DT, NTOK], F32)
        K.h = sb_t("h", [128, DT, SBT], BF16)
        K.act = sb_t("act", [128, 24, SBT], BF16)
        K.z = K.act[:, 0:4, :]
        K.ys5 = K.act[:, 4:8, :]
        K.yml = K.act[:, 8:16, :]
        K.mix = K.act[:, 16:24, :]
        K.sq = sb_t("sq", [128, 2, TBS], F32)
        K.tmp = sb_t("tmp", [128, 2, TBS], F32)
        K.rstd = sb_t("rstd", [128, TBS], F32)
        K.wgu = sb_t("wgu", [128, 2, 2, DT, 128], BF16)
        K.wd = sb_t("wd", [128, 2, FT, 128], BF16)
        K.wbr = sb_t("wbr", [128, 2, 28, 128], BF16)
        K.wglu = sb_t("wglu", [128, 2, 4, 512], BF16)
        K.wif = sb_t("wif", [128, DT, 8], BF16)
        K.ifs = sb_t("ifs", [128, (SBT // 128) * 8], F32)
        K.vecs = sb_t("vecs_s", [128, 40], F32)
        K.ones32 = sb_t("ones_s", [128, 128], F32)
        K.bank = [st.enter_context(nc.psum_tensor("bank%d" % i, [128, 512], F32)) for i in range(8)]
        K.cnt_w = K.cnt_p = K.cnt_d = K.cnt_q = K.cnt_b = 0

        P.dma("sync", lambda e: e.dma_start(out=K.vecs[:, :], in_=vecs_d), ("c", 0), writes=["vecs"])
        P.dma("sync", lambda e: e.dma_start(out=K.ones32[:, :], in_=ones_d), ("c", 1), writes=["consts"])
        xv = xin.rearrange("(dt p) t -> p dt t", p=128)
        for sb in range(NSB):
            P.dma("sync", lambda e, sb=sb: e.dma_start(out=K.x[:, :, sb * SBT:(sb + 1) * SBT], in_=xv[:, :, sb * SBT:(sb + 1) * SBT]), ("xld", sb),
                  writes=[("x", dt, sb, tb) for dt in range(DT) for tb in range(NTB)])
        finals = []
        for sb in range(NSB):
            if do_tb:
                P.barrier()
                emit_tb(K, sb, W)
                P.barrier()
                emit_ffn(K, sb, f2[0], f2[1], f2[2], lambda dt: K.vecs[:, dt:dt + 1], "f2")
            if do_ta:
                emit_ffn(K, sb, f1[0], f1[1], f1[2], lambda dt: K.vecs[:, 8 + dt:9 + dt], "f1")
                finals += emit_ta_tail(K, sb, lambda dt: K.vecs[:, 16 + dt:17 + dt], w_if, hT_out, ifp_out)
                xo = xout.rearrange("(dt p) t -> p dt t", p=128)
                finals.append(P.dma("sync", lambda e, sb=sb, xo=xo: e.dma_start(out=xo[:, :, sb * SBT:(sb + 1) * SBT], in_=K.x[:, :, sb * SBT:(sb + 1) * SBT]),
                                    ("ost", 0), reads=[("x", dt, sb, tb) for dt in range(DT) for tb in range(NTB)]))
            if do_final:
                finals += emit_final(K, sb, lambda dt: K.vecs[:, 24 + dt:25 + dt], xout)
        P.emit(final_waits=finals)
    return nc


def _cols(v):
    return np.ascontiguousarray(np.asarray(v, np.float32).reshape(8, 128).T)


_NC_CACHE = {}


def _get_T(do_tb, do_ta, do_final):
    return build_T(do_tb, do_ta, do_final)


def _run(nc, in_maps):
    res = run_bass_kernel_spmd(nc, in_maps, core_ids=list(range(8)))
    return res.results


def run_T(I, l, x_sh, mixer_out, do_tb, do_ta, do_final):
    ones = np.ones((128, 128), np.float32)
    in_maps = []
    lta = l + 1 if do_tb else 0
    for c in range(8):
        vecs = np.zeros((128, 40), np.float32)
        m = {"xin": x_sh[c], "ones32": ones}
        if do_tb:
            vecs[:, 0:8] = _cols(I["ffn2_norm"][l])
            hT, zT, ymlT = mixer_out
            m.update({"hT_in": hT[c], "zT_in": zT[c], "ymlT_in": ymlT[c],
                      "glu_v": I["s5_glu_v"][l], "glu_g": I["s5_glu_g"][l], "w_br_s5": I["w_br_s5"][l], "w_br_ml": I["w_br_ml"][l],
                      "w_g": np.ascontiguousarray(I["w_in"][l][:, 3592:]), "w_out": I["w_out"][l],
                      "f2_wg": I["ffn2_wg"][l], "f2_wu": I["ffn2_wu"][l], "f2_wd": I["ffn2_wd"][l]})
        if do_ta:
            vecs[:, 8:16] = _cols(I["ffn1_norm"][lta])
            vecs[:, 16:24] = _cols(I["mix_norm"][lta])
            m.update({"f1_wg": I["ffn1_wg"][lta], "f1_wu": I["ffn1_wu"][lta], "f1_wd": I["ffn1_wd"][lta],
                      "w_if": np.ascontiguousarray(I["w_in"][lta][:, 3584:3592])})
        if do_final:
            vecs[:, 24:32] = _cols(I["final_norm"])
        m["vecs"] = vecs
        in_maps.append(m)
    nc = _get_T(do_tb, do_ta, do_final)
    return _run(nc, in_maps)


def kernel(**inputs):
    I = {k: np.asarray(v) for k, v in inputs.items()}
    x = I["x"]
    x_sh = [np.ascontiguousarray(x[c // 4, (c % 4) * NTOK:(c % 4 + 1) * NTOK, :].T) for c in range(8)]
    res = run_T(I, 0, x_sh, None, False, True, False)
    for l in range(DEPTH):
        x_sh = [res[c]["xout"] for c in range(8)]
        hT_sh = [res[c]["hT_out"] for c in range(8)]
        ifp_sh = [res[c]["ifp_out"] for c in range(8)]
        mres = run_M(I, l, hT_sh, ifp_sh)
        zT_in, ymlT_in = [], []
        for c in range(8):
            b, r = c // 4, c % 4
            sl = slice(r * NTOK, (r + 1) * NTOK)
            zT_in.append(np.ascontiguousarray(np.concatenate([mres[4 * b + q]["zT"][:, sl] for q in range(4)], axis=0)))
            ymlT_in.append(np.ascontiguousarray(np.concatenate([mres[4 * b + q]["ymlT"][:, sl] for q in range(4)], axis=0)))
        last = (l == DEPTH - 1)
        res = run_T(I, l, x_sh, (hT_sh, zT_in, ymlT_in), True, not last, last)
    out = np.zeros((2, SEQ, D), np.float32)
    for c in range(8):
        out[c // 4, (c % 4) * NTOK:(c % 4 + 1) * NTOK, :] = res[c]["xout"].T
    return out


NBLK = SEQ // TBS
NCH = SEQ // 128
import os
MLEVEL = int(os.environ.get('MLEVEL', '9'))
MSUB = int(os.environ.get('MSUB', '9'))


def build_M():
    nc = bass.Bass("TRN2", target_bir_lowering=False)
    dr = lambda name, shape, dt, kind="ExternalInput": nc.dram_tensor(name, shape, dt, kind=kind).ap()
    hT = dr("hT", [D, SEQ], BF16)
    ifg_d = dr("ifg", [128, NCH, 2], F32)
    w_m = dr("w_m", [D, 896], F32)
    wq_d = dr("wq", [256, 256], F32)
    wk_d = dr("wk", [256, 256], F32)
    convw_d = dr("convw", [128, 2, 4], F32)
    vm_d = dr("vm", [128, 16], F32)
    lamC_d = dr("lamC", [128, 12], F32)
    lamR_d = dr("lamR", [128, 3, 512], F32)
    BT_d = dr("BT", [128, 2, 512], F32)
    CT_d = dr("CT", [128, 2, 4, 128], F32)
    cf_d = dr("cf32", [128, 4, 128], F32)
    cb_d = dr("cb16", [128, 128], BF16)
    iota_d = dr("iota", [128, 512], F32)
    zT_o = dr("zT", [128, SEQ], BF16, "ExternalOutput")
    ymlT_o = dr("ymlT", [256, SEQ], BF16, "ExternalOutput")

    with contextlib.ExitStack() as st:
        P = Prog(nc)
        T = lambda name, shape, dt: st.enter_context(nc.sbuf_tensor(name, shape, dt))
        wm = T("wm", [128, DT, 896], BF16)
        wq = T("wq_s", [128, 2, 256], BF16)
        wk = T("wk_s", [128, 2, 256], BF16)
        hblk = T("hblk", [128, 2, DT, TBS], BF16)
        convw = T("convw_s", [128, 2, 4], F32)
        vm = T("vm_s", [128, 16], F32)
        lamC = T("lamC_s", [128, 12], F32)
        cf = T("cf_s", [128, 4, 128], F32)
        cb = T("cb_s", [128, 128], BF16)
        iota = T("iota_s", [128, 512], F32)
        R = T("R", [128, 12, 512], F32)
        CTf = T("CTf", [128, 2, 4, 128], F32)
        CTb = T("CTb", [128, 2, 4, 128], BF16)
        BbT = T("BbT", [128, 2, 512], BF16)
        cosT = T("cosT", [128, 4, 512], F32)
        sinT = T("sinT", [128, 4, 512], F32)
        rT = T("rT", [128, 4, 512], F32)
        cs = T("cs", [128, 32], F32)
        ubf = T("ubf", [128, TBS], BF16)
        du = T("du", [128, TBS], F32)
        t1 = T("t1", [128, 2, TBS], F32)
        t2 = T("t2", [128, 2, TBS], F32)
        zin = T("zin", [128, 2, TBS], F32)
        zr = T("zr", [128, 4, TBS], F32)
        zi = T("zi", [128, 4, TBS], F32)
        init = T("init", [128, 4, 4], F32)
        pt = T("pt", [128, 4, TBS], F32)
        sre = T("sre", [128, 2, TBS], BF16)
        sim_ = T("sim", [128, 2, TBS], BF16)
        yy = T("yy", [128, TBS], F32)
        g1 = T("g1", [128, TBS], F32)
        g2 = T("g2", [128, TBS], F32)
        zout = T("zout", [128, 2, TBS], BF16)
        xpad = T("xpad", [128, 2, TBS + 3], F32)
        cacc = T("cacc", [128, 2, TBS], F32)
        xc32 = T("xc32", [128, 2, TBS], F32)
        xcb = T("xcb", [128, 2, TBS], BF16)
        skx = T("skx", [128, 2, TBS], F32)
        qT = T("qT", [128, 2, TBS], BF16)
        kT = T("kT", [128, 2, TBS], BF16)
        kp = T("kp", [128, 4, 256], BF16)
        vext = T("vext", [128, 4, 257], BF16)
        og = T("og", [128, 4, 256], F32)
        Sm = T("Sm", [128, 2, 128], BF16)
        hc = T("hc", [128, 256], F32)
        hsq = T("hsq", [128, 256], F32)
        hn = T("hn", [128, 256], BF16)
        sm = T("sm", [128, 8], F32)
        Cst = T("Cst", [128, 2, 256], F32)
        nst = T("nst", [128, 2], F32)
        Cwb = T("Cwb", [128, 2, 257], BF16)
        ymlo = T("ymlo", [128, 2, 2, TBS], BF16)
        ifs = T("ifs", [128, NCH, 2], F32)
        gsc = T("gsc", [128, 8, NCH], F32)
        grow = T("grow", [128, 4, NCH + 1], F32)
        abc = T("abc", [64, 128], F32)
        acol = T("acol", [64, 1], F32)
        bank = [st.enter_context(nc.psum_tensor("bank%d" % i, [128, 512], F32)) for i in range(7)]
        bankT = st.enter_context(nc.psum_tensor("bankT", [128, 1024], BF16))

        ident = cf[:, 0, :]
        tri = cf[:, 1, :]
        ones = cf[:, 2, :]

        ld = lambda eng, out, in_, key, tok: P.dma(eng, lambda e: e.dma_start(out=out, in_=in_), key, writes=[tok])
        wmv = w_m.rearrange("(dt p) c -> p dt c", p=128)
        for k in range(7):
            P.dma("gpsimd", lambda e, k=k: e.dma_start(out=wm[:, :, k * 128:(k + 1) * 128], in_=wmv[:, :, k * 128:(k + 1) * 128]), ("w", 0), writes=["wm"])
        for k in range(2):
            P.dma("gpsimd", lambda e, k=k: e.dma_start(out=wq[:, :, k * 128:(k + 1) * 128], in_=wq_d.rearrange("(dt p) c -> p dt c", p=128)[:, :, k * 128:(k + 1) * 128]),
                  ("w", 1), writes=["wq"])
            P.dma("gpsimd", lambda e, k=k: e.dma_start(out=wk[:, :, k * 128:(k + 1) * 128], in_=wk_d.rearrange("(dt p) c -> p dt c", p=128)[:, :, k * 128:(k + 1) * 128]),
                  ("w", 2), writes=["wk"])
        for k in range(4):
            P.dma("gpsimd", lambda e, k=k: e.dma_start(out=CTb[:, 0, k, :], in_=CT_d[:, 0, k, :]), ("w", 3), writes=["CTb0"])
        ld("sync", CTf[:, :, :, :], CT_d, ("w", 4), "CTf")
        ld("sync", convw[:, :, :], convw_d, ("w", 5), "convw")
        ld("sync", vm[:, :], vm_d, ("w", 6), "vm")
        ld("sync", lamC[:, :], lamC_d, ("w", 7), "lamC")
        ld("sync", cf[:, :, :], cf_d, ("w", 8), "cf")
        ld("sync", cb[:, :], cb_d, ("w", 9), "cb")
        ld("sync", iota[:, :], iota_d, ("w", 10), "iota")
        ld("sync", R[:, 0:3, :], lamR_d, ("w", 11), "R012")
        ld("sync", R[:, 8:10, :], BT_d, ("w", 12), "R89")
        ld("sync", ifs[:, :, :], ifg_d, ("w", 13), "ifs")

        V = lambda fn, r=(), w=(): P.add("vector", fn, reads=r, writes=w)
        A = lambda fn, r=(), w=(): P.add("scalar", fn, reads=r, writes=w)
        G = lambda fn, r=(), w=(): P.add("vector" if os.environ.get("POOL2V") else "gpsimd", fn, reads=r, writes=w)
        PE = lambda fn, r=(), w=(): P.add("tensor", fn, reads=r, writes=w)

        RI = T("RI", [128, 512], mybir.dt.int32)

        def sincos(out_s, out_c, ang, tmpa, tmpb, rtoks, wtoks, ttoks):
            n = tmpa.shape[-1]
            it = RI[:, 0:n]
            ta, tb = ttoks
            V(lambda e: e.tensor_scalar(out=tmpa, in0=ang, scalar1=1.0 / TWO_PI, scalar2=None, op0=ALU.mult), rtoks, [ta])
            for k, (o_ap, wt) in enumerate(((out_s, wtoks[0]), (out_c, wtoks[1]))):
                if k == 1:
                    V(lambda e: e.tensor_scalar(out=tmpa, in0=tmpa, scalar1=0.25, scalar2=None, op0=ALU.add), [ta], [ta])
                V(lambda e: e.tensor_copy(out=it, in_=tmpa), [ta], ["RI"])
                V(lambda e: e.tensor_copy(out=tmpb, in_=it), ["RI"], [tb])
                V(lambda e: e.tensor_tensor(out=tmpb, in0=tmpa, in1=tmpb, op=ALU.subtract), [ta, tb], [tb])
                A(lambda e, o_ap=o_ap: e.activation(out=o_ap, in_=tmpb, func=AF.Sin, scale=TWO_PI), [tb], [wt])

        V(lambda e: e.memset(zr[:, :, :], 0.0), [], ["zr_all"])
        A(lambda e: e.activation(out=cs[:, 0:4], in_=lamC[:, 8:12], func=AF.Exp), ["lamC"], ["cs_dt"])
        V(lambda e: e.tensor_scalar(out=cs[:, 4:8], in0=lamC[:, 0:4], scalar1=-1e-4, scalar2=None, op0=ALU.min), ["lamC"], ["cs_lr"])
        V(lambda e: e.tensor_tensor(out=cs[:, 28:32], in0=cs[:, 4:8], in1=cs[:, 0:4], op=ALU.mult), ["cs_lr", "cs_dt"], ["cs_tmp"])
        A(lambda e: e.activation(out=cs[:, 8:12], in_=cs[:, 28:32], func=AF.Exp), ["cs_tmp"], ["cs_mag"])
        V(lambda e: e.tensor_tensor(out=cs[:, 12:16], in0=lamC[:, 4:8], in1=cs[:, 0:4], op=ALU.mult), ["lamC", "cs_dt"], ["cs_th"])
        for j in range(4):
            V(lambda e, j=j: e.tensor_scalar(out=R[:, 10, :], in0=iota[:, :], scalar1=cs[:, 12 + j:13 + j], scalar2=None, op0=ALU.mult), ["iota", "cs_th"], ["R10"])
            sincos(sinT[:, j, :], cosT[:, j, :], R[:, 10, :], R[:, 11, :], R[:, 3, :], ["R10"], [("sinT", j), ("cosT", j)], ["R11", "R3"])
            V(lambda e, j=j: e.memset(rT[:, j, :], 1.0), [], [("rT", j)])
            V(lambda e, j=j: e.tensor_scalar(out=rT[:, j, :], in0=rT[:, j, :], scalar1=cs[:, 8 + j:9 + j], scalar2=None, op0=ALU.mult),
              [("rT", j), "cs_mag"], [("rT", j)])
        V(lambda e: e.tensor_scalar(out=cs[:, 28:32], in0=cs[:, 12:16], scalar1=float(TBS), scalar2=None, op0=ALU.mult), ["cs_th", "cs_tmp"], ["cs_tmp"])
        sincos(cs[:, 20:24], cs[:, 16:20], cs[:, 28:32], R[:, 11, 0:4], R[:, 3, 0:4], ["cs_tmp"], ["cs_Es", "cs_Ec"], ["R11", "R3"])
        V(lambda e: e.tensor_scalar(out=cs[:, 24:28], in0=cs[:, 20:24], scalar1=-1.0, scalar2=None, op0=ALU.mult), ["cs_Es"], ["cs_nEs"])
        A(lambda e: e.activation(out=R[:, 2, :], in_=R[:, 2, :], func=AF.Exp), ["R012"], ["R2"])
        V(lambda e: e.tensor_scalar(out=R[:, 0, :], in0=R[:, 0, :], scalar1=-1e-4, scalar2=None, op0=ALU.min), ["R012"], ["R0"])
        V(lambda e: e.tensor_tensor(out=R[:, 4, :], in0=R[:, 0, :], in1=R[:, 2, :], op=ALU.mult), ["R0", "R2"], ["R4"])
        A(lambda e: e.activation(out=R[:, 4, :], in_=R[:, 4, :], func=AF.Exp), ["R4"], ["R4"])
        V(lambda e: e.tensor_tensor(out=R[:, 5, :], in0=R[:, 1, :], in1=R[:, 2, :], op=ALU.mult), ["R012", "R2"], ["R5"])
        sincos(R[:, 6, :], R[:, 7, :], R[:, 5, :], R[:, 11, :], R[:, 3, :], ["R5"], ["R6", "R7"], ["R11", "R3"])
        V(lambda e: e.tensor_tensor(out=R[:, 6, :], in0=R[:, 6, :], in1=R[:, 4, :], op=ALU.mult), ["R6", "R4"], ["R6"])
        V(lambda e: e.tensor_tensor(out=R[:, 7, :], in0=R[:, 7, :], in1=R[:, 4, :], op=ALU.mult), ["R7", "R4"], ["R7"])
        V(lambda e: e.tensor_scalar(out=R[:, 7, :], in0=R[:, 7, :], scalar1=-1.0, scalar2=None, op0=ALU.add), ["R7"], ["R7"])
        V(lambda e: e.tensor_tensor(out=R[:, 4, :], in0=R[:, 0, :], in1=R[:, 0, :], op=ALU.mult), ["R0", "R4"], ["R4"])
        V(lambda e: e.tensor_tensor(out=R[:, 5, :], in0=R[:, 1, :], in1=R[:, 1, :], op=ALU.mult), ["R012", "R5"], ["R5"])
        V(lambda e: e.tensor_tensor(out=R[:, 4, :], in0=R[:, 4, :], in1=R[:, 5, :], op=ALU.add), ["R4", "R5"], ["R4"])
        V(lambda e: e.reciprocal(out=R[:, 4, :], in_=R[:, 4, :]), ["R4"], ["R4"])
        V(lambda e: e.tensor_tensor(out=R[:, 5, :], in0=R[:, 7, :], in1=R[:, 0, :], op=ALU.mult), ["R7", "R0", "R5"], ["R5"])
        V(lambda e: e.tensor_tensor(out=R[:, 11, :], in0=R[:, 6, :], in1=R[:, 1, :], op=ALU.mult), ["R6", "R012", "R11"], ["R11"])
        V(lambda e: e.tensor_tensor(out=R[:, 5, :], in0=R[:, 5, :], in1=R[:, 11, :], op=ALU.add), ["R5", "R11"], ["R5"])
        V(lambda e: e.tensor_tensor(out=R[:, 5, :], in0=R[:, 5, :], in1=R[:, 4, :], op=ALU.mult), ["R5", "R4"], ["R5"])
        V(lambda e: e.tensor_tensor(out=R[:, 11, :], in0=R[:, 6, :], in1=R[:, 0, :], op=ALU.mult), ["R6", "R0", "R11"], ["R11"])
        V(lambda e: e.tensor_tensor(out=R[:, 3, :], in0=R[:, 7, :], in1=R[:, 1, :], op=ALU.mult), ["R7", "R012", "R3"], ["R3"])
        V(lambda e: e.tensor_tensor(out=R[:, 11, :], in0=R[:, 11, :], in1=R[:, 3, :], op=ALU.subtract), ["R11", "R3"], ["R11"])
        V(lambda e: e.tensor_tensor(out=R[:, 11, :], in0=R[:, 11, :], in1=R[:, 4, :], op=ALU.mult), ["R11", "R4"], ["R11"])
        V(lambda e: e.tensor_tensor(out=R[:, 3, :], in0=R[:, 5, :], in1=R[:, 8, :], op=ALU.mult), ["R5", "R89", "R3"], ["R3"])
        V(lambda e: e.tensor_tensor(out=R[:, 6, :], in0=R[:, 11, :], in1=R[:, 9, :], op=ALU.mult), ["R11", "R89", "R6"], ["R6"])
        V(lambda e: e.tensor_tensor(out=BbT[:, 0, :], in0=R[:, 3, :], in1=R[:, 6, :], op=ALU.subtract), ["R3", "R6"], ["BbT0"])
        V(lambda e: e.tensor_tensor(out=R[:, 3, :], in0=R[:, 5, :], in1=R[:, 9, :], op=ALU.mult), ["R5", "R89", "R3"], ["R3"])
        V(lambda e: e.tensor_tensor(out=R[:, 6, :], in0=R[:, 11, :], in1=R[:, 8, :], op=ALU.mult), ["R11", "R89", "R6"], ["R6"])
        V(lambda e: e.tensor_tensor(out=BbT[:, 1, :], in0=R[:, 3, :], in1=R[:, 6, :], op=ALU.add), ["R3", "R6"], ["BbT1"])
        A(lambda e: e.activation(out=CTb[:, 1, :, :], in_=CTf[:, 1, :, :], func=AF.Copy, scale=-1.0), ["CTf"], ["CTb1"])

        if MLEVEL == 0:
            P.emit()
            return nc
        PE_real = PE
        if os.environ.get('MSKIPG'):
            PE = lambda fn, r=(), w=(): None
        V(lambda e: e.tensor_scalar(out=gsc[:, 7, :], in0=ifs[:, :, 1], scalar1=vm[:, 9:10], scalar2=None, op0=ALU.add), ["ifs", "vm"], ["g7"])
        A(lambda e: e.activation(out=gsc[:, 7, :], in_=gsc[:, 7, :], func=AF.Exp, scale=-1.0), ["g7"], ["g7"])
        A(lambda e: e.activation(out=gsc[:, 0, :], in_=gsc[:, 7, :], func=AF.Ln, bias=1.0), ["g7"], ["g0"])
        V(lambda e: e.tensor_scalar(out=gsc[:, 0, :], in0=gsc[:, 0, :], scalar1=-1.0, scalar2=None, op0=ALU.mult), ["g0"], ["g0"])
        PE(lambda e: e.matmul(bank[0][:, 0:NCH], lhsT=tri, rhs=gsc[:, 0, :], start=True, stop=True), ["cf", "g0"], [("bank", 0)])
        V(lambda e: e.tensor_copy(out=gsc[:, 1, :], in_=bank[0][:, 0:NCH]), [("bank", 0)], ["g1"])
        V(lambda e: e.scalar_tensor_tensor(out=gsc[:, 2, :], in0=ifs[:, :, 0], scalar=vm[:, 8:9], in1=gsc[:, 1, :], op0=ALU.add, op1=ALU.subtract),
          ["ifs", "vm", "g1"], ["g2"])
        PE(lambda e: e.transpose(out=bank[1][0:NCH, 0:128], in_=gsc[:, 2, :], identity=ident), ["g2", "cf"], [("bank", 1)])
        V(lambda e: e.tensor_reduce(out=acol[:, :], in_=bank[1][0:NCH, 0:128], axis=AX.X, op=ALU.max), [("bank", 1)], ["acol"])
        V(lambda e: e.tensor_scalar(out=abc[:, :], in0=cf[0:64, 2, :], scalar1=acol[:, 0:1], scalar2=None, op0=ALU.mult), ["acol", "cf"], ["abc"])
        PE(lambda e: e.matmul(bank[2][:, 0:NCH], lhsT=abc[:, :], rhs=cf[0:64, 0, 0:64], start=True, stop=True), ["abc", "cf"], [("bank", 2)])
        PE(lambda e: e.matmul(bank[3][:, 0:NCH], lhsT=cf[:, 3, :], rhs=gsc[:, 1, :], start=True, stop=True), ["g1", "cf"], [("bank", 3)])
        V(lambda e: e.tensor_copy(out=grow[:, 1, 0:NCH], in_=bank[2][:, 0:NCH]), [("bank", 2)], ["gr1"])
        V(lambda e: e.tensor_copy(out=grow[:, 0, 0:NCH], in_=bank[3][:, 0:NCH]), [("bank", 3)], ["gr0"])
        V(lambda e: e.tensor_tensor(out=grow[:, 2, 0:NCH], in0=grow[:, 1, 0:NCH], in1=grow[:, 0, 0:NCH], op=ALU.add), ["gr0", "gr1"], ["gr2"])
        V(lambda e: e.memset(grow[:, 3, 0:1], 0.0), [], ["gr3a"])
        V(lambda e: e.tensor_tensor_scan(out=grow[:, 3, 1:NCH + 1], data0=grow[:, 0, 0:NCH], data1=grow[:, 2, 0:NCH], initial=0.0, op0=ALU.add, op1=ALU.max),
          ["gr0", "gr2", "gr3a"], ["gr3"])
        V(lambda e: e.tensor_tensor(out=gsc[:, 5, :], in0=grow[:, 3, 1:NCH + 1], in1=grow[:, 0, 0:NCH], op=ALU.subtract), ["gr3", "gr0"], ["g5"])
        V(lambda e: e.tensor_tensor(out=gsc[:, 6, :], in0=grow[:, 3, 0:NCH], in1=gsc[:, 5, :], op=ALU.subtract), ["gr3", "gr3a", "g5"], ["g6"])
        A(lambda e: e.activation(out=gsc[:, 6, :], in_=gsc[:, 6, :], func=AF.Exp), ["g6"], ["g6"])
        V(lambda e: e.tensor_tensor(out=gsc[:, 3, :], in0=gsc[:, 2, :], in1=gsc[:, 5, :], op=ALU.subtract), ["g2", "g5"], ["g3"])
        A(lambda e: e.activation(out=gsc[:, 3, :], in_=gsc[:, 3, :], func=AF.Exp), ["g3"], ["g3"])
        V(lambda e: e.tensor_tensor(out=gsc[:, 4, :], in0=gsc[:, 1, :], in1=gsc[:, 5, :], op=ALU.add), ["g1", "g5"], ["g4"])
        A(lambda e: e.activation(out=gsc[:, 4, :], in_=gsc[:, 4, :], func=AF.Exp, scale=-1.0), ["g4"], ["g4"])

        if MLEVEL == 1:
            P.emit()
            return nc
        PE = PE_real
        if MSUB == -1:
            V = lambda fn, r=(), w=(): None
        V(lambda e: e.memset(Cst[:, :, :], 0.0), [], ["Cst"])
        V(lambda e: e.memset(nst[:, :], 0.0), [], ["nst"])
        V(lambda e: e.memset(Cwb[:, :, :], 0.0), [], ["Cwb"])
        V(lambda e: e.memset(xpad[:, :, 0:3], 0.0), [], [("xpadh", 0), ("xpadh", 1)])
        V(lambda e: e.memset(vext[:, :, 256:257], 1.0), [], ["vones"])

        V = lambda fn, r=(), w=(): P.add("vector", fn, reads=r, writes=w)
        hv = hT.rearrange("(dt p) t -> p dt t", p=128)
        yv = ymlT_o.rearrange("(e p) t -> p e t", p=128)
        finals = []
        for bi in range(NBLK if MLEVEL >= 5 else (2 if MLEVEL == 4 else 1)):
            c0 = bi * TBS
            s = bi % 2
            P.dma("sync", lambda e, s=s, c0=c0: e.dma_start(out=hblk[:, s, :, :], in_=hv[:, :, c0:c0 + TBS]), ("hblk", s), writes=[("hblk", s)])
            HB = ("hblk", s)
            b0 = bank[0]
            if MSUB == -2:
                continue
            for dt in range(DT):
                PE(lambda e, dt=dt, s=s: e.matmul(b0[:, :], lhsT=wm[:, dt, 0:128], rhs=hblk[:, s, dt, :], start=(dt == 0), stop=(dt == DT - 1)),
                   ["wm", HB], [("bank", 0)])
            if MSUB == -3:
                continue
            A(lambda e: e.activation(out=ubf[:, :], in_=b0[:, :], func=AF.Copy), [("bank", 0)], ["ubf"])
            if MSUB == -4:
                continue
            V(lambda e: e.tensor_scalar(out=du[:, :], in0=b0[:, :], scalar1=vm[:, 6:7], scalar2=None, op0=ALU.mult), [("bank", 0), "vm", "ubf"], ["du"])
            for eo in range(2 if MSUB >= 1 else 0):
                for dt in range(DT):
                    PE(lambda e, dt=dt, s=s, eo=eo: e.matmul(b0[:, :], lhsT=wm[:, dt, 128 + eo * 128:256 + eo * 128], rhs=hblk[:, s, dt, :],
                                                             start=(dt == 0), stop=(dt == DT - 1)), ["wm", HB], [("bank", 0)])
                A(lambda e, eo=eo: e.activation(out=xpad[:, eo, 3:3 + TBS], in_=b0[:, :], func=AF.Copy), [("bank", 0)], [("xpad", eo)])
                XR = [("xpad", eo), ("xpadh", eo)]
                V(lambda e, eo=eo: e.tensor_scalar(out=cacc[:, eo, :], in0=xpad[:, eo, 0:TBS], scalar1=convw[:, eo, 0:1], scalar2=vm[:, eo:eo + 1],
                                                   op0=ALU.mult, op1=ALU.add), XR + ["convw", "vm"], [("cacc", eo)])
                for j in range(1, 4):
                    V(lambda e, eo=eo, j=j: e.scalar_tensor_tensor(out=cacc[:, eo, :], in0=xpad[:, eo, j:j + TBS], scalar=convw[:, eo, j:j + 1],
                                                                   in1=cacc[:, eo, :], op0=ALU.mult, op1=ALU.add), XR + [("cacc", eo), "convw"], [("cacc", eo)])
                G(lambda e, eo=eo: e.tensor_copy(out=xpad[:, eo, 0:3], in_=xpad[:, eo, TBS:TBS + 3]), XR, [("xpadh", eo)])
                A(lambda e, eo=eo: e.activation(out=xc32[:, eo, :], in_=cacc[:, eo, :], func=AF.Silu), [("cacc", eo)], [("xc32", eo)])
                G(lambda e, eo=eo: e.tensor_copy(out=xcb[:, eo, :], in_=xc32[:, eo, :]), [("xc32", eo)], [("xcb", eo)])
                G(lambda e, eo=eo: e.tensor_scalar(out=skx[:, eo, :], in0=xc32[:, eo, :], scalar1=vm[:, 4 + eo:5 + eo], scalar2=None, op0=ALU.mult),
                  [("xc32", eo), "vm"], [("skx", eo)])
            for eo in range(2 if MSUB >= 2 else 0):
                for dd in range(2):
                    PE(lambda e, eo=eo, dd=dd: e.matmul(b0[:, :], lhsT=wq[:, dd, eo * 128:(eo + 1) * 128], rhs=xcb[:, dd, :], start=(dd == 0), stop=(dd == 1)),
                       ["wq", ("xcb", dd)], [("bank", 0)])
                A(lambda e, eo=eo: e.activation(out=qT[:, eo, :], in_=b0[:, :], func=AF.Copy, scale=1.0 / 16.0), [("bank", 0)], [("qT", eo)])
                for dd in range(2):
                    PE(lambda e, eo=eo, dd=dd: e.matmul(b0[:, :], lhsT=wk[:, dd, eo * 128:(eo + 1) * 128], rhs=xcb[:, dd, :], start=(dd == 0), stop=(dd == 1)),
                       ["wk", ("xcb", dd)], [("bank", 0)])
                V(lambda e, eo=eo: e.tensor_copy(out=kT[:, eo, :], in_=b0[:, :]), [("bank", 0)], [("kT", eo)])
            for j in range(4 if MSUB >= 3 else 0):
                PE(lambda e, j=j: e.matmul(bank[2][:, :], lhsT=BbT[:, 0, j * 128:(j + 1) * 128], rhs=ubf[:, :], start=True, stop=True), ["BbT0", "ubf"], [("bank", 2)])
                PE(lambda e, j=j: e.matmul(bank[3][:, :], lhsT=BbT[:, 1, j * 128:(j + 1) * 128], rhs=ubf[:, :], start=True, stop=True), ["BbT1", "ubf"], [("bank", 3)])
                V(lambda e, j=j: e.tensor_tensor(out=t1[:, 0, :], in0=bank[2][:, :], in1=cosT[:, j, :], op=ALU.mult), [("bank", 2), ("cosT", j)], [("t1", 0)])
                V(lambda e, j=j: e.tensor_tensor(out=t2[:, 0, :], in0=bank[3][:, :], in1=sinT[:, j, :], op=ALU.mult), [("bank", 3), ("sinT", j)], [("t2", 0)])
                V(lambda e, j=j: e.tensor_tensor(out=t1[:, 1, :], in0=bank[3][:, :], in1=cosT[:, j, :], op=ALU.mult), [("bank", 3), ("cosT", j)], [("t1", 1)])
                V(lambda e, j=j: e.tensor_tensor(out=t2[:, 1, :], in0=bank[2][:, :], in1=sinT[:, j, :], op=ALU.mult), [("bank", 2), ("sinT", j)], [("t2", 1)])
                G(lambda e: e.tensor_tensor(out=zin[:, 0, :], in0=t1[:, 0, :], in1=t2[:, 0, :], op=ALU.add), [("t1", 0), ("t2", 0)], [("zin", 0)])
                G(lambda e: e.tensor_tensor(out=zin[:, 1, :], in0=t1[:, 1, :], in1=t2[:, 1, :], op=ALU.subtract), [("t1", 1), ("t2", 1)], [("zin", 1)])
                ZR, ZI = ("zr", j), ("zi", j)
                if bi == 0:
                    if j == 0:
                        V(lambda e: e.memset(init[:, :, :], 0.0), [], [("init", jj) for jj in range(4)])
                else:
                    V(lambda e, j=j: e.tensor_scalar(out=init[:, j, 2:3], in0=zr[:, j, TBS - 1:TBS], scalar1=cs[:, 16 + j:17 + j], scalar2=None, op0=ALU.mult),
                      [ZR, "cs_Ec"], [("initt", j)])
                    V(lambda e, j=j: e.scalar_tensor_tensor(out=init[:, j, 0:1], in0=zi[:, j, TBS - 1:TBS], scalar=cs[:, 24 + j:25 + j], in1=init[:, j, 2:3],
                                                            op0=ALU.mult, op1=ALU.add), [ZI, "cs_nEs", ("initt", j)], [("init", j)])
                    V(lambda e, j=j: e.tensor_scalar(out=init[:, j, 3:4], in0=zi[:, j, TBS - 1:TBS], scalar1=cs[:, 16 + j:17 + j], scalar2=None, op0=ALU.mult),
                      [ZI, "cs_Ec"], [("initu", j)])
                    V(lambda e, j=j: e.scalar_tensor_tensor(out=init[:, j, 1:2], in0=zr[:, j, TBS - 1:TBS], scalar=cs[:, 20 + j:21 + j], in1=init[:, j, 3:4],
                                                            op0=ALU.mult, op1=ALU.add), [ZR, "cs_Es", ("initu", j)], [("init", j)])
                V(lambda e, j=j: e.tensor_tensor_scan(out=zr[:, j, :], data0=rT[:, j, :], data1=zin[:, 0, :], initial=init[:, j, 0:1], op0=ALU.mult, op1=ALU.add),
                  [("rT", j), ("zin", 0), ("init", j), "zr_all"], [ZR])
                V(lambda e, j=j: e.tensor_tensor_scan(out=zi[:, j, :], data0=rT[:, j, :], data1=zin[:, 1, :], initial=init[:, j, 1:2], op0=ALU.mult, op1=ALU.add),
                  [("rT", j), ("zin", 1), ("init", j)], [ZI])
                sj = j % 2
                G(lambda e, j=j: e.tensor_tensor(out=pt[:, 0, :], in0=zr[:, j, :], in1=cosT[:, j, :], op=ALU.mult), [ZR, ("cosT", j)], [("pt", 0)])
                G(lambda e, j=j: e.tensor_tensor(out=pt[:, 1, :], in0=zi[:, j, :], in1=sinT[:, j, :], op=ALU.mult), [ZI, ("sinT", j)], [("pt", 1)])
                G(lambda e, sj=sj: e.tensor_tensor(out=sre[:, sj, :], in0=pt[:, 0, :], in1=pt[:, 1, :], op=ALU.subtract), [("pt", 0), ("pt", 1)], [("sre", sj)])
                G(lambda e, j=j: e.tensor_tensor(out=pt[:, 2, :], in0=zi[:, j, :], in1=cosT[:, j, :], op=ALU.mult), [ZI, ("cosT", j)], [("pt", 2)])
                G(lambda e, j=j: e.tensor_tensor(out=pt[:, 3, :], in0=zr[:, j, :], in1=sinT[:, j, :], op=ALU.mult), [ZR, ("sinT", j)], [("pt", 3)])
                G(lambda e, sj=sj: e.tensor_tensor(out=sim_[:, sj, :], in0=pt[:, 2, :], in1=pt[:, 3, :], op=ALU.add), [("pt", 2), ("pt", 3)], [("sim", sj)])
                PE(lambda e, j=j, sj=sj: e.matmul(bank[4][:, :], lhsT=CTb[:, 0, j, :], rhs=sre[:, sj, :], start=(j == 0), stop=False), ["CTb0", ("sre", sj)], [("bank", 4)])
                PE(lambda e, j=j, sj=sj: e.matmul(bank[4][:, :], lhsT=CTb[:, 1, j, :], rhs=sim_[:, sj, :], start=False, stop=(j == 3)), ["CTb1", ("sim", sj)], [("bank", 4)])
            if MSUB == -1:
                continue
            V(lambda e: e.tensor_tensor(out=yy[:, :], in0=bank[4][:, :], in1=du[:, :], op=ALU.add), [("bank", 4), "du"], ["yy"])
            G(lambda e: e.tensor_tensor(out=g1[:, :], in0=yy[:, :], in1=yy[:, :], op=ALU.mult), ["yy"], ["g1t"])
            G(lambda e: e.tensor_scalar(out=g1[:, :], in0=g1[:, :], scalar1=0.044715, scalar2=1.0, op0=ALU.mult, op1=ALU.add), ["g1t"], ["g1t"])
            G(lambda e: e.tensor_tensor(out=g1[:, :], in0=g1[:, :], in1=yy[:, :], op=ALU.mult), ["g1t", "yy"], ["g1t"])
            A(lambda e: e.activation(out=g2[:, :], in_=g1[:, :], func=AF.Sigmoid, scale=1.5957691216057308), ["g1t"], ["g2t"])
            G(lambda e, s=s: e.tensor_tensor(out=zout[:, s, :], in0=yy[:, :], in1=g2[:, :], op=ALU.mult), ["yy", "g2t"], [("zout", s)])
            finals.append(P.dma("sync", lambda e, s=s, c0=c0: e.dma_start(out=zT_o[:, c0:c0 + TBS], in_=zout[:, s, :]), ("zst", s), reads=[("zout", s)]))
            for ch in range(4 if MLEVEL >= 3 else 0):
                gch = bi * 4 + ch
                cc = ch * 128
                for dd in range(2):
                    PE(lambda e, dd=dd, cc=cc: e.matmul(bank[1][:, 256:512], lhsT=xcb[:, dd, cc:cc + 128], rhs=wk[:, dd, :], start=(dd == 0), stop=(dd == 1)),
                       ["wk", ("xcb", dd)], [("bank", 1, "k")])
                V(lambda e, ch=ch, gch=gch: e.tensor_scalar(out=kp[:, ch, :], in0=bank[1][:, 256:512], scalar1=gsc[:, 3, gch:gch + 1], scalar2=None, op0=ALU.mult),
                  [("bank", 1, "k"), "g3"], [("kp", ch)])
                for dt in range(DT):
                    PE(lambda e, dt=dt, s=s, cc=cc: e.matmul(b0[:, :], lhsT=hblk[:, s, dt, cc:cc + 128], rhs=wm[:, dt, 384:896], start=(dt == 0), stop=(dt == DT - 1)),
                       ["wm", HB], [("bank", 0)])
                V(lambda e, ch=ch: e.tensor_copy(out=vext[:, ch, 0:256], in_=b0[:, 0:256]), [("bank", 0)], [("vext", ch)])
                A(lambda e, ch=ch: e.activation(out=og[:, ch, :], in_=b0[:, 256:512], func=AF.Sigmoid), [("bank", 0)], [("og", ch)])
                for dd in range(2):
                    PE(lambda e, dd=dd, cc=cc: e.matmul(bank[5][:, 0:128], lhsT=kT[:, dd, cc:cc + 128], rhs=qT[:, dd, cc:cc + 128], start=(dd == 0), stop=(dd == 1)),
                       [("kT", dd), ("qT", dd)], [("bank", 5, "s")])
                ss = gch % 2
                V(lambda e, ss=ss, gch=gch: e.scalar_tensor_tensor(out=Sm[:, ss, :], in0=bank[5][:, 0:128], scalar=gsc[:, 3, gch:gch + 1], in1=tri,
                                                                   op0=ALU.mult, op1=ALU.mult), [("bank", 5, "s"), "g3", "cf"], [("Sm", ss)])
                PE(lambda e, ss=ss, ch=ch: e.matmul(bank[6][:, 0:257], lhsT=Sm[:, ss, :], rhs=vext[:, ch, :], start=True, stop=False),
                   [("Sm", ss), ("vext", ch), "vones"], [("bank", 6)])
                for dd in range(2):
                    PE(lambda e, dd=dd, cc=cc: e.matmul(bank[6][:, 0:257], lhsT=qT[:, dd, cc:cc + 128], rhs=Cwb[:, dd, :], start=False, stop=(dd == 1)),
                       [("qT", dd), "Cwb"], [("bank", 6)])
                A(lambda e: e.activation(out=sm[:, 5:6], in_=bank[6][:, 256:257], func=AF.Abs), [("bank", 6)], ["sm5"])
                V(lambda e, gch=gch: e.tensor_scalar(out=sm[:, 0:1], in0=sm[:, 5:6], scalar1=gsc[:, 4, gch:gch + 1], scalar2=None, op0=ALU.max),
                  ["sm5", "g4"], ["sm0"])
                V(lambda e: e.reciprocal(out=sm[:, 1:2], in_=sm[:, 0:1]), ["sm0"], ["sm1"])
                V(lambda e, ch=ch: e.scalar_tensor_tensor(out=hc[:, :], in0=bank[6][:, 0:256], scalar=sm[:, 1:2], in1=og[:, ch, :], op0=ALU.mult, op1=ALU.mult),
                  [("bank", 6), "sm1", ("og", ch)], ["hc"])
                A(lambda e: e.activation(out=hsq[:, :], in_=hc[:, :], func=AF.Square), ["hc"], ["hsq"])
                V(lambda e: e.tensor_reduce(out=sm[:, 2:3], in_=hsq[:, :], axis=AX.X, op=ALU.add), ["hsq"], ["sm2"])
                V(lambda e: e.tensor_scalar(out=sm[:, 3:4], in0=sm[:, 2:3], scalar1=1.0 / 256.0, scalar2=EPS, op0=ALU.mult, op1=ALU.add), ["sm2"], ["sm3"])
                A(lambda e: e.activation(out=sm[:, 3:4], in_=sm[:, 3:4], func=AF.Sqrt), ["sm3"], ["sm3"])
                V(lambda e: e.reciprocal(out=sm[:, 4:5], in_=sm[:, 3:4]), ["sm3"], ["sm4"])
                A(lambda e: e.activation(out=hn[:, :], in_=hc[:, :], func=AF.Copy, scale=sm[:, 4:5]), ["hc", "sm4"], ["hn"])
                for eo in range(2):
                    PE(lambda e, eo=eo: e.transpose(out=bankT[:, eo * 128:(eo + 1) * 128], in_=hn[:, eo * 128:(eo + 1) * 128], identity=cb[:, :]),
                       ["hn", "cb"], [("bankT", eo)])
                    V(lambda e, eo=eo, s=s, cc=cc: e.scalar_tensor_tensor(out=ymlo[:, s, eo, cc:cc + 128], in0=bankT[:, eo * 128:(eo + 1) * 128], scalar=vm[:, 2 + eo:3 + eo],
                                                                          in1=skx[:, eo, cc:cc + 128], op0=ALU.mult, op1=ALU.add),
                      [("bankT", eo), "vm", ("skx", eo)], [("ymlo", s)])
                PE(lambda e, ch=ch: e.matmul(bank[1][:, 0:256], lhsT=kp[:, ch, 0:128], rhs=vext[:, ch, 0:256], start=True, stop=True),
                   [("kp", ch), ("vext", ch)], [("bank", 1, "kv")])
                PE(lambda e, ch=ch: e.matmul(bank[5][:, 128:384], lhsT=kp[:, ch, 128:256], rhs=vext[:, ch, 0:256], start=True, stop=True),
                   [("kp", ch), ("vext", ch)], [("bank", 5, "kv")])
                PE(lambda e, ch=ch: e.matmul(bank[5][:, 384:385], lhsT=kp[:, ch, 0:128], rhs=vext[:, ch, 256:257], start=True, stop=True),
                   [("kp", ch), "vones"], [("bank", 5, "n0")])
                PE(lambda e, ch=ch: e.matmul(bank[5][:, 385:386], lhsT=kp[:, ch, 128:256], rhs=vext[:, ch, 256:257], start=True, stop=True),
                   [("kp", ch), "vones"], [("bank", 5, "n1")])
                wcol = gsc[:, 6, gch:gch + 1]
                V(lambda e, wcol=wcol: e.scalar_tensor_tensor(out=Cst[:, 0, :], in0=Cst[:, 0, :], scalar=wcol, in1=bank[1][:, 0:256], op0=ALU.mult, op1=ALU.add),
                  ["Cst", "g6", ("bank", 1, "kv")], ["Cst"])
                V(lambda e, wcol=wcol: e.scalar_tensor_tensor(out=Cst[:, 1, :], in0=Cst[:, 1, :], scalar=wcol, in1=bank[5][:, 128:384], op0=ALU.mult, op1=ALU.add),
                  ["Cst", "g6", ("bank", 5, "kv")], ["Cst"])
                V(lambda e, wcol=wcol: e.scalar_tensor_tensor(out=nst[:, :], in0=nst[:, :], scalar=wcol, in1=bank[5][:, 384:386], op0=ALU.mult, op1=ALU.add),
                  ["nst", "g6", ("bank", 5, "n0"), ("bank", 5, "n1")], ["nst"])
                if gch + 1 < NCH:
                    wn = gsc[:, 6, gch + 1:gch + 2]
                    A(lambda e, wn=wn: e.activation(out=Cwb[:, :, 0:256], in_=Cst[:, :, :], func=AF.Copy, scale=wn), ["Cst", "g6"], ["Cwb"])
                    A(lambda e, wn=wn: e.activation(out=Cwb[:, :, 256], in_=nst[:, :], func=AF.Copy, scale=wn), ["nst", "g6"], ["Cwb"])
            finals.append(P.dma("sync", lambda e, s=s, c0=c0: e.dma_start(out=yv[:, :, c0:c0 + TBS], in_=ymlo[:, s, :, :]), ("yst", s), reads=[("ymlo", s)]))
        P.emit(final_waits=finals)
    return nc


def m_inputs(I, l, c, hT_full, ifp_full):
    b, r = c // 4, c % 4
    f32 = np.float32
    w_in = I["w_in"][l]
    w_m = np.concatenate([w_in[:, 128 * r:128 * r + 128], w_in[:, 512 + 256 * r:512 + 256 * r + 256],
                          w_in[:, 1536 + 256 * r:1536 + 256 * r + 256], w_in[:, 2560 + 256 * r:2560 + 256 * r + 256]], axis=1)
    ifg = np.stack([ifp_full[b][:, r], ifp_full[b][:, 4 + r]], axis=-1).reshape(NCH, 128, 2).transpose(1, 0, 2)
    convw = I["ml_conv_w"][l][:, 256 * r:256 * r + 256].reshape(4, 2, 128).transpose(2, 1, 0)
    vm = np.zeros((128, 16), f32)
    col2 = lambda v: v[256 * r:256 * r + 256].reshape(2, 128).T
    vm[:, 0:2] = col2(I["ml_conv_b"][l])
    vm[:, 2:4] = col2(I["ml_norm"][l])
    vm[:, 4:6] = col2(I["ml_skip"][l])
    vm[:, 6] = I["s5_d"][l][128 * r:128 * r + 128]
    vm[:, 8] = I["b_if"][l][r]
    vm[:, 9] = I["b_if"][l][4 + r]
    gs = slice(8 * r, 8 * r + 8)
    lam_re = I["s5_lam_re"][l][gs].reshape(512)
    lam_im = I["s5_lam_im"][l][gs].reshape(512)
    ldt = np.repeat(I["s5_log_dt"][l][gs], 64)
    lamC = np.zeros((128, 12), f32)
    lamC[:, 0:4] = lam_re.reshape(4, 128).T
    lamC[:, 4:8] = lam_im.reshape(4, 128).T
    lamC[:, 8:12] = ldt.reshape(4, 128).T
    lamR = np.broadcast_to(np.stack([lam_re, lam_im, ldt])[None], (128, 3, 512))
    BT = np.zeros((128, 2, 512), f32)
    CT = np.zeros((128, 2, 4, 128), f32)
    for gl in range(8):
        g = 8 * r + gl
        BT[16 * gl:16 * gl + 16, 0, gl * 64:(gl + 1) * 64] = I["s5_b_re"][l][g].T
        BT[16 * gl:16 * gl + 16, 1, gl * 64:(gl + 1) * 64] = I["s5_b_im"][l][g].T
        j, half = gl // 2, gl % 2
        CT[half * 64:(half + 1) * 64, 0, j, 16 * gl:16 * gl + 16] = I["s5_c_re"][l][g].T
        CT[half * 64:(half + 1) * 64, 1, j, 16 * gl:16 * gl + 16] = I["s5_c_im"][l][g].T
    return {"hT": hT_full[b], "ifg": np.ascontiguousarray(ifg, f32), "w_m": np.ascontiguousarray(w_m),
            "wq": I["ml_wq"][l][r], "wk": I["ml_wk"][l][r], "convw": np.ascontiguousarray(convw, f32), "vm": vm,
            "lamC": lamC, "lamR": np.ascontiguousarray(lamR, f32), "BT": BT, "CT": CT}


def m_consts():
    f32 = np.float32
    cf = np.zeros((128, 4, 128), f32)
    cf[:, 0, :] = np.eye(128)
    cf[:, 1, :] = np.triu(np.ones((128, 128)))
    cf[:, 2, :] = 1.0
    cf[127, 3, :] = 1.0
    iota = np.broadcast_to(np.arange(512, dtype=f32)[None], (128, 512))
    return {"cf32": cf, "cb16": np.eye(128).astype(ml_dtypes.bfloat16), "iota": np.ascontiguousarray(iota)}


def run_M(I, l, hT_sh, ifp_sh):
    hT_full = [np.concatenate([hT_sh[4 * b + r] for r in range(4)], axis=1) for b in range(2)]
    ifp_full = [np.concatenate([ifp_sh[4 * b + r] for r in range(4)], axis=0) for b in range(2)]
    cst = m_consts()
    in_maps = []
    for c in range(8):
        m = m_inputs(I, l, c, hT_full, ifp_full)
        m.update(cst)
        in_maps.append(m)
    nc = build_M()
    return _run(nc, in_maps)
```

```python
import contextlib
import math
import os
import numpy as np
import ml_dtypes
import concourse.bass as bass
import concourse.mybir as mybir
from concourse.bass_utils import run_bass_kernel_spmd

F32 = mybir.dt.float32
BF16 = mybir.dt.bfloat16
AF = mybir.ActivationFunctionType
ALU = mybir.AluOpType
AX = mybir.AxisListType

D = 1024
DT = 8
SEQ = 8192
NTOK = 2048
DFF = 2816
FT = 22
DEPTH = 4
EPS = 1e-6
SBT = 1024
NSB = NTOK // SBT
TBS = 512
NTB = SBT // TBS
ENGS = ("tensor", "vector", "scalar", "gpsimd", "sync")
TWO_PI = 2.0 * math.pi


class Op:
    def __init__(self, eng, fn):
        self.eng = eng
        self.fn = fn
        self.dma_sem = None
        self.dma_val = 0
        self.has_dep = False
        self.deps = []
        self.cval = 0


class Prog:
    def __init__(self, nc):
        self.nc = nc
        self.ops = {e: [] for e in ENGS}
        self.last_writer = {}
        self.readers = {}
        self.dma_counts = {}
        self.dma_last = {}
        self.pending_barrier = {e: [] for e in ENGS}

    def add(self, eng, fn, reads=(), writes=(), extra_deps=()):
        rd, wr = [], []
        for t in reads:
            if isinstance(t, tuple) and t[0] in ("bank", "bankT"):
                wr.append(("bank", t[1]) if t[0] == "bank" else ("bankT", 0))
            else:
                rd.append(t)
        for t in writes:
            if isinstance(t, tuple) and t[0] in ("bank", "bankT"):
                wr.append(("bank", t[1]) if t[0] == "bank" else ("bankT", 0))
            else:
                wr.append(t)
        reads, writes = rd, wr
        op = Op(eng, fn)
        deps = []
        for t in reads:
            w = self.last_writer.get(t)
            if w is not None:
                deps.append(w)
        for t in writes:
            w = self.last_writer.get(t)
            if w is not None:
                deps.append(w)
            deps.extend(self.readers.get(t, ()))
        for t in reads:
            self.readers.setdefault(t, []).append(op)
        for t in writes:
            self.last_writer[t] = op
            self.readers[t] = []
        deps.extend(extra_deps)
        if self.pending_barrier[eng]:
            deps.extend(self.pending_barrier[eng])
            self.pending_barrier[eng] = []
        op.deps = [d for d in deps if d is not op]
        for d in op.deps:
            d.has_dep = True
        self.ops[eng].append(op)
        return op

    def dma(self, eng, fn, semkey, reads=(), writes=(), extra_deps=()):
        op = self.add(eng, fn, reads, writes, extra_deps)
        op.dma_sem = semkey
        self.dma_counts[semkey] = self.dma_counts.get(semkey, 0) + 1
        op.dma_val = 16 * self.dma_counts[semkey]
        self.dma_last[semkey] = op
        return op

    def barrier(self):
        lasts = []
        for e in ENGS:
            for op in reversed(self.ops[e]):
                if op.dma_sem is None:
                    lasts.append(op)
                    break
        lasts.extend(self.dma_last.values())
        for e in ENGS:
            self.pending_barrier[e] = list(lasts)

    def emit(self, final_waits=()):
        nc = self.nc
        cnt = {e: 0 for e in ENGS}
        for e in ENGS:
            for op in self.ops[e]:
                if op.dma_sem is None and op.has_dep:
                    cnt[e] += 1
                    op.cval = cnt[e]
        with contextlib.ExitStack() as st:
            sems = {}
            for e in ENGS:
                if cnt[e] > 0:
                    sems[("eng", e)] = st.enter_context(nc.semaphore("s_" + e))
            for i, k in enumerate(sorted(self.dma_counts.keys(), key=str)):
                sems[("dma", k)] = st.enter_context(nc.semaphore("d%d" % i))
            block = st.enter_context(nc.Block())

            def body_for(e):
                def body(engine):
                    waited = {}
                    for op in self.ops[e]:
                        need = {}
                        for d in op.deps:
                            if d.dma_sem is not None:
                                key, val = ("dma", d.dma_sem), d.dma_val
                            else:
                                key, val = ("eng", d.eng), d.cval
                            if need.get(key, 0) < val:
                                need[key] = val
                        for key, val in need.items():
                            if waited.get(key, 0) >= val:
                                continue
                            engine.wait_ge(sems[key], val)
                            waited[key] = val
                        ins = op.fn(engine)
                        if op.dma_sem is not None:
                            ins.then_inc(sems[("dma", op.dma_sem)], 16)
                        elif op.has_dep:
                            ins.then_inc(sems[("eng", e)], 1)
                    if e == "sync":
                        for op in final_waits:
                            engine.wait_ge(sems[("dma", op.dma_sem)], op.dma_val)
                return body

            for e in ENGS:
                if self.ops[e] or (e == "sync" and final_waits):
                    getattr(block, e)(body_for(e))


class Ctx:
    pass


def emit_norm(K, sb, gcol, tag):
    P = K.P
    for tb in range(NTB):
        c0 = sb * SBT + tb * TBS
        lc = tb * TBS
        psn = K.bank[6]
        for dt in range(DT):
            sq = K.sq[:, dt % 2, :]
            P.add("scalar", lambda e, sq=sq, dt=dt, c0=c0: e.activation(out=sq, in_=K.x[:, dt, c0:c0 + TBS], func=AF.Square),
                  reads=[("x", dt, sb, tb)], writes=[("sq", dt % 2)])
            P.add("tensor", lambda e, sq=sq, dt=dt, psn=psn: e.matmul(psn[:, :], lhsT=K.ones32[:, :], rhs=sq, start=(dt == 0), stop=(dt == DT - 1)),
                  reads=[("sq", dt % 2), "consts"], writes=[("bank", 6)])
        P.add("vector", lambda e, psn=psn: e.tensor_scalar(out=K.rstd[:, :], in0=psn[:, :], scalar1=1.0 / D, scalar2=EPS, op0=ALU.mult, op1=ALU.add),
              reads=[("bank", 6)], writes=["rstd"])
        P.add("scalar", lambda e: e.activation(out=K.rstd[:, :], in_=K.rstd[:, :], func=AF.Sqrt), reads=["rstd"], writes=["rstd"])
        P.add("vector", lambda e: e.reciprocal(out=K.rstd[:, :], in_=K.rstd[:, :]), reads=["rstd"], writes=["rstd"])
        for dt in range(DT):
            eng = "vector"
            P.add(eng, lambda e, dt=dt, c0=c0, lc=lc: e.scalar_tensor_tensor(out=K.h[:, dt, lc:lc + TBS], in0=K.x[:, dt, c0:c0 + TBS], scalar=gcol(dt),
                                                                           in1=K.rstd[:, :], op0=ALU.mult, op1=ALU.mult),
                  reads=[("x", dt, sb, tb), "rstd", "vecs"], writes=[("h", dt, tb)])


def emit_ffn(K, sb, wg, wu, wd, gcol, tag):
    P = K.P
    emit_norm(K, sb, gcol, tag)
    wg_v = wg.rearrange("(dt p) f -> p dt f", p=128)
    wu_v = wu.rearrange("(dt p) f -> p dt f", p=128)
    wd_v = wd.rearrange("(ft p) d -> p ft d", p=128)
    for fg in range(FT // 2):
        s = K.cnt_w % 2
        K.cnt_w += 1
        P.dma("gpsimd", lambda e, s=s, fg=fg: e.dma_start(out=K.wgu[:, s, 0, :, :], in_=wg_v[:, :, fg * 256:(fg + 1) * 256]),
              ("wgu", s), writes=[("wg", s)])
        P.dma("gpsimd", lambda e, s=s, fg=fg: e.dma_start(out=K.wgu[:, s, 1, :, :], in_=wu_v[:, :, fg * 256:(fg + 1) * 256]),
              ("wgu", s), writes=[("wu", s)])
        for fi in range(2):
            ft = fg * 2 + fi
            for tb in range(NTB):
                lc = tb * TBS
                pb = K.cnt_p % 2
                K.cnt_p += 1
                pg, pu = K.bank[pb], K.bank[2 + pb]
                for dt in range(DT):
                    P.add("tensor", lambda e, dt=dt, s=s, lc=lc, pg=pg, fi=fi: e.matmul(pg[:, :], lhsT=K.wgu[:, s, 0, dt, fi * 128:(fi + 1) * 128], rhs=K.h[:, dt, lc:lc + TBS],
                                                                                      start=(dt == 0), stop=(dt == DT - 1)),
                          reads=[("wg", s), ("h", dt, tb)], writes=[("bank", pb)])
                for dt in range(DT):
                    P.add("tensor", lambda e, dt=dt, s=s, lc=lc, pu=pu, fi=fi: e.matmul(pu[:, :], lhsT=K.wgu[:, s, 1, dt, fi * 128:(fi + 1) * 128], rhs=K.h[:, dt, lc:lc + TBS],
                                                                                      start=(dt == 0), stop=(dt == DT - 1)),
                          reads=[("wu", s), ("h", dt, tb)], writes=[("bank", 2 + pb)])
                sg = K.tmp[:, pb, :]
                P.add("scalar", lambda e, sg=sg, pg=pg: e.activation(out=sg, in_=pg[:, :], func=AF.Silu),
                      reads=[("bank", pb)], writes=[("tmp", pb)])
                P.add("vector", lambda e, sg=sg, pu=pu, ft=ft, lc=lc: e.tensor_tensor(out=K.act[:, ft, lc:lc + TBS], in0=pu[:, :], in1=sg, op=ALU.mult),
                      reads=[("bank", 2 + pb), ("tmp", pb)], writes=[("act", ft, tb)])
    for dg in range(DT // 2):
        s = K.cnt_d % 2
        K.cnt_d += 1
        P.dma("gpsimd", lambda e, s=s, dg=dg: e.dma_start(out=K.wd[:, s, :, :], in_=wd_v[:, :, dg * 256:(dg + 1) * 256]),
              ("wd", s), writes=[("wd", s)])
        for di in range(2):
            dt = dg * 2 + di
            for tb in range(NTB):
                lc = tb * TBS
                c0 = sb * SBT + lc
                pb = K.cnt_q % 2
                K.cnt_q += 1
                pd = K.bank[4 + pb]
                for ft in range(FT):
                    P.add("tensor", lambda e, ft=ft, s=s, lc=lc, pd=pd, di=di: e.matmul(pd[:, :], lhsT=K.wd[:, s, ft, di * 128:(di + 1) * 128], rhs=K.act[:, ft, lc:lc + TBS],
                                                                                      start=(ft == 0), stop=(ft == FT - 1)),
                          reads=[("wd", s), ("act", ft, tb)], writes=[("bank", 4 + pb)])
                P.add("vector", lambda e, dt=dt, c0=c0, pd=pd: e.scalar_tensor_tensor(out=K.x[:, dt, c0:c0 + TBS], in0=pd[:, :], scalar=0.5,
                                                                                     in1=K.x[:, dt, c0:c0 + TBS], op0=ALU.mult, op1=ALU.add),
                      reads=[("bank", 4 + pb), ("x", dt, sb, tb)], writes=[("x", dt, sb, tb)])


def emit_ta_tail(K, sb, gcol, w_if, hT_out, ifp_out):
    P = K.P
    emit_norm(K, sb, gcol, "mix")
    hv = hT_out.rearrange("(dt p) t -> p dt t", p=128)
    st_ops = []
    st_ops.append(P.dma("sync", lambda e: e.dma_start(out=hv[:, :, sb * SBT:(sb + 1) * SBT], in_=K.h[:, :, :]), ("hst", 0),
                        reads=[("h", dt, tb) for dt in range(DT) for tb in range(NTB)]))
    P.dma("gpsimd", lambda e: e.dma_start(out=K.wif[:, :, :], in_=w_if.rearrange("(dt p) c -> p dt c", p=128)), ("wif", 0), writes=["wif"])
    pif = K.bank[7]
    ntt = SBT // 128
    for tt in range(ntt):
        for dt in range(DT):
            P.add("tensor", lambda e, tt=tt, dt=dt: e.matmul(pif[:, tt * 8:(tt + 1) * 8], lhsT=K.h[:, dt, tt * 128:(tt + 1) * 128], rhs=K.wif[:, dt, :],
                                                             start=(dt == 0), stop=(dt == DT - 1)),
                  reads=[("h", dt, tt // 4), "wif"], writes=[("bank", 7)])
    P.add("vector", lambda e: e.tensor_copy(out=K.ifs[:, :], in_=pif[:, 0:ntt * 8]), reads=[("bank", 7)], writes=["ifs"])
    iv = ifp_out.rearrange("(tt p) c -> p tt c", p=128)
    st_ops.append(P.dma("sync", lambda e: e.dma_start(out=iv[:, sb * ntt:(sb + 1) * ntt, :], in_=K.ifs[:, :].rearrange("p (tt c) -> p tt c", c=8)),
                        ("ifst", 0), reads=["ifs"]))
    return st_ops


def emit_tb(K, sb, W):
    P = K.P
    c0s = sb * SBT
    P.dma("sync", lambda e: e.dma_start(out=K.h[:, :, :], in_=W.hT.rearrange("(dt p) t -> p dt t", p=128)[:, :, c0s:c0s + SBT]), ("hld", 0),
          writes=[("h", dt, tb) for dt in range(DT) for tb in range(NTB)])
    P.dma("sync", lambda e: e.dma_start(out=K.z[:, :, :], in_=W.zT.rearrange("(ct p) t -> p ct t", p=128)[:, :, c0s:c0s + SBT]), ("zld", 0),
          writes=["z"])
    P.dma("sync", lambda e: e.dma_start(out=K.yml[:, :, :], in_=W.ymlT.rearrange("(ct p) t -> p ct t", p=128)[:, :, c0s:c0s + SBT]), ("yld", 0),
          writes=["yml"])
    P.dma("gpsimd", lambda e: e.dma_start(out=K.wglu[:, 0, :, :], in_=W.glu_v.rearrange("(ct p) c -> p ct c", p=128)), ("wglu", 0), writes=["wglu0"])
    P.dma("gpsimd", lambda e: e.dma_start(out=K.wglu[:, 1, :, :], in_=W.glu_g.rearrange("(ct p) c -> p ct c", p=128)), ("wglu", 0), writes=["wglu1"])
    for co in range(4):
        for tb in range(NTB):
            lc = tb * TBS
            pb = K.cnt_p % 2
            K.cnt_p += 1
            pv, pg = K.bank[pb], K.bank[2 + pb]
            for ci in range(4):
                P.add("tensor", lambda e, ci=ci, co=co, lc=lc, pv=pv: e.matmul(pv[:, :], lhsT=K.wglu[:, 0, ci, co * 128:(co + 1) * 128], rhs=K.z[:, ci, lc:lc + TBS],
                                                                             start=(ci == 0), stop=(ci == 3)),
                      reads=["wglu0", "z"], writes=[("bank", pb)])
            for ci in range(4):
                P.add("tensor", lambda e, ci=ci, co=co, lc=lc, pg=pg: e.matmul(pg[:, :], lhsT=K.wglu[:, 1, ci, co * 128:(co + 1) * 128], rhs=K.z[:, ci, lc:lc + TBS],
                                                                             start=(ci == 0), stop=(ci == 3)),
                      reads=["wglu1", "z"], writes=[("bank", 2 + pb)])
            sg = K.tmp[:, pb, :]
            P.add("scalar", lambda e, sg=sg, pg=pg: e.activation(out=sg, in_=pg[:, :], func=AF.Sigmoid), reads=[("bank", 2 + pb)], writes=[("tmp", pb)])
            P.add("vector", lambda e, sg=sg, pv=pv, co=co, lc=lc: e.tensor_tensor(out=K.ys5[:, co, lc:lc + TBS], in0=pv[:, :], in1=sg, op=ALU.mult),
                  reads=[("bank", pb), ("tmp", pb)], writes=[("ys5", co, tb)])
    wbs_v = W.w_br_s5.rearrange("(ct p) d -> p ct d", p=128)
    wbm_v = W.w_br_ml.rearrange("(ct p) d -> p ct d", p=128)
    wg_v = W.w_g.rearrange("(ct p) d -> p ct d", p=128)
    for dt in range(DT):
        s = K.cnt_b % 2
        K.cnt_b += 1
        P.dma("gpsimd", lambda e, s=s, dt=dt: e.dma_start(out=K.wbr[:, s, 0:4, :], in_=wbs_v[:, :, dt * 128:(dt + 1) * 128]), ("wbr", s), writes=[("wbr0", s)])
        P.dma("gpsimd", lambda e, s=s, dt=dt: e.dma_start(out=K.wbr[:, s, 4:12, :], in_=wbm_v[:, :, dt * 128:(dt + 1) * 128]), ("wbr", s), writes=[("wbr1", s)])
        P.dma("gpsimd", lambda e, s=s, dt=dt: e.dma_start(out=K.wbr[:, s, 12:20, :], in_=wg_v[:, :, dt * 128:(dt + 1) * 128]), ("wbr", s), writes=[("wbr2", s)])
        P.dma("gpsimd", lambda e, s=s, dt=dt: e.dma_start(out=K.wbr[:, s, 20:28, :], in_=wg_v[:, :, D + dt * 128:D + (dt + 1) * 128]), ("wbr", s), writes=[("wbr3", s)])
        for tb in range(NTB):
            lc = tb * TBS
            b0, b1, b2, b3 = K.bank[0], K.bank[1], K.bank[2], K.bank[3]
            for ci in range(4):
                P.add("tensor", lambda e, ci=ci, s=s, lc=lc: e.matmul(b0[:, :], lhsT=K.wbr[:, s, ci, :], rhs=K.ys5[:, ci, lc:lc + TBS], start=(ci == 0), stop=(ci == 3)),
                      reads=[("wbr0", s), ("ys5", ci, tb)], writes=[("bank", 0)])
            for ci in range(8):
                P.add("tensor", lambda e, ci=ci, s=s, lc=lc: e.matmul(b1[:, :], lhsT=K.wbr[:, s, 4 + ci, :], rhs=K.yml[:, ci, lc:lc + TBS], start=(ci == 0), stop=(ci == 7)),
                      reads=[("wbr1", s), "yml"], writes=[("bank", 1)])
            for ci in range(8):
                P.add("tensor", lambda e, ci=ci, s=s, lc=lc: e.matmul(b2[:, :], lhsT=K.wbr[:, s, 12 + ci, :], rhs=K.h[:, ci, lc:lc + TBS], start=(ci == 0), stop=(ci == 7)),
                      reads=[("wbr2", s), ("h", ci, tb)], writes=[("bank", 2)])
            for ci in range(8):
                P.add("tensor", lambda e, ci=ci, s=s, lc=lc: e.matmul(b3[:, :], lhsT=K.wbr[:, s, 20 + ci, :], rhs=K.h[:, ci, lc:lc + TBS], start=(ci == 0), stop=(ci == 7)),
                      reads=[("wbr3", s), ("h", ci, tb)], writes=[("bank", 3)])
            P.add("scalar", lambda e: e.activation(out=K.tmp[:, 0, :], in_=b2[:, :], func=AF.Sigmoid), reads=[("bank", 2)], writes=[("tmp", 0)])
            P.add("scalar", lambda e: e.activation(out=K.tmp[:, 1, :], in_=b3[:, :], func=AF.Sigmoid), reads=[("bank", 3)], writes=[("tmp", 1)])
            P.add("vector", lambda e: e.tensor_tensor(out=K.tmp[:, 0, :], in0=b0[:, :], in1=K.tmp[:, 0, :], op=ALU.mult), reads=[("bank", 0), ("tmp", 0)], writes=[("tmp", 0)])
            P.add("vector", lambda e: e.tensor_tensor(out=K.tmp[:, 1, :], in0=b1[:, :], in1=K.tmp[:, 1, :], op=ALU.mult), reads=[("bank", 1), ("tmp", 1)], writes=[("tmp", 1)])
            P.add("vector", lambda e, dt=dt, lc=lc: e.tensor_tensor(out=K.mix[:, dt, lc:lc + TBS], in0=K.tmp[:, 0, :], in1=K.tmp[:, 1, :], op=ALU.add),
                  reads=[("tmp", 0), ("tmp", 1)], writes=[("mix", dt, tb)])
    wo_v = W.w_out.rearrange("(ct p) d -> p ct d", p=128)
    for dt in range(DT):
        s = K.cnt_b % 2
        K.cnt_b += 1
        P.dma("gpsimd", lambda e, s=s, dt=dt: e.dma_start(out=K.wbr[:, s, 0:8, :], in_=wo_v[:, :, dt * 128:(dt + 1) * 128]), ("wbr", s),
              writes=[("wbr0", s), ("wbr1", s)])
        for tb in range(NTB):
            lc = tb * TBS
            c0 = c0s + lc
            pb = K.cnt_q % 2
            K.cnt_q += 1
            pd = K.bank[4 + pb]
            for ci in range(8):
                P.add("tensor", lambda e, ci=ci, s=s, lc=lc, pd=pd: e.matmul(pd[:, :], lhsT=K.wbr[:, s, ci, :], rhs=K.mix[:, ci, lc:lc + TBS], start=(ci == 0), stop=(ci == 7)),
                      reads=[("wbr0", s), ("wbr1", s), ("mix", ci, tb)], writes=[("bank", 4 + pb)])
            P.add("vector", lambda e, dt=dt, c0=c0, pd=pd: e.tensor_tensor(out=K.x[:, dt, c0:c0 + TBS], in0=pd[:, :], in1=K.x[:, dt, c0:c0 + TBS], op=ALU.add),
                  reads=[("bank", 4 + pb), ("x", dt, sb, tb)], writes=[("x", dt, sb, tb)])


def emit_final(K, sb, gcol, outT):
    P = K.P
    emit_norm_f32(K, sb, gcol)
    ov = outT.rearrange("(dt p) t -> p dt t", p=128)
    return [P.dma("sync", lambda e: e.dma_start(out=ov[:, :, sb * SBT:(sb + 1) * SBT], in_=K.x[:, :, sb * SBT:(sb + 1) * SBT]), ("ost", 0),
                  reads=[("x", dt, sb, tb) for dt in range(DT) for tb in range(NTB)])]


def emit_norm_f32(K, sb, gcol):
    P = K.P
    for tb in range(NTB):
        c0 = sb * SBT + tb * TBS
        psn = K.bank[6]
        for dt in range(DT):
            sq = K.sq[:, dt % 2, :]
            P.add("scalar", lambda e, sq=sq, dt=dt, c0=c0: e.activation(out=sq, in_=K.x[:, dt, c0:c0 + TBS], func=AF.Square),
                  reads=[("x", dt, sb, tb)], writes=[("sq", dt % 2)])
            P.add("tensor", lambda e, sq=sq, dt=dt: e.matmul(psn[:, :], lhsT=K.ones32[:, :], rhs=sq, start=(dt == 0), stop=(dt == DT - 1)),
                  reads=[("sq", dt % 2), "consts"], writes=[("bank", 6)])
        P.add("vector", lambda e: e.tensor_scalar(out=K.rstd[:, :], in0=psn[:, :], scalar1=1.0 / D, scalar2=EPS, op0=ALU.mult, op1=ALU.add),
              reads=[("bank", 6)], writes=["rstd"])
        P.add("scalar", lambda e: e.activation(out=K.rstd[:, :], in_=K.rstd[:, :], func=AF.Sqrt), reads=["rstd"], writes=["rstd"])
        P.add("vector", lambda e: e.reciprocal(out=K.rstd[:, :], in_=K.rstd[:, :]), reads=["rstd"], writes=["rstd"])
        for dt in range(DT):
            eng = "vector"
            P.add(eng, lambda e, dt=dt, c0=c0: e.scalar_tensor_tensor(out=K.x[:, dt, c0:c0 + TBS], in0=K.x[:, dt, c0:c0 + TBS], scalar=gcol(dt),
                                                                     in1=K.rstd[:, :], op0=ALU.mult, op1=ALU.mult),
                  reads=[("x", dt, sb, tb), "rstd", "vecs"], writes=[("x", dt, sb, tb)])


def build_T(do_tb, do_ta, do_final):
    nc = bass.Bass("TRN2", target_bir_lowering=False)
    dr = lambda name, shape, dt, kind="ExternalInput": nc.dram_tensor(name, shape, dt, kind=kind).ap()
    W = Ctx()
    xin = dr("xin", [D, NTOK], F32)
    vecs_d = dr("vecs", [128, 40], F32)
    ones_d = dr("ones32", [128, 128], F32)
    if do_tb:
        W.hT = dr("hT_in", [D, NTOK], BF16)
        W.zT = dr("zT_in", [512, NTOK], BF16)
        W.ymlT = dr("ymlT_in", [D, NTOK], BF16)
        W.glu_v = dr("glu_v", [512, 512], F32)
        W.glu_g = dr("glu_g", [512, 512], F32)
        W.w_br_s5 = dr("w_br_s5", [512, D], F32)
        W.w_br_ml = dr("w_br_ml", [D, D], F32)
        W.w_g = dr("w_g", [D, 2 * D], F32)
        W.w_out = dr("w_out", [D, D], F32)
        f2 = (dr("f2_wg", [D, DFF], F32), dr("f2_wu", [D, DFF], F32), dr("f2_wd", [DFF, D], F32))
    if do_ta:
        f1 = (dr("f1_wg", [D, DFF], F32), dr("f1_wu", [D, DFF], F32), dr("f1_wd", [DFF, D], F32))
        w_if = dr("w_if", [D, 8], F32)
        hT_out = dr("hT_out", [D, NTOK], BF16, "ExternalOutput")
        ifp_out = dr("ifp_out", [NTOK, 8], F32, "ExternalOutput")
    xout = dr("xout", [D, NTOK], F32, "ExternalOutput")

    with contextlib.ExitStack() as st:
        K = Ctx()
        K.nc = nc
        K.P = P = Prog(nc)
        sb_t = lambda name, shape, dt: st.enter_context(nc.sbuf_tensor(name, shape, dt))
        K.x = sb_t("x", [128, DT, NTOK], F32)
        K.h = sb_t("h", [128, DT, SBT], BF16)
        K.act = sb_t("act", [128, 24, SBT], BF16)
        K.z = K.act[:, 0:4, :]
        K.ys5 = K.act[:, 4:8, :]
        K.yml = K.act[:, 8:16, :]
        K.mix = K.act[:, 16:24, :]
        K.sq = sb_t("sq", [128, 2, TBS], F32)
        K.tmp = sb_t("tmp", [128, 2, TBS], F32)
        K.rstd = sb_t("rstd", [128, TBS], F32)
        K.wgu = sb_t("wgu", [128, 2, 2, DT, 256], BF16)
        K.wd = sb_t("wd", [128, 2, FT, 256], BF16)
        K.wbr = sb_t("wbr", [128, 2, 28, 128], BF16)
        K.wglu = sb_t("wglu", [128, 2, 4, 512], BF16)
        K.wif = sb_t("wif", [128, DT, 8], BF16)
        K.ifs = sb_t("ifs", [128, (SBT // 128) * 8], F32)
        K.vecs = sb_t("vecs_s", [128, 40], F32)
        K.ones32 = sb_t("ones_s", [128, 128], F32)
        K.bank = [st.enter_context(nc.psum_tensor("bank%d" % i, [128, 512], F32)) for i in range(8)]
        K.cnt_w = K.cnt_p = K.cnt_d = K.cnt_q = K.cnt_b = 0

        P.dma("sync", lambda e: e.dma_start(out=K.vecs[:, :], in_=vecs_d), ("c", 0), writes=["vecs"])
        P.dma("sync", lambda e: e.dma_start(out=K.ones32[:, :], in_=ones_d), ("c", 1), writes=["consts"])
        xv = xin.rearrange("(dt p) t -> p dt t", p=128)
        for sb in range(NSB):
            P.dma("sync", lambda e, sb=sb: e.dma_start(out=K.x[:, :, sb * SBT:(sb + 1) * SBT], in_=xv[:, :, sb * SBT:(sb + 1) * SBT]), ("xld", sb),
                  writes=[("x", dt, sb, tb) for dt in range(DT) for tb in range(NTB)])
        finals = []
        for sb in range(NSB):
            if do_tb:
                P.barrier()
                emit_tb(K, sb, W)
                P.barrier()
                emit_ffn(K, sb, f2[0], f2[1], f2[2], lambda dt: K.vecs[:, dt:dt + 1], "f2")
            if do_ta:
                emit_ffn(K, sb, f1[0], f1[1], f1[2], lambda dt: K.vecs[:, 8 + dt:9 + dt], "f1")
                finals += emit_ta_tail(K, sb, lambda dt: K.vecs[:, 16 + dt:17 + dt], w_if, hT_out, ifp_out)
                xo = xout.rearrange("(dt p) t -> p dt t", p=128)
                finals.append(P.dma("sync", lambda e, sb=sb, xo=xo: e.dma_start(out=xo[:, :, sb * SBT:(sb + 1) * SBT], in_=K.x[:, :, sb * SBT:(sb + 1) * SBT]),
                                    ("ost", 0), reads=[("x", dt, sb, tb) for dt in range(DT) for tb in range(NTB)]))
            if do_final:
                finals += emit_final(K, sb, lambda dt: K.vecs[:, 24 + dt:25 + dt], xout)
        P.emit(final_waits=finals)
    return nc


def _cols(v):
    return np.ascontiguousarray(np.asarray(v, np.float32).reshape(8, 128).T)


_NC_CACHE = {}


def _get_T(do_tb, do_ta, do_final):
    return build_T(do_tb, do_ta, do_final)


def _run(nc, in_maps):
    res = run_bass_kernel_spmd(nc, in_maps, core_ids=list(range(8)))
    return res.results


def run_T(I, l, x_sh, mixer_out, do_tb, do_ta, do_final):
    ones = np.ones((128, 128), np.float32)
    in_maps = []
    lta = l + 1 if do_tb else 0
    for c in range(8):
        vecs = np.zeros((128, 40), np.float32)
        m = {"xin": x_sh[c], "ones32": ones}
        if do_tb:
            vecs[:, 0:8] = _cols(I["ffn2_norm"][l])
            hT, zT, ymlT = mixer_out
            m.update({"hT_in": hT[c], "zT_in": zT[c], "ymlT_in": ymlT[c],
                      "glu_v": I["s5_glu_v"][l], "glu_g": I["s5_glu_g"][l], "w_br_s5": I["w_br_s5"][l], "w_br_ml": I["w_br_ml"][l],
                      "w_g": np.ascontiguousarray(I["w_in"][l][:, 3592:]), "w_out": I["w_out"][l],
                      "f2_wg": I["ffn2_wg"][l], "f2_wu": I["ffn2_wu"][l], "f2_wd": I["ffn2_wd"][l]})
        if do_ta:
            vecs[:, 8:16] = _cols(I["ffn1_norm"][lta])
            vecs[:, 16:24] = _cols(I["mix_norm"][lta])
            m.update({"f1_wg": I["ffn1_wg"][lta], "f1_wu": I["ffn1_wu"][lta], "f1_wd": I["ffn1_wd"][lta],
                      "w_if": np.ascontiguousarray(I["w_in"][lta][:, 3584:3592])})
        if do_final:
            vecs[:, 24:32] = _cols(I["final_norm"])
        m["vecs"] = vecs
        in_maps.append(m)
    nc = _get_T(do_tb, do_ta, do_final)
    return _run(nc, in_maps)


def kernel(**inputs):
    I = {k: np.asarray(v) for k, v in inputs.items()}
    x = I["x"]
    x_sh = [np.ascontiguousarray(x[c // 4, (c % 4) * NTOK:(c % 4 + 1) * NTOK, :].T) for c in range(8)]
    res = run_T(I, 0, x_sh, None, False, True, False)
    for l in range(DEPTH):
        x_sh = [res[c]["xout"] for c in range(8)]
        hT_sh = [res[c]["hT_out"] for c in range(8)]
        ifp_sh = [res[c]["ifp_out"] for c in range(8)]
        mres = run_M(I, l, hT_sh, ifp_sh)
        if os.environ.get('KSTOP') == 'M0':
            return np.zeros((2, SEQ, D), np.float32)
        zT_in, ymlT_in = [], []
        for c in range(8):
            b, r = c // 4, c % 4
            sl = slice(r * NTOK, (r + 1) * NTOK)
            zT_in.append(np.ascontiguousarray(np.concatenate([mres[4 * b + q]["zT"][:, sl] for q in range(4)], axis=0)))
            ymlT_in.append(np.ascontiguousarray(np.concatenate([mres[4 * b + q]["ymlT"][:, sl] for q in range(4)], axis=0)))
        last = (l == DEPTH - 1)
        res = run_T(I, l, x_sh, (hT_sh, zT_in, ymlT_in), True, not last, last)
    out = np.zeros((2, SEQ, D), np.float32)
    for c in range(8):
        out[c // 4, (c % 4) * NTOK:(c % 4 + 1) * NTOK, :] = res[c]["xout"].T
    return out


NBLK = SEQ // TBS
NCH = SEQ // 128
import os
MLEVEL = int(os.environ.get('MLEVEL', '9'))
MSUB = int(os.environ.get('MSUB', '9'))


def build_M():
    nc = bass.Bass("TRN2", target_bir_lowering=False)
    dr = lambda name, shape, dt, kind="ExternalInput": nc.dram_tensor(name, shape, dt, kind=kind).ap()
    hT = dr("hT", [D, SEQ], BF16)
    ifg_d = dr("ifg", [128, NCH, 2], F32)
    w_m = dr("w_m", [D, 896], F32)
    wq_d = dr("wq", [256, 256], F32)
    wk_d = dr("wk", [256, 256], F32)
    convw_d = dr("convw", [128, 2, 4], F32)
    vm_d = dr("vm", [128, 16], F32)
    lamC_d = dr("lamC", [128, 12], F32)
    lamR_d = dr("lamR", [128, 3, 512], F32)
    BT_d = dr("BT", [128, 2, 512], F32)
    CT_d = dr("CT", [128, 2, 4, 128], F32)
    cf_d = dr("cf32", [128, 4, 128], F32)
    cb_d = dr("cb16", [128, 128], BF16)
    iota_d = dr("iota", [128, 512], F32)
    zT_o = dr("zT", [128, SEQ], BF16, "ExternalOutput")
    ymlT_o = dr("ymlT", [256, SEQ], BF16, "ExternalOutput")

    with contextlib.ExitStack() as st:
        P = Prog(nc)
        T = lambda name, shape, dt: st.enter_context(nc.sbuf_tensor(name, shape, dt))
        wm = T("wm", [128, DT, 896], BF16)
        wq = T("wq_s", [128, 2, 256], BF16)
        wk = T("wk_s", [128, 2, 256], BF16)
        hblk = T("hblk", [128, 2, DT, TBS], BF16)
        convw = T("convw_s", [128, 2, 4], F32)
        vm = T("vm_s", [128, 16], F32)
        lamC = T("lamC_s", [128, 12], F32)
        cf = T("cf_s", [128, 4, 128], F32)
        cb = T("cb_s", [128, 128], BF16)
        iota = T("iota_s", [128, 512], F32)
        R = T("R", [128, 12, 512], F32)
        CTf = T("CTf", [128, 2, 4, 128], F32)
        CTb = T("CTb", [128, 2, 4, 128], BF16)
        BbT = T("BbT", [128, 2, 512], BF16)
        cosT = T("cosT", [128, 4, 512], F32)
        sinT = T("sinT", [128, 4, 512], F32)
        rT = T("rT", [128, 4, 512], F32)
        cs = T("cs", [128, 32], F32)
        ubf = T("ubf", [128, TBS], BF16)
        du = T("du", [128, TBS], F32)
        t1 = T("t1", [128, 2, TBS], F32)
        t2 = T("t2", [128, 2, TBS], F32)
        zin = T("zin", [128, 2, TBS], F32)
        zr = T("zr", [128, 4, TBS], F32)
        zi = T("zi", [128, 4, TBS], F32)
        init = T("init", [128, 4, 4], F32)
        pt = T("pt", [128, 4, TBS], F32)
        sre = T("sre", [128, 2, TBS], BF16)
        sim_ = T("sim", [128, 2, TBS], BF16)
        yy = T("yy", [128, TBS], F32)
        g1 = T("g1", [128, TBS], F32)
        g2 = T("g2", [128, TBS], F32)
        zout = T("zout", [128, 2, TBS], BF16)
        xpad = T("xpad", [128, 2, TBS + 3], F32)
        cacc = T("cacc", [128, 2, TBS], F32)
        xc32 = T("xc32", [128, 2, TBS], F32)
        xcb = T("xcb", [128, 2, TBS], BF16)
        skx = T("skx", [128, 2, TBS], F32)
        qT = T("qT", [128, 2, TBS], BF16)
        kT = T("kT", [128, 2, TBS], BF16)
        kp = T("kp", [128, 4, 256], BF16)
        vext = T("vext", [128, 4, 257], BF16)
        og = T("og", [128, 4, 256], F32)
        Sm = T("Sm", [128, 2, 128], BF16)
        hc4 = T("hc4", [128, 4, 256], F32)
        ssq = T("ssq", [128, 8], F32)
        hsq = T("hsq", [128, 256], F32)
        hn = T("hn", [128, 2, 256], BF16)
        sm = T("sm", [128, 8], F32)
        Cst = T("Cst", [128, 2, 256], F32)
        nst = T("nst", [128, 2], F32)
        Cwb = T("Cwb", [128, 2, 257], BF16)
        ymlo = T("ymlo", [128, 2, 2, TBS], BF16)
        ifs = T("ifs", [128, NCH, 2], F32)
        gsc = T("gsc", [128, 8, NCH], F32)
        grow = T("grow", [128, 4, NCH + 1], F32)
        abc = T("abc", [64, 128], F32)
        acol = T("acol", [64, 1], F32)
        bank = [st.enter_context(nc.psum_tensor("bank%d" % i, [128, 512], F32)) for i in range(7)]
        bankT = st.enter_context(nc.psum_tensor("bankT", [128, 1024], BF16))

        ident = cf[:, 0, :]
        tri = cf[:, 1, :]
        ones = cf[:, 2, :]

        ld = lambda eng, out, in_, key, tok: P.dma(eng, lambda e: e.dma_start(out=out, in_=in_), key, writes=[tok])
        wmv = w_m.rearrange("(dt p) c -> p dt c", p=128)
        for k in range(7):
            P.dma("gpsimd", lambda e, k=k: e.dma_start(out=wm[:, :, k * 128:(k + 1) * 128], in_=wmv[:, :, k * 128:(k + 1) * 128]), ("w", 0), writes=["wm"])
        for k in range(2):
            P.dma("gpsimd", lambda e, k=k: e.dma_start(out=wq[:, :, k * 128:(k + 1) * 128], in_=wq_d.rearrange("(dt p) c -> p dt c", p=128)[:, :, k * 128:(k + 1) * 128]),
                  ("w", 1), writes=["wq"])
            P.dma("gpsimd", lambda e, k=k: e.dma_start(out=wk[:, :, k * 128:(k + 1) * 128], in_=wk_d.rearrange("(dt p) c -> p dt c", p=128)[:, :, k * 128:(k + 1) * 128]),
                  ("w", 2), writes=["wk"])
        for k in range(4):
            P.dma("gpsimd", lambda e, k=k: e.dma_start(out=CTb[:, 0, k, :], in_=CT_d[:, 0, k, :]), ("w", 3), writes=["CTb0"])
        ld("sync", CTf[:, :, :, :], CT_d, ("w", 4), "CTf")
        ld("sync", convw[:, :, :], convw_d, ("w", 5), "convw")
        ld("sync", vm[:, :], vm_d, ("w", 6), "vm")
        ld("sync", lamC[:, :], lamC_d, ("w", 7), "lamC")
        ld("sync", cf[:, :, :], cf_d, ("w", 8), "cf")
        ld("sync", cb[:, :], cb_d, ("w", 9), "cb")
        ld("sync", iota[:, :], iota_d, ("w", 10), "iota")
        ld("sync", R[:, 0:3, :], lamR_d, ("w", 11), "R012")
        ld("sync", R[:, 8:10, :], BT_d, ("w", 12), "R89")
        ld("sync", ifs[:, :, :], ifg_d, ("w", 13), "ifs")

        V = lambda fn, r=(), w=(): P.add("vector", fn, reads=r, writes=w)
        A = lambda fn, r=(), w=(): P.add("scalar", fn, reads=r, writes=w)
        G = lambda fn, r=(), w=(): P.add("vector" if os.environ.get("POOL2V") else "gpsimd", fn, reads=r, writes=w)
        PE = lambda fn, r=(), w=(): P.add("tensor", fn, reads=r, writes=w)

        RI = T("RI", [128, 512], mybir.dt.int32)

        def sincos(out_s, out_c, ang, tmpa, tmpb, rtoks, wtoks, ttoks):
            n = tmpa.shape[-1]
            it = RI[:, 0:n]
            ta, tb = ttoks
            V(lambda e: e.tensor_scalar(out=tmpa, in0=ang, scalar1=1.0 / TWO_PI, scalar2=None, op0=ALU.mult), rtoks, [ta])
            for k, (o_ap, wt) in enumerate(((out_s, wtoks[0]), (out_c, wtoks[1]))):
                if k == 1:
                    V(lambda e: e.tensor_scalar(out=tmpa, in0=tmpa, scalar1=0.25, scalar2=None, op0=ALU.add), [ta], [ta])
                V(lambda e: e.tensor_copy(out=it, in_=tmpa), [ta], ["RI"])
                V(lambda e: e.tensor_copy(out=tmpb, in_=it), ["RI"], [tb])
                V(lambda e: e.tensor_tensor(out=tmpb, in0=tmpa, in1=tmpb, op=ALU.subtract), [ta, tb], [tb])
                A(lambda e, o_ap=o_ap: e.activation(out=o_ap, in_=tmpb, func=AF.Sin, scale=TWO_PI), [tb], [wt])

        V(lambda e: e.memset(zr[:, :, :], 0.0), [], ["zr_all"])
        A(lambda e: e.activation(out=cs[:, 0:4], in_=lamC[:, 8:12], func=AF.Exp), ["lamC"], ["cs_dt"])
        V(lambda e: e.tensor_scalar(out=cs[:, 4:8], in0=lamC[:, 0:4], scalar1=-1e-4, scalar2=None, op0=ALU.min), ["lamC"], ["cs_lr"])
        V(lambda e: e.tensor_tensor(out=cs[:, 28:32], in0=cs[:, 4:8], in1=cs[:, 0:4], op=ALU.mult), ["cs_lr", "cs_dt"], ["cs_tmp"])
        A(lambda e: e.activation(out=cs[:, 8:12], in_=cs[:, 28:32], func=AF.Exp), ["cs_tmp"], ["cs_mag"])
        V(lambda e: e.tensor_tensor(out=cs[:, 12:16], in0=lamC[:, 4:8], in1=cs[:, 0:4], op=ALU.mult), ["lamC", "cs_dt"], ["cs_th"])
        for j in range(4):
            V(lambda e, j=j: e.tensor_scalar(out=R[:, 10, :], in0=iota[:, :], scalar1=cs[:, 12 + j:13 + j], scalar2=None, op0=ALU.mult), ["iota", "cs_th"], ["R10"])
            sincos(sinT[:, j, :], cosT[:, j, :], R[:, 10, :], R[:, 11, :], R[:, 3, :], ["R10"], [("sinT", j), ("cosT", j)], ["R11", "R3"])
            V(lambda e, j=j: e.memset(rT[:, j, :], 1.0), [], [("rT", j)])
            V(lambda e, j=j: e.tensor_scalar(out=rT[:, j, :], in0=rT[:, j, :], scalar1=cs[:, 8 + j:9 + j], scalar2=None, op0=ALU.mult),
              [("rT", j), "cs_mag"], [("rT", j)])
        V(lambda e: e.tensor_scalar(out=cs[:, 28:32], in0=cs[:, 12:16], scalar1=float(TBS), scalar2=None, op0=ALU.mult), ["cs_th", "cs_tmp"], ["cs_tmp"])
        sincos(cs[:, 20:24], cs[:, 16:20], cs[:, 28:32], R[:, 11, 0:4], R[:, 3, 0:4], ["cs_tmp"], ["cs_Es", "cs_Ec"], ["R11", "R3"])
        V(lambda e: e.tensor_scalar(out=cs[:, 24:28], in0=cs[:, 20:24], scalar1=-1.0, scalar2=None, op0=ALU.mult), ["cs_Es"], ["cs_nEs"])
        A(lambda e: e.activation(out=R[:, 2, :], in_=R[:, 2, :], func=AF.Exp), ["R012"], ["R2"])
        V(lambda e: e.tensor_scalar(out=R[:, 0, :], in0=R[:, 0, :], scalar1=-1e-4, scalar2=None, op0=ALU.min), ["R012"], ["R0"])
        V(lambda e: e.tensor_tensor(out=R[:, 4, :], in0=R[:, 0, :], in1=R[:, 2, :], op=ALU.mult), ["R0", "R2"], ["R4"])
        A(lambda e: e.activation(out=R[:, 4, :], in_=R[:, 4, :], func=AF.Exp), ["R4"], ["R4"])
        V(lambda e: e.tensor_tensor(out=R[:, 5, :], in0=R[:, 1, :], in1=R[:, 2, :], op=ALU.mult), ["R012", "R2"], ["R5"])
        sincos(R[:, 6, :], R[:, 7, :], R[:, 5, :], R[:, 11, :], R[:, 3, :], ["R5"], ["R6", "R7"], ["R11", "R3"])
        V(lambda e: e.tensor_tensor(out=R[:, 6, :], in0=R[:, 6, :], in1=R[:, 4, :], op=ALU.mult), ["R6", "R4"], ["R6"])
        V(lambda e: e.tensor_tensor(out=R[:, 7, :], in0=R[:, 7, :], in1=R[:, 4, :], op=ALU.mult), ["R7", "R4"], ["R7"])
        V(lambda e: e.tensor_scalar(out=R[:, 7, :], in0=R[:, 7, :], scalar1=-1.0, scalar2=None, op0=ALU.add), ["R7"], ["R7"])
        V(lambda e: e.tensor_tensor(out=R[:, 4, :], in0=R[:, 0, :], in1=R[:, 0, :], op=ALU.mult), ["R0", "R4"], ["R4"])
        V(lambda e: e.tensor_tensor(out=R[:, 5, :], in0=R[:, 1, :], in1=R[:, 1, :], op=ALU.mult), ["R012", "R5"], ["R5"])
        V(lambda e: e.tensor_tensor(out=R[:, 4, :], in0=R[:, 4, :], in1=R[:, 5, :], op=ALU.add), ["R4", "R5"], ["R4"])
        V(lambda e: e.reciprocal(out=R[:, 4, :], in_=R[:, 4, :]), ["R4"], ["R4"])
        V(lambda e: e.tensor_tensor(out=R[:, 5, :], in0=R[:, 7, :], in1=R[:, 0, :], op=ALU.mult), ["R7", "R0", "R5"], ["R5"])
        V(lambda e: e.tensor_tensor(out=R[:, 11, :], in0=R[:, 6, :], in1=R[:, 1, :], op=ALU.mult), ["R6", "R012", "R11"], ["R11"])
        V(lambda e: e.tensor_tensor(out=R[:, 5, :], in0=R[:, 5, :], in1=R[:, 11, :], op=ALU.add), ["R5", "R11"], ["R5"])
        V(lambda e: e.tensor_tensor(out=R[:, 5, :], in0=R[:, 5, :], in1=R[:, 4, :], op=ALU.mult), ["R5", "R4"], ["R5"])
        V(lambda e: e.tensor_tensor(out=R[:, 11, :], in0=R[:, 6, :], in1=R[:, 0, :], op=ALU.mult), ["R6", "R0", "R11"], ["R11"])
        V(lambda e: e.tensor_tensor(out=R[:, 3, :], in0=R[:, 7, :], in1=R[:, 1, :], op=ALU.mult), ["R7", "R012", "R3"], ["R3"])
        V(lambda e: e.tensor_tensor(out=R[:, 11, :], in0=R[:, 11, :], in1=R[:, 3, :], op=ALU.subtract), ["R11", "R3"], ["R11"])
        V(lambda e: e.tensor_tensor(out=R[:, 11, :], in0=R[:, 11, :], in1=R[:, 4, :], op=ALU.mult), ["R11", "R4"], ["R11"])
        V(lambda e: e.tensor_tensor(out=R[:, 3, :], in0=R[:, 5, :], in1=R[:, 8, :], op=ALU.mult), ["R5", "R89", "R3"], ["R3"])
        V(lambda e: e.tensor_tensor(out=R[:, 6, :], in0=R[:, 11, :], in1=R[:, 9, :], op=ALU.mult), ["R11", "R89", "R6"], ["R6"])
        V(lambda e: e.tensor_tensor(out=BbT[:, 0, :], in0=R[:, 3, :], in1=R[:, 6, :], op=ALU.subtract), ["R3", "R6"], ["BbT0"])
        V(lambda e: e.tensor_tensor(out=R[:, 3, :], in0=R[:, 5, :], in1=R[:, 9, :], op=ALU.mult), ["R5", "R89", "R3"], ["R3"])
        V(lambda e: e.tensor_tensor(out=R[:, 6, :], in0=R[:, 11, :], in1=R[:, 8, :], op=ALU.mult), ["R11", "R89", "R6"], ["R6"])
        V(lambda e: e.tensor_tensor(out=BbT[:, 1, :], in0=R[:, 3, :], in1=R[:, 6, :], op=ALU.add), ["R3", "R6"], ["BbT1"])
        A(lambda e: e.activation(out=CTb[:, 1, :, :], in_=CTf[:, 1, :, :], func=AF.Copy, scale=-1.0), ["CTf"], ["CTb1"])

        if MLEVEL == 0:
            P.emit()
            return nc
        PE_real = PE
        if os.environ.get('MSKIPG'):
            PE = lambda fn, r=(), w=(): None
        V(lambda e: e.tensor_scalar(out=gsc[:, 7, :], in0=ifs[:, :, 1], scalar1=vm[:, 9:10], scalar2=None, op0=ALU.add), ["ifs", "vm"], ["g7"])
        A(lambda e: e.activation(out=gsc[:, 7, :], in_=gsc[:, 7, :], func=AF.Exp, scale=-1.0), ["g7"], ["g7"])
        A(lambda e: e.activation(out=gsc[:, 0, :], in_=gsc[:, 7, :], func=AF.Ln, bias=1.0), ["g7"], ["g0"])
        V(lambda e: e.tensor_scalar(out=gsc[:, 0, :], in0=gsc[:, 0, :], scalar1=-1.0, scalar2=None, op0=ALU.mult), ["g0"], ["g0"])
        PE(lambda e: e.matmul(bank[0][:, 0:NCH], lhsT=tri, rhs=gsc[:, 0, :], start=True, stop=True), ["cf", "g0"], [("bank", 0)])
        V(lambda e: e.tensor_copy(out=gsc[:, 1, :], in_=bank[0][:, 0:NCH]), [("bank", 0)], ["g1"])
        V(lambda e: e.scalar_tensor_tensor(out=gsc[:, 2, :], in0=ifs[:, :, 0], scalar=vm[:, 8:9], in1=gsc[:, 1, :], op0=ALU.add, op1=ALU.subtract),
          ["ifs", "vm", "g1"], ["g2"])
        PE(lambda e: e.transpose(out=bank[1][0:NCH, 0:128], in_=gsc[:, 2, :], identity=ident), ["g2", "cf"], [("bank", 1)])
        V(lambda e: e.tensor_reduce(out=acol[:, :], in_=bank[1][0:NCH, 0:128], axis=AX.X, op=ALU.max), [("bank", 1)], ["acol"])
        V(lambda e: e.tensor_scalar(out=abc[:, :], in0=cf[0:64, 2, :], scalar1=acol[:, 0:1], scalar2=None, op0=ALU.mult), ["acol", "cf"], ["abc"])
        PE(lambda e: e.matmul(bank[2][:, 0:NCH], lhsT=abc[:, :], rhs=cf[0:64, 0, 0:64], start=True, stop=True), ["abc", "cf"], [("bank", 2)])
        PE(lambda e: e.matmul(bank[3][:, 0:NCH], lhsT=cf[:, 3, :], rhs=gsc[:, 1, :], start=True, stop=True), ["g1", "cf"], [("bank", 3)])
        V(lambda e: e.tensor_copy(out=grow[:, 1, 0:NCH], in_=bank[2][:, 0:NCH]), [("bank", 2)], ["gr1"])
        V(lambda e: e.tensor_copy(out=grow[:, 0, 0:NCH], in_=bank[3][:, 0:NCH]), [("bank", 3)], ["gr0"])
        V(lambda e: e.tensor_tensor(out=grow[:, 2, 0:NCH], in0=grow[:, 1, 0:NCH], in1=grow[:, 0, 0:NCH], op=ALU.add), ["gr0", "gr1"], ["gr2"])
        V(lambda e: e.memset(grow[:, 3, 0:1], 0.0), [], ["gr3a"])
        V(lambda e: e.tensor_tensor_scan(out=grow[:, 3, 1:NCH + 1], data0=grow[:, 0, 0:NCH], data1=grow[:, 2, 0:NCH], initial=0.0, op0=ALU.add, op1=ALU.max),
          ["gr0", "gr2", "gr3a"], ["gr3"])
        V(lambda e: e.tensor_tensor(out=gsc[:, 5, :], in0=grow[:, 3, 1:NCH + 1], in1=grow[:, 0, 0:NCH], op=ALU.subtract), ["gr3", "gr0"], ["g5"])
        V(lambda e: e.tensor_tensor(out=gsc[:, 6, :], in0=grow[:, 3, 0:NCH], in1=gsc[:, 5, :], op=ALU.subtract), ["gr3", "gr3a", "g5"], ["g6"])
        A(lambda e: e.activation(out=gsc[:, 6, :], in_=gsc[:, 6, :], func=AF.Exp), ["g6"], ["g6"])
        V(lambda e: e.tensor_tensor(out=gsc[:, 3, :], in0=gsc[:, 2, :], in1=gsc[:, 5, :], op=ALU.subtract), ["g2", "g5"], ["g3"])
        A(lambda e: e.activation(out=gsc[:, 3, :], in_=gsc[:, 3, :], func=AF.Exp), ["g3"], ["g3"])
        V(lambda e: e.tensor_tensor(out=gsc[:, 4, :], in0=gsc[:, 1, :], in1=gsc[:, 5, :], op=ALU.add), ["g1", "g5"], ["g4"])
        A(lambda e: e.activation(out=gsc[:, 4, :], in_=gsc[:, 4, :], func=AF.Exp, scale=-1.0), ["g4"], ["g4"])

        if MLEVEL == 1:
            P.emit()
            return nc
        PE = PE_real
        if MSUB == -1:
            V = lambda fn, r=(), w=(): None
        V(lambda e: e.memset(Cst[:, :, :], 0.0), [], ["Cst"])
        V(lambda e: e.memset(nst[:, :], 0.0), [], ["nst"])
        V(lambda e: e.memset(Cwb[:, :, :], 0.0), [], ["Cwb"])
        V(lambda e: e.memset(xpad[:, :, 0:3], 0.0), [], [("xpadh", 0), ("xpadh", 1)])
        V(lambda e: e.memset(vext[:, :, 256:257], 1.0), [], ["vones"])

        V = lambda fn, r=(), w=(): P.add("vector", fn, reads=r, writes=w)
        hv = hT.rearrange("(dt p) t -> p dt t", p=128)
        yv = ymlT_o.rearrange("(e p) t -> p e t", p=128)
        finals = []
        for bi in range(NBLK if MLEVEL >= 5 else (2 if MLEVEL == 4 else 1)):
            c0 = bi * TBS
            s = bi % 2
            P.dma("sync", lambda e, s=s, c0=c0: e.dma_start(out=hblk[:, s, :, :], in_=hv[:, :, c0:c0 + TBS]), ("hblk", s), writes=[("hblk", s)])
            HB = ("hblk", s)
            b0 = bank[0]
            if MSUB == -2:
                continue
            for dt in range(DT):
                PE(lambda e, dt=dt, s=s: e.matmul(b0[:, :], lhsT=wm[:, dt, 0:128], rhs=hblk[:, s, dt, :], start=(dt == 0), stop=(dt == DT - 1)),
                   ["wm", HB], [("bank", 0)])
            if MSUB == -3:
                continue
            A(lambda e: e.activation(out=ubf[:, :], in_=b0[:, :], func=AF.Copy), [("bank", 0)], ["ubf"])
            if MSUB == -4:
                continue
            V(lambda e: e.tensor_scalar(out=du[:, :], in0=b0[:, :], scalar1=vm[:, 6:7], scalar2=None, op0=ALU.mult), [("bank", 0), "vm", "ubf"], ["du"])
            for eo in range(2 if MSUB >= 1 else 0):
                for dt in range(DT):
                    PE(lambda e, dt=dt, s=s, eo=eo: e.matmul(b0[:, :], lhsT=wm[:, dt, 128 + eo * 128:256 + eo * 128], rhs=hblk[:, s, dt, :],
                                                             start=(dt == 0), stop=(dt == DT - 1)), ["wm", HB], [("bank", 0)])
                A(lambda e, eo=eo: e.activation(out=xpad[:, eo, 3:3 + TBS], in_=b0[:, :], func=AF.Copy), [("bank", 0)], [("xpad", eo)])
                XR = [("xpad", eo), ("xpadh", eo)]
                V(lambda e, eo=eo: e.tensor_scalar(out=cacc[:, eo, :], in0=xpad[:, eo, 0:TBS], scalar1=convw[:, eo, 0:1], scalar2=vm[:, eo:eo + 1],
                                                   op0=ALU.mult, op1=ALU.add), XR + ["convw", "vm"], [("cacc", eo)])
                for j in range(1, 4):
                    V(lambda e, eo=eo, j=j: e.scalar_tensor_tensor(out=cacc[:, eo, :], in0=xpad[:, eo, j:j + TBS], scalar=convw[:, eo, j:j + 1],
                                                                   in1=cacc[:, eo, :], op0=ALU.mult, op1=ALU.add), XR + [("cacc", eo), "convw"], [("cacc", eo)])
                G(lambda e, eo=eo: e.tensor_copy(out=xpad[:, eo, 0:3], in_=xpad[:, eo, TBS:TBS + 3]), XR, [("xpadh", eo)])
                A(lambda e, eo=eo: e.activation(out=xc32[:, eo, :], in_=cacc[:, eo, :], func=AF.Sigmoid), [("cacc", eo)], [("xc32", eo)])
                G(lambda e, eo=eo: e.tensor_tensor(out=xc32[:, eo, :], in0=xc32[:, eo, :], in1=cacc[:, eo, :], op=ALU.mult), [("xc32", eo), ("cacc", eo)], [("xc32", eo)])
                G(lambda e, eo=eo: e.tensor_copy(out=xcb[:, eo, :], in_=xc32[:, eo, :]), [("xc32", eo)], [("xcb", eo)])
                A(lambda e, eo=eo: e.activation(out=skx[:, eo, :], in_=xc32[:, eo, :], func=AF.Copy, scale=vm[:, 4 + eo:5 + eo]), [("xc32", eo), "vm"], [("skx", eo)])
            for eo in range(2 if MSUB >= 2 else 0):
                for dd in range(2):
                    PE(lambda e, eo=eo, dd=dd: e.matmul(b0[:, :], lhsT=wq[:, dd, eo * 128:(eo + 1) * 128], rhs=xcb[:, dd, :], start=(dd == 0), stop=(dd == 1)),
                       ["wq", ("xcb", dd)], [("bank", 0)])
                A(lambda e, eo=eo: e.activation(out=qT[:, eo, :], in_=b0[:, :], func=AF.Copy, scale=1.0 / 16.0), [("bank", 0)], [("qT", eo)])
                for dd in range(2):
                    PE(lambda e, eo=eo, dd=dd: e.matmul(b0[:, :], lhsT=wk[:, dd, eo * 128:(eo + 1) * 128], rhs=xcb[:, dd, :], start=(dd == 0), stop=(dd == 1)),
                       ["wk", ("xcb", dd)], [("bank", 0)])
                V(lambda e, eo=eo: e.tensor_copy(out=kT[:, eo, :], in_=b0[:, :]), [("bank", 0)], [("kT", eo)])
            for i in range(4):
                j = i
                ch = i
                gch = bi * 4 + ch
                cc = ch * 128
                ss = gch % 2
                ZR, ZI = ("zr", j), ("zi", j)
                sj = j % 2
                PE(lambda e, j=j: e.matmul(bank[2][:, :], lhsT=BbT[:, 0, j * 128:(j + 1) * 128], rhs=ubf[:, :], start=True, stop=True), ["BbT0", "ubf"], [("bank", 2)])
                PE(lambda e, j=j: e.matmul(bank[3][:, :], lhsT=BbT[:, 1, j * 128:(j + 1) * 128], rhs=ubf[:, :], start=True, stop=True), ["BbT1", "ubf"], [("bank", 3)])
                for dd in range(2):
                    PE(lambda e, dd=dd, cc=cc: e.matmul(bank[1][:, 256:512], lhsT=xcb[:, dd, cc:cc + 128], rhs=wk[:, dd, :], start=(dd == 0), stop=(dd == 1)),
                       ["wk", ("xcb", dd)], [("bank", 1, "k")])
                for dt in range(DT):
                    PE(lambda e, dt=dt, s=s, cc=cc: e.matmul(b0[:, :], lhsT=hblk[:, s, dt, cc:cc + 128], rhs=wm[:, dt, 384:896], start=(dt == 0), stop=(dt == DT - 1)),
                       ["wm", HB], [("bank", 0)])
                for dd in range(2):
                    PE(lambda e, dd=dd, cc=cc: e.matmul(bank[5][:, 0:128], lhsT=kT[:, dd, cc:cc + 128], rhs=qT[:, dd, cc:cc + 128], start=(dd == 0), stop=(dd == 1)),
                       [("kT", dd), ("qT", dd)], [("bank", 5, "s")])
                V(lambda e, j=j: e.tensor_tensor(out=t1[:, 0, :], in0=bank[2][:, :], in1=cosT[:, j, :], op=ALU.mult), [("bank", 2), ("cosT", j)], [("t1", 0)])
                V(lambda e, j=j: e.tensor_tensor(out=t2[:, 0, :], in0=bank[3][:, :], in1=sinT[:, j, :], op=ALU.mult), [("bank", 3), ("sinT", j)], [("t2", 0)])
                G(lambda e: e.tensor_tensor(out=zin[:, 0, :], in0=t1[:, 0, :], in1=t2[:, 0, :], op=ALU.add), [("t1", 0), ("t2", 0)], [("zin", 0)])
                V(lambda e, j=j: e.tensor_tensor(out=t1[:, 1, :], in0=bank[3][:, :], in1=cosT[:, j, :], op=ALU.mult), [("bank", 3), ("cosT", j)], [("t1", 1)])
                V(lambda e, j=j: e.tensor_tensor(out=t2[:, 1, :], in0=bank[2][:, :], in1=sinT[:, j, :], op=ALU.mult), [("bank", 2), ("sinT", j)], [("t2", 1)])
                G(lambda e: e.tensor_tensor(out=zin[:, 1, :], in0=t1[:, 1, :], in1=t2[:, 1, :], op=ALU.subtract), [("t1", 1), ("t2", 1)], [("zin", 1)])
                V(lambda e, ch=ch, gch=gch: e.tensor_scalar(out=kp[:, ch, :], in0=bank[1][:, 256:512], scalar1=gsc[:, 3, gch:gch + 1], scalar2=None, op0=ALU.mult),
                  [("bank", 1, "k"), "g3"], [("kp", ch)])
                V(lambda e, ch=ch: e.tensor_copy(out=vext[:, ch, 0:256], in_=b0[:, 0:256]), [("bank", 0)], [("vext", ch)])
                A(lambda e, ch=ch: e.activation(out=og[:, ch, :], in_=b0[:, 256:512], func=AF.Sigmoid), [("bank", 0)], [("og", ch)])
                V(lambda e, ss=ss, gch=gch: e.scalar_tensor_tensor(out=Sm[:, ss, :], in0=bank[5][:, 0:128], scalar=gsc[:, 3, gch:gch + 1], in1=tri,
                                                                   op0=ALU.mult, op1=ALU.mult), [("bank", 5, "s"), "g3", "cf"], [("Sm", ss)])
                PE(lambda e, ss=ss, ch=ch: e.matmul(bank[6][:, 0:257], lhsT=Sm[:, ss, :], rhs=vext[:, ch, :], start=True, stop=False),
                   [("Sm", ss), ("vext", ch), "vones"], [("bank", 6)])
                for dd in range(2):
                    PE(lambda e, dd=dd, cc=cc: e.matmul(bank[6][:, 0:257], lhsT=qT[:, dd, cc:cc + 128], rhs=Cwb[:, dd, :], start=False, stop=(dd == 1)),
                       [("qT", dd), "Cwb"], [("bank", 6)])
                if bi == 0:
                    if j == 0:
                        V(lambda e: e.memset(init[:, :, :], 0.0), [], [("init", jj) for jj in range(4)])
                else:
                    V(lambda e, j=j: e.tensor_scalar(out=init[:, j, 2:3], in0=zr[:, j, TBS - 1:TBS], scalar1=cs[:, 16 + j:17 + j], scalar2=None, op0=ALU.mult),
                      [ZR, "cs_Ec"], [("initt", j)])
                    V(lambda e, j=j: e.scalar_tensor_tensor(out=init[:, j, 0:1], in0=zi[:, j, TBS - 1:TBS], scalar=cs[:, 24 + j:25 + j], in1=init[:, j, 2:3],
                                                            op0=ALU.mult, op1=ALU.add), [ZI, "cs_nEs", ("initt", j)], [("init", j)])
                    V(lambda e, j=j: e.tensor_scalar(out=init[:, j, 3:4], in0=zi[:, j, TBS - 1:TBS], scalar1=cs[:, 16 + j:17 + j], scalar2=None, op0=ALU.mult),
                      [ZI, "cs_Ec"], [("initu", j)])
                    V(lambda e, j=j: e.scalar_tensor_tensor(out=init[:, j, 1:2], in0=zr[:, j, TBS - 1:TBS], scalar=cs[:, 20 + j:21 + j], in1=init[:, j, 3:4],
                                                            op0=ALU.mult, op1=ALU.add), [ZR, "cs_Es", ("initu", j)], [("init", j)])
                V(lambda e, j=j: e.tensor_tensor_scan(out=zr[:, j, :], data0=rT[:, j, :], data1=zin[:, 0, :], initial=init[:, j, 0:1], op0=ALU.mult, op1=ALU.add),
                  [("rT", j), ("zin", 0), ("init", j), "zr_all"], [ZR])
                V(lambda e, j=j: e.tensor_tensor_scan(out=zi[:, j, :], data0=rT[:, j, :], data1=zin[:, 1, :], initial=init[:, j, 1:2], op0=ALU.mult, op1=ALU.add),
                  [("rT", j), ("zin", 1), ("init", j)], [ZI])
                G(lambda e, j=j: e.tensor_tensor(out=pt[:, 0, :], in0=zr[:, j, :], in1=cosT[:, j, :], op=ALU.mult), [ZR, ("cosT", j)], [("pt", 0)])
                G(lambda e, j=j: e.tensor_tensor(out=pt[:, 1, :], in0=zi[:, j, :], in1=sinT[:, j, :], op=ALU.mult), [ZI, ("sinT", j)], [("pt", 1)])
                G(lambda e, sj=sj: e.tensor_tensor(out=sre[:, sj, :], in0=pt[:, 0, :], in1=pt[:, 1, :], op=ALU.subtract), [("pt", 0), ("pt", 1)], [("sre", sj)])
                G(lambda e, j=j: e.tensor_tensor(out=pt[:, 2, :], in0=zi[:, j, :], in1=cosT[:, j, :], op=ALU.mult), [ZI, ("cosT", j)], [("pt", 2)])
                G(lambda e, j=j: e.tensor_tensor(out=pt[:, 3, :], in0=zr[:, j, :], in1=sinT[:, j, :], op=ALU.mult), [ZR, ("sinT", j)], [("pt", 3)])
                G(lambda e, sj=sj: e.tensor_tensor(out=sim_[:, sj, :], in0=pt[:, 2, :], in1=pt[:, 3, :], op=ALU.add), [("pt", 2), ("pt", 3)], [("sim", sj)])
                A(lambda e: e.activation(out=sm[:, 5:6], in_=bank[6][:, 256:257], func=AF.Abs), [("bank", 6)], ["sm5"])
                V(lambda e, gch=gch: e.tensor_scalar(out=sm[:, 0:1], in0=sm[:, 5:6], scalar1=gsc[:, 4, gch:gch + 1], scalar2=None, op0=ALU.max),
                  ["sm5", "g4"], ["sm0"])
                V(lambda e: e.reciprocal(out=sm[:, 1:2], in_=sm[:, 0:1]), ["sm0"], ["sm1"])
                V(lambda e, ch=ch: e.scalar_tensor_tensor(out=hc4[:, ch, :], in0=bank[6][:, 0:256], scalar=sm[:, 1:2], in1=og[:, ch, :], op0=ALU.mult, op1=ALU.mult),
                  [("bank", 6), "sm1", ("og", ch)], [("hc", ch)])
                PE(lambda e, ch=ch: e.matmul(bank[1][:, 0:256], lhsT=kp[:, ch, 0:128], rhs=vext[:, ch, 0:256], start=True, stop=True),
                   [("kp", ch), ("vext", ch)], [("bank", 1, "kv")])
                PE(lambda e, ch=ch: e.matmul(bank[5][:, 128:384], lhsT=kp[:, ch, 128:256], rhs=vext[:, ch, 0:256], start=True, stop=True),
                   [("kp", ch), ("vext", ch)], [("bank", 5, "kv")])
                PE(lambda e, ch=ch: e.matmul(bank[5][:, 384:385], lhsT=kp[:, ch, 0:128], rhs=vext[:, ch, 256:257], start=True, stop=True),
                   [("kp", ch), "vones"], [("bank", 5, "n0")])
                PE(lambda e, ch=ch: e.matmul(bank[5][:, 385:386], lhsT=kp[:, ch, 128:256], rhs=vext[:, ch, 256:257], start=True, stop=True),
                   [("kp", ch), "vones"], [("bank", 5, "n1")])
                wcol = gsc[:, 6, gch:gch + 1]
                V(lambda e, wcol=wcol: e.scalar_tensor_tensor(out=Cst[:, 0, :], in0=Cst[:, 0, :], scalar=wcol, in1=bank[1][:, 0:256], op0=ALU.mult, op1=ALU.add),
                  ["Cst", "g6", ("bank", 1, "kv")], ["Cst"])
                V(lambda e, wcol=wcol: e.scalar_tensor_tensor(out=Cst[:, 1, :], in0=Cst[:, 1, :], scalar=wcol, in1=bank[5][:, 128:384], op0=ALU.mult, op1=ALU.add),
                  ["Cst", "g6", ("bank", 5, "kv")], ["Cst"])
                V(lambda e, wcol=wcol: e.scalar_tensor_tensor(out=nst[:, :], in0=nst[:, :], scalar=wcol, in1=bank[5][:, 384:386], op0=ALU.mult, op1=ALU.add),
                  ["nst", "g6", ("bank", 5, "n0"), ("bank", 5, "n1")], ["nst"])
                if gch + 1 < NCH:
                    wn = gsc[:, 6, gch + 1:gch + 2]
                    A(lambda e, wn=wn: e.activation(out=Cwb[:, :, 0:256], in_=Cst[:, :, :], func=AF.Copy, scale=wn), ["Cst", "g6"], ["Cwb"])
                    A(lambda e, wn=wn: e.activation(out=Cwb[:, :, 256], in_=nst[:, :], func=AF.Copy, scale=wn), ["nst", "g6"], ["Cwb"])
                PE(lambda e, j=j, sj=sj: e.matmul(bank[4][:, :], lhsT=CTb[:, 0, j, :], rhs=sre[:, sj, :], start=(j == 0), stop=False), ["CTb0", ("sre", sj)], [("bank", 4)])
                PE(lambda e, j=j, sj=sj: e.matmul(bank[4][:, :], lhsT=CTb[:, 1, j, :], rhs=sim_[:, sj, :], start=False, stop=(j == 3)), ["CTb1", ("sim", sj)], [("bank", 4)])
                A(lambda e, ch=ch: e.activation(out=hsq[:, :], in_=hc4[:, ch, :], func=AF.Square), [("hc", ch)], ["hsq"])
                V(lambda e, ch=ch: e.tensor_reduce(out=ssq[:, ch:ch + 1], in_=hsq[:, :], axis=AX.X, op=ALU.add), ["hsq"], [("ssq", ch)])
            V(lambda e: e.tensor_tensor(out=yy[:, :], in0=bank[4][:, :], in1=du[:, :], op=ALU.add), [("bank", 4), "du"], ["yy"])
            G(lambda e: e.tensor_tensor(out=g1[:, :], in0=yy[:, :], in1=yy[:, :], op=ALU.mult), ["yy"], ["g1t"])
            G(lambda e: e.tensor_scalar(out=g1[:, :], in0=g1[:, :], scalar1=0.044715, scalar2=1.0, op0=ALU.mult, op1=ALU.add), ["g1t"], ["g1t"])
            G(lambda e: e.tensor_tensor(out=g1[:, :], in0=g1[:, :], in1=yy[:, :], op=ALU.mult), ["g1t", "yy"], ["g1t"])
            A(lambda e: e.activation(out=g2[:, :], in_=g1[:, :], func=AF.Sigmoid, scale=1.5957691216057308), ["g1t"], ["g2t"])
            G(lambda e, s=s: e.tensor_tensor(out=zout[:, s, :], in0=yy[:, :], in1=g2[:, :], op=ALU.mult), ["yy", "g2t"], [("zout", s)])
            finals.append(P.dma("sync", lambda e, s=s, c0=c0: e.dma_start(out=zT_o[:, c0:c0 + TBS], in_=zout[:, s, :]), ("zst", s), reads=[("zout", s)]))
            V(lambda e: e.tensor_scalar(out=ssq[:, 4:8], in0=ssq[:, 0:4], scalar1=1.0 / 256.0, scalar2=EPS, op0=ALU.mult, op1=ALU.add),
              [("ssq", c) for c in range(4)], ["rs"])
            A(lambda e: e.activation(out=ssq[:, 4:8], in_=ssq[:, 4:8], func=AF.Sqrt), ["rs"], ["rs"])
            V(lambda e: e.reciprocal(out=ssq[:, 4:8], in_=ssq[:, 4:8]), ["rs"], ["rs"])
            for ch in range(4):
                cc = ch * 128
                hs = ch % 2
                A(lambda e, ch=ch, hs=hs: e.activation(out=hn[:, hs, :], in_=hc4[:, ch, :], func=AF.Copy, scale=ssq[:, 4 + ch:5 + ch]), [("hc", ch), "rs"], [("hn", hs)])
                for eo in range(2):
                    PE(lambda e, eo=eo, hs=hs: e.transpose(out=bankT[:, eo * 128:(eo + 1) * 128], in_=hn[:, hs, eo * 128:(eo + 1) * 128], identity=cb[:, :]),
                       [("hn", hs), "cb"], [("bankT", eo)])
                    V(lambda e, eo=eo, s=s, cc=cc: e.scalar_tensor_tensor(out=ymlo[:, s, eo, cc:cc + 128], in0=bankT[:, eo * 128:(eo + 1) * 128], scalar=vm[:, 2 + eo:3 + eo],
                                                                          in1=skx[:, eo, cc:cc + 128], op0=ALU.mult, op1=ALU.add),
                      [("bankT", eo), "vm", ("skx", eo)], [("ymlo", s)])
            finals.append(P.dma("sync", lambda e, s=s, c0=c0: e.dma_start(out=yv[:, :, c0:c0 + TBS], in_=ymlo[:, s, :, :]), ("yst", s), reads=[("ymlo", s)]))
        P.emit(final_waits=finals)
    return nc


def m_inputs(I, l, c, hT_full, ifp_full):
    b, r = c // 4, c % 4
    f32 = np.float32
    w_in = I["w_in"][l]
    w_m = np.concatenate([w_in[:, 128 * r:128 * r + 128], w_in[:, 512 + 256 * r:512 + 256 * r + 256],
                          w_in[:, 1536 + 256 * r:1536 + 256 * r + 256], w_in[:, 2560 + 256 * r:2560 + 256 * r + 256]], axis=1)
    ifg = np.stack([ifp_full[b][:, r], ifp_full[b][:, 4 + r]], axis=-1).reshape(NCH, 128, 2).transpose(1, 0, 2)
    convw = I["ml_conv_w"][l][:, 256 * r:256 * r + 256].reshape(4, 2, 128).transpose(2, 1, 0)
    vm = np.zeros((128, 16), f32)
    col2 = lambda v: v[256 * r:256 * r + 256].reshape(2, 128).T
    vm[:, 0:2] = col2(I["ml_conv_b"][l])
    vm[:, 2:4] = col2(I["ml_norm"][l])
    vm[:, 4:6] = col2(I["ml_skip"][l])
    vm[:, 6] = I["s5_d"][l][128 * r:128 * r + 128]
    vm[:, 8] = I["b_if"][l][r]
    vm[:, 9] = I["b_if"][l][4 + r]
    gs = slice(8 * r, 8 * r + 8)
    lam_re = I["s5_lam_re"][l][gs].reshape(512)
    lam_im = I["s5_lam_im"][l][gs].reshape(512)
    ldt = np.repeat(I["s5_log_dt"][l][gs], 64)
    lamC = np.zeros((128, 12), f32)
    lamC[:, 0:4] = lam_re.reshape(4, 128).T
    lamC[:, 4:8] = lam_im.reshape(4, 128).T
    lamC[:, 8:12] = ldt.reshape(4, 128).T
    lamR = np.broadcast_to(np.stack([lam_re, lam_im, ldt])[None], (128, 3, 512))
    BT = np.zeros((128, 2, 512), f32)
    CT = np.zeros((128, 2, 4, 128), f32)
    for gl in range(8):
        g = 8 * r + gl
        BT[16 * gl:16 * gl + 16, 0, gl * 64:(gl + 1) * 64] = I["s5_b_re"][l][g].T
        BT[16 * gl:16 * gl + 16, 1, gl * 64:(gl + 1) * 64] = I["s5_b_im"][l][g].T
        j, half = gl // 2, gl % 2
        CT[half * 64:(half + 1) * 64, 0, j, 16 * gl:16 * gl + 16] = I["s5_c_re"][l][g].T
        CT[half * 64:(half + 1) * 64, 1, j, 16 * gl:16 * gl + 16] = I["s5_c_im"][l][g].T
    return {"hT": hT_full[b], "ifg": np.ascontiguousarray(ifg, f32), "w_m": np.ascontiguousarray(w_m),
            "wq": I["ml_wq"][l][r], "wk": I["ml_wk"][l][r], "convw": np.ascontiguousarray(convw, f32), "vm": vm,
            "lamC": lamC, "lamR": np.ascontiguousarray(lamR, f32), "BT": BT, "CT": CT}


def m_consts():
    f32 = np.float32
    cf = np.zeros((128, 4, 128), f32)
    cf[:, 0, :] = np.eye(128)
    cf[:, 1, :] = np.triu(np.ones((128, 128)))
    cf[:, 2, :] = 1.0
    cf[127, 3, :] = 1.0
    iota = np.broadcast_to(np.arange(512, dtype=f32)[None], (128, 512))
    return {"cf32": cf, "cb16": np.eye(128).astype(ml_dtypes.bfloat16), "iota": np.ascontiguousarray(iota)}


def run_M(I, l, hT_sh, ifp_sh):
    hT_full = [np.concatenate([hT_sh[4 * b + r] for r in range(4)], axis=1) for b in range(2)]
    ifp_full = [np.concatenate([ifp_sh[4 * b + r] for r in range(4)], axis=0) for b in range(2)]
    cst = m_consts()
    in_maps = []
    for c in range(8):
        m = m_inputs(I, l, c, hT_full, ifp_full)
        m.update(cst)
        in_maps.append(m)
    nc = build_M()
    return _run(nc, in_maps)
```

```python
import contextlib
import math
import os
import numpy as np
import ml_dtypes
import concourse.bass as bass
import concourse.mybir as mybir
from concourse.bass_utils import run_bass_kernel_spmd

F32 = mybir.dt.float32
BF16 = mybir.dt.bfloat16
AF = mybir.ActivationFunctionType
ALU = mybir.AluOpType
AX = mybir.AxisListType

D = 1024
DT = 8
SEQ = 8192
NTOK = 2048
DFF = 2816
FT = 22
DEPTH = 4
EPS = 1e-6
SBT = 1024
NSB = NTOK // SBT
TBS = 512
NTB = SBT // TBS
ENGS = ("tensor", "vector", "scalar", "gpsimd", "sync")
TWO_PI = 2.0 * math.pi


class Op:
    def __init__(self, eng, fn):
        self.eng = eng
        self.fn = fn
        self.dma_sem = None
        self.dma_val = 0
        self.has_dep = False
        self.deps = []
        self.cval = 0


class Prog:
    def __init__(self, nc):
        self.nc = nc
        self.ops = {e: [] for e in ENGS}
        self.last_writer = {}
        self.readers = {}
        self.dma_counts = {}
        self.dma_last = {}
        self.pending_barrier = {e: [] for e in ENGS}

    def add(self, eng, fn, reads=(), writes=(), extra_deps=()):
        rd, wr = [], []
        for t in reads:
            if isinstance(t, tuple) and t[0] in ("bank", "bankT"):
                wr.append(("bank", t[1]) if t[0] == "bank" else ("bankT", 0))
            else:
                rd.append(t)
        for t in writes:
            if isinstance(t, tuple) and t[0] in ("bank", "bankT"):
                wr.append(("bank", t[1]) if t[0] == "bank" else ("bankT", 0))
            else:
                wr.append(t)
        reads, writes = rd, wr
        op = Op(eng, fn)
        deps = []
        for t in reads:
            w = self.last_writer.get(t)
            if w is not None:
                deps.append(w)
        for t in writes:
            w = self.last_writer.get(t)
            if w is not None:
                deps.append(w)
            deps.extend(self.readers.get(t, ()))
        for t in reads:
            self.readers.setdefault(t, []).append(op)
        for t in writes:
            self.last_writer[t] = op
            self.readers[t] = []
        deps.extend(extra_deps)
        if self.pending_barrier[eng]:
            deps.extend(self.pending_barrier[eng])
            self.pending_barrier[eng] = []
        op.deps = [d for d in deps if d is not op and not (eng == "tensor" and d.eng == "tensor" and d.dma_sem is None)]
        for d in op.deps:
            d.has_dep = True
        self.ops[eng].append(op)
        return op

    def dma(self, eng, fn, semkey, reads=(), writes=(), extra_deps=()):
        op = self.add(eng, fn, reads, writes, extra_deps)
        op.dma_sem = semkey
        self.dma_counts[semkey] = self.dma_counts.get(semkey, 0) + 1
        op.dma_val = 16 * self.dma_counts[semkey]
        self.dma_last[semkey] = op
        return op

    def barrier(self):
        lasts = []
        for e in ENGS:
            for op in reversed(self.ops[e]):
                if op.dma_sem is None:
                    lasts.append(op)
                    break
        lasts.extend(self.dma_last.values())
        for e in ENGS:
            self.pending_barrier[e] = list(lasts)

    def emit(self, final_waits=()):
        nc = self.nc
        cnt = {e: 0 for e in ENGS}
        for e in ENGS:
            for op in self.ops[e]:
                if op.dma_sem is None and op.has_dep:
                    cnt[e] += 1
                    op.cval = cnt[e]
        with contextlib.ExitStack() as st:
            sems = {}
            for e in ENGS:
                if cnt[e] > 0:
                    sems[("eng", e)] = st.enter_context(nc.semaphore("s_" + e))
            for i, k in enumerate(sorted(self.dma_counts.keys(), key=str)):
                sems[("dma", k)] = st.enter_context(nc.semaphore("d%d" % i))
            block = st.enter_context(nc.Block())

            def body_for(e):
                def body(engine):
                    waited = {}
                    for op in self.ops[e]:
                        need = {}
                        for d in op.deps:
                            if d.dma_sem is not None:
                                key, val = ("dma", d.dma_sem), d.dma_val
                            else:
                                key, val = ("eng", d.eng), d.cval
                            if need.get(key, 0) < val:
                                need[key] = val
                        for key, val in need.items():
                            if waited.get(key, 0) >= val:
                                continue
                            engine.wait_ge(sems[key], val)
                            waited[key] = val
                        ins = op.fn(engine)
                        if op.dma_sem is not None:
                            ins.then_inc(sems[("dma", op.dma_sem)], 16)
                        elif op.has_dep:
                            ins.then_inc(sems[("eng", e)], 1)
                    if e == "sync":
                        for op in final_waits:
                            engine.wait_ge(sems[("dma", op.dma_sem)], op.dma_val)
                return body

            for e in ENGS:
                if self.ops[e] or (e == "sync" and final_waits):
                    getattr(block, e)(body_for(e))


class Ctx:
    pass


def emit_norm(K, sb, gcol, tag):
    P = K.P
    for tb in range(NTB):
        c0 = sb * SBT + tb * TBS
        lc = tb * TBS
        psn = K.bank[6]
        for dt in range(DT):
            sq = K.sq[:, dt % 2, :]
            P.add("scalar", lambda e, sq=sq, dt=dt, c0=c0: e.activation(out=sq, in_=K.x[:, dt, c0:c0 + TBS], func=AF.Square),
                  reads=[("x", dt, sb, tb)], writes=[("sq", dt % 2)])
            P.add("tensor", lambda e, sq=sq, dt=dt, psn=psn: e.matmul(psn[:, :], lhsT=K.ones32[:, :], rhs=sq, start=(dt == 0), stop=(dt == DT - 1)),
                  reads=[("sq", dt % 2), "consts"], writes=[("bank", 6)])
        P.add("vector", lambda e, psn=psn: e.tensor_scalar(out=K.rstd[:, :], in0=psn[:, :], scalar1=1.0 / D, scalar2=EPS, op0=ALU.mult, op1=ALU.add),
              reads=[("bank", 6)], writes=["rstd"])
        P.add("scalar", lambda e: e.activation(out=K.rstd[:, :], in_=K.rstd[:, :], func=AF.Sqrt), reads=["rstd"], writes=["rstd"])
        P.add("vector", lambda e: e.reciprocal(out=K.rstd[:, :], in_=K.rstd[:, :]), reads=["rstd"], writes=["rstd"])
        for dt in range(DT):
            eng = "vector"
            P.add(eng, lambda e, dt=dt, c0=c0, lc=lc: e.scalar_tensor_tensor(out=K.h[:, dt, lc:lc + TBS], in0=K.x[:, dt, c0:c0 + TBS], scalar=gcol(dt),
                                                                           in1=K.rstd[:, :], op0=ALU.mult, op1=ALU.mult),
                  reads=[("x", dt, sb, tb), "rstd", "vecs"], writes=[("h", dt, tb)])


def emit_ffn(K, sb, wg, wu, wd, gcol, tag):
    P = K.P
    emit_norm(K, sb, gcol, tag)
    wg_v = wg.rearrange("(dt p) f -> p dt f", p=128)
    wu_v = wu.rearrange("(dt p) f -> p dt f", p=128)
    wd_v = wd.rearrange("(ft p) d -> p ft d", p=128)
    for fg in range(FT // 2):
        s = K.cnt_w % 2
        K.cnt_w += 1
        P.dma("gpsimd", lambda e, s=s, fg=fg: e.dma_start(out=K.wgu[:, s, 0, :, :], in_=wg_v[:, :, fg * 256:(fg + 1) * 256]),
              ("wgu", s), writes=[("wg", s)])
        P.dma("gpsimd", lambda e, s=s, fg=fg: e.dma_start(out=K.wgu[:, s, 1, :, :], in_=wu_v[:, :, fg * 256:(fg + 1) * 256]),
              ("wgu", s), writes=[("wu", s)])
        for fi in range(2):
            ft = fg * 2 + fi
            for tb in range(NTB):
                lc = tb * TBS
                pb = K.cnt_p % 2
                K.cnt_p += 1
                pg, pu = K.bank[pb], K.bank[2 + pb]
                for dt in range(DT):
                    P.add("tensor", lambda e, dt=dt, s=s, lc=lc, pg=pg, fi=fi: e.matmul(pg[:, :], lhsT=K.wgu[:, s, 0, dt, fi * 128:(fi + 1) * 128], rhs=K.h[:, dt, lc:lc + TBS],
                                                                                      start=(dt == 0), stop=(dt == DT - 1)),
                          reads=[("wg", s), ("h", dt, tb)], writes=[("bank", pb)])
                for dt in range(DT):
                    P.add("tensor", lambda e, dt=dt, s=s, lc=lc, pu=pu, fi=fi: e.matmul(pu[:, :], lhsT=K.wgu[:, s, 1, dt, fi * 128:(fi + 1) * 128], rhs=K.h[:, dt, lc:lc + TBS],
                                                                                      start=(dt == 0), stop=(dt == DT - 1)),
                          reads=[("wu", s), ("h", dt, tb)], writes=[("bank", 2 + pb)])
                sg = K.tmp[:, pb, :]
                P.add("scalar", lambda e, sg=sg, pg=pg: e.activation(out=sg, in_=pg[:, :], func=AF.Silu),
                      reads=[("bank", pb)], writes=[("tmp", pb)])
                P.add("vector", lambda e, sg=sg, pu=pu, ft=ft, lc=lc: e.tensor_tensor(out=K.act[:, ft, lc:lc + TBS], in0=pu[:, :], in1=sg, op=ALU.mult),
                      reads=[("bank", 2 + pb), ("tmp", pb)], writes=[("act", ft, tb)])
    for dg in range(DT // 2):
        s = K.cnt_d % 2
        K.cnt_d += 1
        P.dma("gpsimd", lambda e, s=s, dg=dg: e.dma_start(out=K.wd[:, s, :, :], in_=wd_v[:, :, dg * 256:(dg + 1) * 256]),
              ("wd", s), writes=[("wd", s)])
        for di in range(2):
            dt = dg * 2 + di
            for tb in range(NTB):
                lc = tb * TBS
                c0 = sb * SBT + lc
                pb = K.cnt_q % 2
                K.cnt_q += 1
                pd = K.bank[4 + pb]
                for ft in range(FT):
                    P.add("tensor", lambda e, ft=ft, s=s, lc=lc, pd=pd, di=di: e.matmul(pd[:, :], lhsT=K.wd[:, s, ft, di * 128:(di + 1) * 128], rhs=K.act[:, ft, lc:lc + TBS],
                                                                                      start=(ft == 0), stop=(ft == FT - 1)),
                          reads=[("wd", s), ("act", ft, tb)], writes=[("bank", 4 + pb)])
                P.add("vector", lambda e, dt=dt, c0=c0, pd=pd: e.scalar_tensor_tensor(out=K.x[:, dt, c0:c0 + TBS], in0=pd[:, :], scalar=0.5,
                                                                                     in1=K.x[:, dt, c0:c0 + TBS], op0=ALU.mult, op1=ALU.add),
                      reads=[("bank", 4 + pb), ("x", dt, sb, tb)], writes=[("x", dt, sb, tb)])


def emit_ta_tail(K, sb, gcol, w_if, hT_out, ifp_out):
    P = K.P
    emit_norm(K, sb, gcol, "mix")
    hv = hT_out.rearrange("(dt p) t -> p dt t", p=128)
    st_ops = []
    st_ops.append(P.dma("sync", lambda e: e.dma_start(out=hv[:, :, sb * SBT:(sb + 1) * SBT], in_=K.h[:, :, :]), ("hst", 0),
                        reads=[("h", dt, tb) for dt in range(DT) for tb in range(NTB)]))
    P.dma("gpsimd", lambda e: e.dma_start(out=K.wif[:, :, :], in_=w_if.rearrange("(dt p) c -> p dt c", p=128)), ("wif", 0), writes=["wif"])
    pif = K.bank[7]
    ntt = SBT // 128
    for tt in range(ntt):
        for dt in range(DT):
            P.add("tensor", lambda e, tt=tt, dt=dt: e.matmul(pif[:, tt * 8:(tt + 1) * 8], lhsT=K.h[:, dt, tt * 128:(tt + 1) * 128], rhs=K.wif[:, dt, :],
                                                             start=(dt == 0), stop=(dt == DT - 1)),
                  reads=[("h", dt, tt // 4), "wif"], writes=[("bank", 7)])
    P.add("vector", lambda e: e.tensor_copy(out=K.ifs[:, :], in_=pif[:, 0:ntt * 8]), reads=[("bank", 7)], writes=["ifs"])
    iv = ifp_out.rearrange("(tt p) c -> p tt c", p=128)
    st_ops.append(P.dma("sync", lambda e: e.dma_start(out=iv[:, sb * ntt:(sb + 1) * ntt, :], in_=K.ifs[:, :].rearrange("p (tt c) -> p tt c", c=8)),
                        ("ifst", 0), reads=["ifs"]))
    return st_ops


def emit_tb(K, sb, W):
    P = K.P
    c0s = sb * SBT
    P.dma("sync", lambda e: e.dma_start(out=K.h[:, :, :], in_=W.hT.rearrange("(dt p) t -> p dt t", p=128)[:, :, c0s:c0s + SBT]), ("hld", 0),
          writes=[("h", dt, tb) for dt in range(DT) for tb in range(NTB)])
    P.dma("sync", lambda e: e.dma_start(out=K.z[:, :, :], in_=W.zT.rearrange("(ct p) t -> p ct t", p=128)[:, :, c0s:c0s + SBT]), ("zld", 0),
          writes=["z"])
    P.dma("sync", lambda e: e.dma_start(out=K.yml[:, :, :], in_=W.ymlT.rearrange("(ct p) t -> p ct t", p=128)[:, :, c0s:c0s + SBT]), ("yld", 0),
          writes=["yml"])
    P.dma("gpsimd", lambda e: e.dma_start(out=K.wglu[:, 0, :, :], in_=W.glu_v.rearrange("(ct p) c -> p ct c", p=128)), ("wglu", 0), writes=["wglu0"])
    P.dma("gpsimd", lambda e: e.dma_start(out=K.wglu[:, 1, :, :], in_=W.glu_g.rearrange("(ct p) c -> p ct c", p=128)), ("wglu", 0), writes=["wglu1"])
    for co in range(4):
        for tb in range(NTB):
            lc = tb * TBS
            pb = K.cnt_p % 2
            K.cnt_p += 1
            pv, pg = K.bank[pb], K.bank[2 + pb]
            for ci in range(4):
                P.add("tensor", lambda e, ci=ci, co=co, lc=lc, pv=pv: e.matmul(pv[:, :], lhsT=K.wglu[:, 0, ci, co * 128:(co + 1) * 128], rhs=K.z[:, ci, lc:lc + TBS],
                                                                             start=(ci == 0), stop=(ci == 3)),
                      reads=["wglu0", "z"], writes=[("bank", pb)])
            for ci in range(4):
                P.add("tensor", lambda e, ci=ci, co=co, lc=lc, pg=pg: e.matmul(pg[:, :], lhsT=K.wglu[:, 1, ci, co * 128:(co + 1) * 128], rhs=K.z[:, ci, lc:lc + TBS],
                                                                             start=(ci == 0), stop=(ci == 3)),
                      reads=["wglu1", "z"], writes=[("bank", 2 + pb)])
            sg = K.tmp[:, pb, :]
            P.add("scalar", lambda e, sg=sg, pg=pg: e.activation(out=sg, in_=pg[:, :], func=AF.Sigmoid), reads=[("bank", 2 + pb)], writes=[("tmp", pb)])
            P.add("vector", lambda e, sg=sg, pv=pv, co=co, lc=lc: e.tensor_tensor(out=K.ys5[:, co, lc:lc + TBS], in0=pv[:, :], in1=sg, op=ALU.mult),
                  reads=[("bank", pb), ("tmp", pb)], writes=[("ys5", co, tb)])
    wbs_v = W.w_br_s5.rearrange("(ct p) d -> p ct d", p=128)
    wbm_v = W.w_br_ml.rearrange("(ct p) d -> p ct d", p=128)
    wg_v = W.w_g.rearrange("(ct p) d -> p ct d", p=128)
    for dt in range(DT):
        s = K.cnt_b % 2
        K.cnt_b += 1
        P.dma("gpsimd", lambda e, s=s, dt=dt: e.dma_start(out=K.wbr[:, s, 0:4, :], in_=wbs_v[:, :, dt * 128:(dt + 1) * 128]), ("wbr", s), writes=[("wbr0", s)])
        P.dma("gpsimd", lambda e, s=s, dt=dt: e.dma_start(out=K.wbr[:, s, 4:12, :], in_=wbm_v[:, :, dt * 128:(dt + 1) * 128]), ("wbr", s), writes=[("wbr1", s)])
        P.dma("gpsimd", lambda e, s=s, dt=dt: e.dma_start(out=K.wbr[:, s, 12:20, :], in_=wg_v[:, :, dt * 128:(dt + 1) * 128]), ("wbr", s), writes=[("wbr2", s)])
        P.dma("gpsimd", lambda e, s=s, dt=dt: e.dma_start(out=K.wbr[:, s, 20:28, :], in_=wg_v[:, :, D + dt * 128:D + (dt + 1) * 128]), ("wbr", s), writes=[("wbr3", s)])
        for tb in range(NTB):
            lc = tb * TBS
            b0, b1, b2, b3 = K.bank[0], K.bank[1], K.bank[2], K.bank[3]
            for ci in range(4):
                P.add("tensor", lambda e, ci=ci, s=s, lc=lc: e.matmul(b0[:, :], lhsT=K.wbr[:, s, ci, :], rhs=K.ys5[:, ci, lc:lc + TBS], start=(ci == 0), stop=(ci == 3)),
                      reads=[("wbr0", s), ("ys5", ci, tb)], writes=[("bank", 0)])
            for ci in range(8):
                P.add("tensor", lambda e, ci=ci, s=s, lc=lc: e.matmul(b1[:, :], lhsT=K.wbr[:, s, 4 + ci, :], rhs=K.yml[:, ci, lc:lc + TBS], start=(ci == 0), stop=(ci == 7)),
                      reads=[("wbr1", s), "yml"], writes=[("bank", 1)])
            for ci in range(8):
                P.add("tensor", lambda e, ci=ci, s=s, lc=lc: e.matmul(b2[:, :], lhsT=K.wbr[:, s, 12 + ci, :], rhs=K.h[:, ci, lc:lc + TBS], start=(ci == 0), stop=(ci == 7)),
                      reads=[("wbr2", s), ("h", ci, tb)], writes=[("bank", 2)])
            for ci in range(8):
                P.add("tensor", lambda e, ci=ci, s=s, lc=lc: e.matmul(b3[:, :], lhsT=K.wbr[:, s, 20 + ci, :], rhs=K.h[:, ci, lc:lc + TBS], start=(ci == 0), stop=(ci == 7)),
                      reads=[("wbr3", s), ("h", ci, tb)], writes=[("bank", 3)])
            P.add("scalar", lambda e: e.activation(out=K.tmp[:, 0, :], in_=b2[:, :], func=AF.Sigmoid), reads=[("bank", 2)], writes=[("tmp", 0)])
            P.add("scalar", lambda e: e.activation(out=K.tmp[:, 1, :], in_=b3[:, :], func=AF.Sigmoid), reads=[("bank", 3)], writes=[("tmp", 1)])
            P.add("vector", lambda e: e.tensor_tensor(out=K.tmp[:, 0, :], in0=b0[:, :], in1=K.tmp[:, 0, :], op=ALU.mult), reads=[("bank", 0), ("tmp", 0)], writes=[("tmp", 0)])
            P.add("vector", lambda e: e.tensor_tensor(out=K.tmp[:, 1, :], in0=b1[:, :], in1=K.tmp[:, 1, :], op=ALU.mult), reads=[("bank", 1), ("tmp", 1)], writes=[("tmp", 1)])
            P.add("vector", lambda e, dt=dt, lc=lc: e.tensor_tensor(out=K.mix[:, dt, lc:lc + TBS], in0=K.tmp[:, 0, :], in1=K.tmp[:, 1, :], op=ALU.add),
                  reads=[("tmp", 0), ("tmp", 1)], writes=[("mix", dt, tb)])
    wo_v = W.w_out.rearrange("(ct p) d -> p ct d", p=128)
    for dt in range(DT):
        s = K.cnt_b % 2
        K.cnt_b += 1
        P.dma("gpsimd", lambda e, s=s, dt=dt: e.dma_start(out=K.wbr[:, s, 0:8, :], in_=wo_v[:, :, dt * 128:(dt + 1) * 128]), ("wbr", s),
              writes=[("wbr0", s), ("wbr1", s)])
        for tb in range(NTB):
            lc = tb * TBS
            c0 = c0s + lc
            pb = K.cnt_q % 2
            K.cnt_q += 1
            pd = K.bank[4 + pb]
            for ci in range(8):
                P.add("tensor", lambda e, ci=ci, s=s, lc=lc, pd=pd: e.matmul(pd[:, :], lhsT=K.wbr[:, s, ci, :], rhs=K.mix[:, ci, lc:lc + TBS], start=(ci == 0), stop=(ci == 7)),
                      reads=[("wbr0", s), ("wbr1", s), ("mix", ci, tb)], writes=[("bank", 4 + pb)])
            P.add("vector", lambda e, dt=dt, c0=c0, pd=pd: e.tensor_tensor(out=K.x[:, dt, c0:c0 + TBS], in0=pd[:, :], in1=K.x[:, dt, c0:c0 + TBS], op=ALU.add),
                  reads=[("bank", 4 + pb), ("x", dt, sb, tb)], writes=[("x", dt, sb, tb)])


def emit_final(K, sb, gcol, outT):
    P = K.P
    emit_norm_f32(K, sb, gcol)
    ov = outT.rearrange("(dt p) t -> p dt t", p=128)
    return [P.dma("sync", lambda e: e.dma_start(out=ov[:, :, sb * SBT:(sb + 1) * SBT], in_=K.x[:, :, sb * SBT:(sb + 1) * SBT]), ("ost", 0),
                  reads=[("x", dt, sb, tb) for dt in range(DT) for tb in range(NTB)])]


def emit_norm_f32(K, sb, gcol):
    P = K.P
    for tb in range(NTB):
        c0 = sb * SBT + tb * TBS
        psn = K.bank[6]
        for dt in range(DT):
            sq = K.sq[:, dt % 2, :]
            P.add("scalar", lambda e, sq=sq, dt=dt, c0=c0: e.activation(out=sq, in_=K.x[:, dt, c0:c0 + TBS], func=AF.Square),
                  reads=[("x", dt, sb, tb)], writes=[("sq", dt % 2)])
            P.add("tensor", lambda e, sq=sq, dt=dt: e.matmul(psn[:, :], lhsT=K.ones32[:, :], rhs=sq, start=(dt == 0), stop=(dt == DT - 1)),
                  reads=[("sq", dt % 2), "consts"], writes=[("bank", 6)])
        P.add("vector", lambda e: e.tensor_scalar(out=K.rstd[:, :], in0=psn[:, :], scalar1=1.0 / D, scalar2=EPS, op0=ALU.mult, op1=ALU.add),
              reads=[("bank", 6)], writes=["rstd"])
        P.add("scalar", lambda e: e.activation(out=K.rstd[:, :], in_=K.rstd[:, :], func=AF.Sqrt), reads=["rstd"], writes=["rstd"])
        P.add("vector", lambda e: e.reciprocal(out=K.rstd[:, :], in_=K.rstd[:, :]), reads=["rstd"], writes=["rstd"])
        for dt in range(DT):
            eng = "vector"
            P.add(eng, lambda e, dt=dt, c0=c0: e.scalar_tensor_tensor(out=K.x[:, dt, c0:c0 + TBS], in0=K.x[:, dt, c0:c0 + TBS], scalar=gcol(dt),
                                                                     in1=K.rstd[:, :], op0=ALU.mult, op1=ALU.mult),
                  reads=[("x", dt, sb, tb), "rstd", "vecs"], writes=[("x", dt, sb, tb)])


def build_T(do_tb, do_ta, do_final):
    nc = bass.Bass("TRN2", target_bir_lowering=False)
    dr = lambda name, shape, dt, kind="ExternalInput": nc.dram_tensor(name, shape, dt, kind=kind).ap()
    W = Ctx()
    xin = dr("xin", [D, NTOK], F32)
    vecs_d = dr("vecs", [128, 40], F32)
    ones_d = dr("ones32", [128, 128], F32)
    if do_tb:
        W.hT = dr("hT_in", [D, NTOK], BF16)
        W.zT = dr("zT_in", [512, NTOK], BF16)
        W.ymlT = dr("ymlT_in", [D, NTOK], BF16)
        W.glu_v = dr("glu_v", [512, 512], F32)
        W.glu_g = dr("glu_g", [512, 512], F32)
        W.w_br_s5 = dr("w_br_s5", [512, D], F32)
        W.w_br_ml = dr("w_br_ml", [D, D], F32)
        W.w_g = dr("w_g", [D, 2 * D], F32)
        W.w_out = dr("w_out", [D, D], F32)
        f2 = (dr("f2_wg", [D, DFF], F32), dr("f2_wu", [D, DFF], F32), dr("f2_wd", [DFF, D], F32))
    if do_ta:
        f1 = (dr("f1_wg", [D, DFF], F32), dr("f1_wu", [D, DFF], F32), dr("f1_wd", [DFF, D], F32))
        w_if = dr("w_if", [D, 8], F32)
        hT_out = dr("hT_out", [D, NTOK], BF16, "ExternalOutput")
        ifp_out = dr("ifp_out", [NTOK, 8], F32, "ExternalOutput")
    xout = dr("xout", [D, NTOK], F32, "ExternalOutput")

    with contextlib.ExitStack() as st:
        K = Ctx()
        K.nc = nc
        K.P = P = Prog(nc)
        sb_t = lambda name, shape, dt: st.enter_context(nc.sbuf_tensor(name, shape, dt))
        K.x = sb_t("x", [128, DT, NTOK], F32)
        K.h = sb_t("h", [128, DT, SBT], BF16)
        K.act = sb_t("act", [128, 24, SBT], BF16)
        K.z = K.act[:, 0:4, :]
        K.ys5 = K.act[:, 4:8, :]
        K.yml = K.act[:, 8:16, :]
        K.mix = K.act[:, 16:24, :]
        K.sq = sb_t("sq", [128, 2, TBS], F32)
        K.tmp = sb_t("tmp", [128, 2, TBS], F32)
        K.rstd = sb_t("rstd", [128, TBS], F32)
        K.wgu = sb_t("wgu", [128, 2, 2, DT, 256], BF16)
        K.wd = sb_t("wd", [128, 2, FT, 256], BF16)
        K.wbr = sb_t("wbr", [128, 2, 28, 128], BF16)
        K.wglu = sb_t("wglu", [128, 2, 4, 512], BF16)
        K.wif = sb_t("wif", [128, DT, 8], BF16)
        K.ifs = sb_t("ifs", [128, (SBT // 128) * 8], F32)
        K.vecs = sb_t("vecs_s", [128, 40], F32)
        K.ones32 = sb_t("ones_s", [128, 128], F32)
        K.bank = [st.enter_context(nc.psum_tensor("bank%d" % i, [128, 512], F32)) for i in range(8)]
        K.cnt_w = K.cnt_p = K.cnt_d = K.cnt_q = K.cnt_b = 0

        P.dma("sync", lambda e: e.dma_start(out=K.vecs[:, :], in_=vecs_d), ("c", 0), writes=["vecs"])
        P.dma("sync", lambda e: e.dma_start(out=K.ones32[:, :], in_=ones_d), ("c", 1), writes=["consts"])
        xv = xin.rearrange("(dt p) t -> p dt t", p=128)
        for sb in range(NSB):
            P.dma("sync", lambda e, sb=sb: e.dma_start(out=K.x[:, :, sb * SBT:(sb + 1) * SBT], in_=xv[:, :, sb * SBT:(sb + 1) * SBT]), ("xld", sb),
                  writes=[("x", dt, sb, tb) for dt in range(DT) for tb in range(NTB)])
        finals = []
        for sb in range(NSB):
            if do_tb:
                P.barrier()
                emit_tb(K, sb, W)
                P.barrier()
                emit_ffn(K, sb, f2[0], f2[1], f2[2], lambda dt: K.vecs[:, dt:dt + 1], "f2")
            if do_ta:
                emit_ffn(K, sb, f1[0], f1[1], f1[2], lambda dt: K.vecs[:, 8 + dt:9 + dt], "f1")
                finals += emit_ta_tail(K, sb, lambda dt: K.vecs[:, 16 + dt:17 + dt], w_if, hT_out, ifp_out)
                xo = xout.rearrange("(dt p) t -> p dt t", p=128)
                finals.append(P.dma("sync", lambda e, sb=sb, xo=xo: e.dma_start(out=xo[:, :, sb * SBT:(sb + 1) * SBT], in_=K.x[:, :, sb * SBT:(sb + 1) * SBT]),
                                    ("ost", 0), reads=[("x", dt, sb, tb) for dt in range(DT) for tb in range(NTB)]))
            if do_final:
                finals += emit_final(K, sb, lambda dt: K.vecs[:, 24 + dt:25 + dt], xout)
        P.emit(final_waits=finals)
    return nc


def _cols(v):
    return np.ascontiguousarray(np.asarray(v, np.float32).reshape(8, 128).T)


_NC_CACHE = {}


def _get_T(do_tb, do_ta, do_final):
    return build_T(do_tb, do_ta, do_final)


def _run(nc, in_maps):
    res = run_bass_kernel_spmd(nc, in_maps, core_ids=list(range(8)))
    return res.results


def run_T(I, l, x_sh, mixer_out, do_tb, do_ta, do_final):
    ones = np.ones((128, 128), np.float32)
    in_maps = []
    lta = l + 1 if do_tb else 0
    for c in range(8):
        vecs = np.zeros((128, 40), np.float32)
        m = {"xin": x_sh[c], "ones32": ones}
        if do_tb:
            vecs[:, 0:8] = _cols(I["ffn2_norm"][l])
            hT, zT, ymlT = mixer_out
            m.update({"hT_in": hT[c], "zT_in": zT[c], "ymlT_in": ymlT[c],
                      "glu_v": I["s5_glu_v"][l], "glu_g": I["s5_glu_g"][l], "w_br_s5": I["w_br_s5"][l], "w_br_ml": I["w_br_ml"][l],
                      "w_g": np.ascontiguousarray(I["w_in"][l][:, 3592:]), "w_out": I["w_out"][l],
                      "f2_wg": I["ffn2_wg"][l], "f2_wu": I["ffn2_wu"][l], "f2_wd": I["ffn2_wd"][l]})
        if do_ta:
            vecs[:, 8:16] = _cols(I["ffn1_norm"][lta])
            vecs[:, 16:24] = _cols(I["mix_norm"][lta])
            m.update({"f1_wg": I["ffn1_wg"][lta], "f1_wu": I["ffn1_wu"][lta], "f1_wd": I["ffn1_wd"][lta],
                      "w_if": np.ascontiguousarray(I["w_in"][lta][:, 3584:3592])})
        if do_final:
            vecs[:, 24:32] = _cols(I["final_norm"])
        m["vecs"] = vecs
        in_maps.append(m)
    nc = _get_T(do_tb, do_ta, do_final)
    return _run(nc, in_maps)


def kernel(**inputs):
    I = {k: np.asarray(v) for k, v in inputs.items()}
    x = I["x"]
    x_sh = [np.ascontiguousarray(x[c // 4, (c % 4) * NTOK:(c % 4 + 1) * NTOK, :].T) for c in range(8)]
    res = run_T(I, 0, x_sh, None, False, True, False)
    for l in range(DEPTH):
        x_sh = [res[c]["xout"] for c in range(8)]
        hT_sh = [res[c]["hT_out"] for c in range(8)]
        ifp_sh = [res[c]["ifp_out"] for c in range(8)]
        mres = run_M(I, l, hT_sh, ifp_sh)
        if os.environ.get('KSTOP') == 'M0':
            return np.zeros((2, SEQ, D), np.float32)
        zT_in, ymlT_in = [], []
        for c in range(8):
            b, r = c // 4, c % 4
            sl = slice(r * NTOK, (r + 1) * NTOK)
            zT_in.append(np.ascontiguousarray(np.concatenate([mres[4 * b + q]["zT"][:, sl] for q in range(4)], axis=0)))
            ymlT_in.append(np.ascontiguousarray(np.concatenate([mres[4 * b + q]["ymlT"][:, sl] for q in range(4)], axis=0)))
        last = (l == DEPTH - 1)
        res = run_T(I, l, x_sh, (hT_sh, zT_in, ymlT_in), True, not last, last)
    out = np.zeros((2, SEQ, D), np.float32)
    for c in range(8):
        out[c // 4, (c % 4) * NTOK:(c % 4 + 1) * NTOK, :] = res[c]["xout"].T
    return out


NBLK = SEQ // TBS
NCH = SEQ // 128
import os
MLEVEL = int(os.environ.get('MLEVEL', '9'))
MSUB = int(os.environ.get('MSUB', '9'))


def build_M():
    nc = bass.Bass("TRN2", target_bir_lowering=False)
    dr = lambda name, shape, dt, kind="ExternalInput": nc.dram_tensor(name, shape, dt, kind=kind).ap()
    hT = dr("hT", [D, SEQ], BF16)
    ifg_d = dr("ifg", [128, NCH, 2], F32)
    w_m = dr("w_m", [D, 896], F32)
    wq_d = dr("wq", [256, 256], F32)
    wk_d = dr("wk", [256, 256], F32)
    convw_d = dr("convw", [128, 2, 4], F32)
    vm_d = dr("vm", [128, 16], F32)
    lamC_d = dr("lamC", [128, 12], F32)
    lamR_d = dr("lamR", [128, 3, 512], F32)
    BT_d = dr("BT", [128, 2, 512], F32)
    CT_d = dr("CT", [128, 2, 4, 128], F32)
    cf_d = dr("cf32", [128, 4, 128], F32)
    cb_d = dr("cb16", [128, 128], BF16)
    iota_d = dr("iota", [128, 512], F32)
    zT_o = dr("zT", [128, SEQ], BF16, "ExternalOutput")
    ymlT_o = dr("ymlT", [256, SEQ], BF16, "ExternalOutput")

    with contextlib.ExitStack() as st:
        P = Prog(nc)
        T = lambda name, shape, dt: st.enter_context(nc.sbuf_tensor(name, shape, dt))
        wm = T("wm", [128, DT, 896], BF16)
        wq = T("wq_s", [128, 2, 256], BF16)
        wk = T("wk_s", [128, 2, 256], BF16)
        hblk = T("hblk", [128, 2, DT, TBS], BF16)
        convw = T("convw_s", [128, 2, 4], F32)
        vm = T("vm_s", [128, 16], F32)
        lamC = T("lamC_s", [128, 12], F32)
        cf = T("cf_s", [128, 4, 128], F32)
        cb = T("cb_s", [128, 128], BF16)
        iota = T("iota_s", [128, 512], F32)
        R = T("R", [128, 12, 512], F32)
        CTf = T("CTf", [128, 2, 4, 128], F32)
        CTb = T("CTb", [128, 2, 4, 128], BF16)
        BbT = T("BbT", [128, 2, 512], BF16)
        cosT = T("cosT", [128, 4, 512], F32)
        sinT = T("sinT", [128, 4, 512], F32)
        rT = T("rT", [128, 4, 512], F32)
        cs = T("cs", [128, 32], F32)
        ubf = T("ubf", [128, TBS], BF16)
        du = T("du", [128, TBS], F32)
        t1 = T("t1", [128, 2, TBS], F32)
        t2 = T("t2", [128, 2, TBS], F32)
        zin = T("zin", [128, 2, TBS], F32)
        zr = T("zr", [128, 4, TBS], F32)
        zi = T("zi", [128, 4, TBS], F32)
        init = T("init", [128, 4, 4], F32)
        pt = T("pt", [128, 4, TBS], F32)
        sre = T("sre", [128, 2, TBS], BF16)
        sim_ = T("sim", [128, 2, TBS], BF16)
        yy = T("yy", [128, TBS], F32)
        g1 = T("g1", [128, TBS], F32)
        g2 = T("g2", [128, TBS], F32)
        zout = T("zout", [128, 2, TBS], BF16)
        xpad = T("xpad", [128, 2, TBS + 3], F32)
        cacc = T("cacc", [128, 2, TBS], F32)
        xc32 = T("xc32", [128, 2, TBS], F32)
        xcb = T("xcb", [128, 2, TBS], BF16)
        skx = T("skx", [128, 2, TBS], F32)
        qT = T("qT", [128, 2, TBS], BF16)
        kT = T("kT", [128, 2, TBS], BF16)
        kp = T("kp", [128, 4, 256], BF16)
        vext = T("vext", [128, 4, 257], BF16)
        og = T("og", [128, 4, 256], F32)
        Sm = T("Sm", [128, 2, 128], BF16)
        hc4 = T("hc4", [128, 4, 256], F32)
        ssq = T("ssq", [128, 8], F32)
        hsq = T("hsq", [128, 256], F32)
        hn = T("hn", [128, 2, 256], BF16)
        sm = T("sm", [128, 8], F32)
        Cst = T("Cst", [128, 2, 256], F32)
        nst = T("nst", [128, 2], F32)
        Cwb = T("Cwb", [128, 2, 257], BF16)
        ymlo = T("ymlo", [128, 2, 2, TBS], BF16)
        ifs = T("ifs", [128, NCH, 2], F32)
        gsc = T("gsc", [128, 8, NCH], F32)
        grow = T("grow", [128, 4, NCH + 1], F32)
        abc = T("abc", [64, 128], F32)
        acol = T("acol", [64, 1], F32)
        bank = [st.enter_context(nc.psum_tensor("bank%d" % i, [128, 512], F32)) for i in range(7)]
        bankT = st.enter_context(nc.psum_tensor("bankT", [128, 1024], BF16))

        ident = cf[:, 0, :]
        tri = cf[:, 1, :]
        ones = cf[:, 2, :]

        ld = lambda eng, out, in_, key, tok: P.dma(eng, lambda e: e.dma_start(out=out, in_=in_), key, writes=[tok])
        wmv = w_m.rearrange("(dt p) c -> p dt c", p=128)
        for k in range(7):
            P.dma("gpsimd", lambda e, k=k: e.dma_start(out=wm[:, :, k * 128:(k + 1) * 128], in_=wmv[:, :, k * 128:(k + 1) * 128]), ("w", 0), writes=["wm"])
        for k in range(2):
            P.dma("gpsimd", lambda e, k=k: e.dma_start(out=wq[:, :, k * 128:(k + 1) * 128], in_=wq_d.rearrange("(dt p) c -> p dt c", p=128)[:, :, k * 128:(k + 1) * 128]),
                  ("w", 1), writes=["wq"])
            P.dma("gpsimd", lambda e, k=k: e.dma_start(out=wk[:, :, k * 128:(k + 1) * 128], in_=wk_d.rearrange("(dt p) c -> p dt c", p=128)[:, :, k * 128:(k + 1) * 128]),
                  ("w", 2), writes=["wk"])
        for k in range(4):
            P.dma("gpsimd", lambda e, k=k: e.dma_start(out=CTb[:, 0, k, :], in_=CT_d[:, 0, k, :]), ("w", 3), writes=["CTb0"])
        ld("sync", CTf[:, :, :, :], CT_d, ("w", 4), "CTf")
        ld("sync", convw[:, :, :], convw_d, ("w", 5), "convw")
        ld("sync", vm[:, :], vm_d, ("w", 6), "vm")
        ld("sync", lamC[:, :], lamC_d, ("w", 7), "lamC")
        ld("sync", cf[:, :, :], cf_d, ("w", 8), "cf")
        ld("sync", cb[:, :], cb_d, ("w", 9), "cb")
        ld("sync", iota[:, :], iota_d, ("w", 10), "iota")
        ld("sync", R[:, 0:3, :], lamR_d, ("w", 11), "R012")
        ld("sync", R[:, 8:10, :], BT_d, ("w", 12), "R89")
        ld("sync", ifs[:, :, :], ifg_d, ("w", 13), "ifs")

        V = lambda fn, r=(), w=(): P.add("vector", fn, reads=r, writes=w)
        A = lambda fn, r=(), w=(): P.add("scalar", fn, reads=r, writes=w)
        G = lambda fn, r=(), w=(): P.add("vector" if os.environ.get("POOL2V") else "gpsimd", fn, reads=r, writes=w)
        PE = lambda fn, r=(), w=(): P.add("tensor", fn, reads=r, writes=w)

        RI = T("RI", [128, 512], mybir.dt.int32)

        def sincos(out_s, out_c, ang, tmpa, tmpb, rtoks, wtoks, ttoks):
            n = tmpa.shape[-1]
            it = RI[:, 0:n]
            ta, tb = ttoks
            V(lambda e: e.tensor_scalar(out=tmpa, in0=ang, scalar1=1.0 / TWO_PI, scalar2=None, op0=ALU.mult), rtoks, [ta])
            for k, (o_ap, wt) in enumerate(((out_s, wtoks[0]), (out_c, wtoks[1]))):
                if k == 1:
                    V(lambda e: e.tensor_scalar(out=tmpa, in0=tmpa, scalar1=0.25, scalar2=None, op0=ALU.add), [ta], [ta])
                V(lambda e: e.tensor_copy(out=it, in_=tmpa), [ta], ["RI"])
                V(lambda e: e.tensor_copy(out=tmpb, in_=it), ["RI"], [tb])
                V(lambda e: e.tensor_tensor(out=tmpb, in0=tmpa, in1=tmpb, op=ALU.subtract), [ta, tb], [tb])
                A(lambda e, o_ap=o_ap: e.activation(out=o_ap, in_=tmpb, func=AF.Sin, scale=TWO_PI), [tb], [wt])

        V(lambda e: e.memset(zr[:, :, :], 0.0), [], ["zr_all"])
        A(lambda e: e.activation(out=cs[:, 0:4], in_=lamC[:, 8:12], func=AF.Exp), ["lamC"], ["cs_dt"])
        V(lambda e: e.tensor_scalar(out=cs[:, 4:8], in0=lamC[:, 0:4], scalar1=-1e-4, scalar2=None, op0=ALU.min), ["lamC"], ["cs_lr"])
        V(lambda e: e.tensor_tensor(out=cs[:, 28:32], in0=cs[:, 4:8], in1=cs[:, 0:4], op=ALU.mult), ["cs_lr", "cs_dt"], ["cs_tmp"])
        A(lambda e: e.activation(out=cs[:, 8:12], in_=cs[:, 28:32], func=AF.Exp), ["cs_tmp"], ["cs_mag"])
        V(lambda e: e.tensor_tensor(out=cs[:, 12:16], in0=lamC[:, 4:8], in1=cs[:, 0:4], op=ALU.mult), ["lamC", "cs_dt"], ["cs_th"])
        for j in range(4):
            V(lambda e, j=j: e.tensor_scalar(out=R[:, 10, :], in0=iota[:, :], scalar1=cs[:, 12 + j:13 + j], scalar2=None, op0=ALU.mult), ["iota", "cs_th"], ["R10"])
            sincos(sinT[:, j, :], cosT[:, j, :], R[:, 10, :], R[:, 11, :], R[:, 3, :], ["R10"], [("sinT", j), ("cosT", j)], ["R11", "R3"])
            V(lambda e, j=j: e.memset(rT[:, j, :], 1.0), [], [("rT", j)])
            V(lambda e, j=j: e.tensor_scalar(out=rT[:, j, :], in0=rT[:, j, :], scalar1=cs[:, 8 + j:9 + j], scalar2=None, op0=ALU.mult),
              [("rT", j), "cs_mag"], [("rT", j)])
        V(lambda e: e.tensor_scalar(out=cs[:, 28:32], in0=cs[:, 12:16], scalar1=float(TBS), scalar2=None, op0=ALU.mult), ["cs_th", "cs_tmp"], ["cs_tmp"])
        sincos(cs[:, 20:24], cs[:, 16:20], cs[:, 28:32], R[:, 11, 0:4], R[:, 3, 0:4], ["cs_tmp"], ["cs_Es", "cs_Ec"], ["R11", "R3"])
        V(lambda e: e.tensor_scalar(out=cs[:, 24:28], in0=cs[:, 20:24], scalar1=-1.0, scalar2=None, op0=ALU.mult), ["cs_Es"], ["cs_nEs"])
        A(lambda e: e.activation(out=R[:, 2, :], in_=R[:, 2, :], func=AF.Exp), ["R012"], ["R2"])
        V(lambda e: e.tensor_scalar(out=R[:, 0, :], in0=R[:, 0, :], scalar1=-1e-4, scalar2=None, op0=ALU.min), ["R012"], ["R0"])
        V(lambda e: e.tensor_tensor(out=R[:, 4, :], in0=R[:, 0, :], in1=R[:, 2, :], op=ALU.mult), ["R0", "R2"], ["R4"])
        A(lambda e: e.activation(out=R[:, 4, :], in_=R[:, 4, :], func=AF.Exp), ["R4"], ["R4"])
        V(lambda e: e.tensor_tensor(out=R[:, 5, :], in0=R[:, 1, :], in1=R[:, 2, :], op=ALU.mult), ["R012", "R2"], ["R5"])
        sincos(R[:, 6, :], R[:, 7, :], R[:, 5, :], R[:, 11, :], R[:, 3, :], ["R5"], ["R6", "R7"], ["R11", "R3"])
        V(lambda e: e.tensor_tensor(out=R[:, 6, :], in0=R[:, 6, :], in1=R[:, 4, :], op=ALU.mult), ["R6", "R4"], ["R6"])
        V(lambda e: e.tensor_tensor(out=R[:, 7, :], in0=R[:, 7, :], in1=R[:, 4, :], op=ALU.mult), ["R7", "R4"], ["R7"])
        V(lambda e: e.tensor_scalar(out=R[:, 7, :], in0=R[:, 7, :], scalar1=-1.0, scalar2=None, op0=ALU.add), ["R7"], ["R7"])
        V(lambda e: e.tensor_tensor(out=R[:, 4, :], in0=R[:, 0, :], in1=R[:, 0, :], op=ALU.mult), ["R0", "R4"], ["R4"])
        V(lambda e: e.tensor_tensor(out=R[:, 5, :], in0=R[:, 1, :], in1=R[:, 1, :], op=ALU.mult), ["R012", "R5"], ["R5"])
        V(lambda e: e.tensor_tensor(out=R[:, 4, :], in0=R[:, 4, :], in1=R[:, 5, :], op=ALU.add), ["R4", "R5"], ["R4"])
        V(lambda e: e.reciprocal(out=R[:, 4, :], in_=R[:, 4, :]), ["R4"], ["R4"])
        V(lambda e: e.tensor_tensor(out=R[:, 5, :], in0=R[:, 7, :], in1=R[:, 0, :], op=ALU.mult), ["R7", "R0", "R5"], ["R5"])
        V(lambda e: e.tensor_tensor(out=R[:, 11, :], in0=R[:, 6, :], in1=R[:, 1, :], op=ALU.mult), ["R6", "R012", "R11"], ["R11"])
        V(lambda e: e.tensor_tensor(out=R[:, 5, :], in0=R[:, 5, :], in1=R[:, 11, :], op=ALU.add), ["R5", "R11"], ["R5"])
        V(lambda e: e.tensor_tensor(out=R[:, 5, :], in0=R[:, 5, :], in1=R[:, 4, :], op=ALU.mult), ["R5", "R4"], ["R5"])
        V(lambda e: e.tensor_tensor(out=R[:, 11, :], in0=R[:, 6, :], in1=R[:, 0, :], op=ALU.mult), ["R6", "R0", "R11"], ["R11"])
        V(lambda e: e.tensor_tensor(out=R[:, 3, :], in0=R[:, 7, :], in1=R[:, 1, :], op=ALU.mult), ["R7", "R012", "R3"], ["R3"])
        V(lambda e: e.tensor_tensor(out=R[:, 11, :], in0=R[:, 11, :], in1=R[:, 3, :], op=ALU.subtract), ["R11", "R3"], ["R11"])
        V(lambda e: e.tensor_tensor(out=R[:, 11, :], in0=R[:, 11, :], in1=R[:, 4, :], op=ALU.mult), ["R11", "R4"], ["R11"])
        V(lambda e: e.tensor_tensor(out=R[:, 3, :], in0=R[:, 5, :], in1=R[:, 8, :], op=ALU.mult), ["R5", "R89", "R3"], ["R3"])
        V(lambda e: e.tensor_tensor(out=R[:, 6, :], in0=R[:, 11, :], in1=R[:, 9, :], op=ALU.mult), ["R11", "R89", "R6"], ["R6"])
        V(lambda e: e.tensor_tensor(out=BbT[:, 0, :], in0=R[:, 3, :], in1=R[:, 6, :], op=ALU.subtract), ["R3", "R6"], ["BbT0"])
        V(lambda e: e.tensor_tensor(out=R[:, 3, :], in0=R[:, 5, :], in1=R[:, 9, :], op=ALU.mult), ["R5", "R89", "R3"], ["R3"])
        V(lambda e: e.tensor_tensor(out=R[:, 6, :], in0=R[:, 11, :], in1=R[:, 8, :], op=ALU.mult), ["R11", "R89", "R6"], ["R6"])
        V(lambda e: e.tensor_tensor(out=BbT[:, 1, :], in0=R[:, 3, :], in1=R[:, 6, :], op=ALU.add), ["R3", "R6"], ["BbT1"])
        A(lambda e: e.activation(out=CTb[:, 1, :, :], in_=CTf[:, 1, :, :], func=AF.Copy, scale=-1.0), ["CTf"], ["CTb1"])

        if MLEVEL == 0:
            P.emit()
            return nc
        PE_real = PE
        if os.environ.get('MSKIPG'):
            PE = lambda fn, r=(), w=(): None
        V(lambda e: e.tensor_scalar(out=gsc[:, 7, :], in0=ifs[:, :, 1], scalar1=vm[:, 9:10], scalar2=None, op0=ALU.add), ["ifs", "vm"], ["g7"])
        A(lambda e: e.activation(out=gsc[:, 7, :], in_=gsc[:, 7, :], func=AF.Exp, scale=-1.0), ["g7"], ["g7"])
        A(lambda e: e.activation(out=gsc[:, 0, :], in_=gsc[:, 7, :], func=AF.Ln, bias=1.0), ["g7"], ["g0"])
        V(lambda e: e.tensor_scalar(out=gsc[:, 0, :], in0=gsc[:, 0, :], scalar1=-1.0, scalar2=None, op0=ALU.mult), ["g0"], ["g0"])
        PE(lambda e: e.matmul(bank[0][:, 0:NCH], lhsT=tri, rhs=gsc[:, 0, :], start=True, stop=True), ["cf", "g0"], [("bank", 0)])
        V(lambda e: e.tensor_copy(out=gsc[:, 1, :], in_=bank[0][:, 0:NCH]), [("bank", 0)], ["g1"])
        V(lambda e: e.scalar_tensor_tensor(out=gsc[:, 2, :], in0=ifs[:, :, 0], scalar=vm[:, 8:9], in1=gsc[:, 1, :], op0=ALU.add, op1=ALU.subtract),
          ["ifs", "vm", "g1"], ["g2"])
        PE(lambda e: e.transpose(out=bank[1][0:NCH, 0:128], in_=gsc[:, 2, :], identity=ident), ["g2", "cf"], [("bank", 1)])
        V(lambda e: e.tensor_reduce(out=acol[:, :], in_=bank[1][0:NCH, 0:128], axis=AX.X, op=ALU.max), [("bank", 1)], ["acol"])
        V(lambda e: e.tensor_scalar(out=abc[:, :], in0=cf[0:64, 2, :], scalar1=acol[:, 0:1], scalar2=None, op0=ALU.mult), ["acol", "cf"], ["abc"])
        PE(lambda e: e.matmul(bank[2][:, 0:NCH], lhsT=abc[:, :], rhs=cf[0:64, 0, 0:64], start=True, stop=True), ["abc", "cf"], [("bank", 2)])
        PE(lambda e: e.matmul(bank[3][:, 0:NCH], lhsT=cf[:, 3, :], rhs=gsc[:, 1, :], start=True, stop=True), ["g1", "cf"], [("bank", 3)])
        V(lambda e: e.tensor_copy(out=grow[:, 1, 0:NCH], in_=bank[2][:, 0:NCH]), [("bank", 2)], ["gr1"])
        V(lambda e: e.tensor_copy(out=grow[:, 0, 0:NCH], in_=bank[3][:, 0:NCH]), [("bank", 3)], ["gr0"])
        V(lambda e: e.tensor_tensor(out=grow[:, 2, 0:NCH], in0=grow[:, 1, 0:NCH], in1=grow[:, 0, 0:NCH], op=ALU.add), ["gr0", "gr1"], ["gr2"])
        V(lambda e: e.memset(grow[:, 3, 0:1], 0.0), [], ["gr3a"])
        V(lambda e: e.tensor_tensor_scan(out=grow[:, 3, 1:NCH + 1], data0=grow[:, 0, 0:NCH], data1=grow[:, 2, 0:NCH], initial=0.0, op0=ALU.add, op1=ALU.max),
          ["gr0", "gr2", "gr3a"], ["gr3"])
        V(lambda e: e.tensor_tensor(out=gsc[:, 5, :], in0=grow[:, 3, 1:NCH + 1], in1=grow[:, 0, 0:NCH], op=ALU.subtract), ["gr3", "gr0"], ["g5"])
        V(lambda e: e.tensor_tensor(out=gsc[:, 6, :], in0=grow[:, 3, 0:NCH], in1=gsc[:, 5, :], op=ALU.subtract), ["gr3", "gr3a", "g5"], ["g6"])
        A(lambda e: e.activation(out=gsc[:, 6, :], in_=gsc[:, 6, :], func=AF.Exp), ["g6"], ["g6"])
        V(lambda e: e.tensor_tensor(out=gsc[:, 3, :], in0=gsc[:, 2, :], in1=gsc[:, 5, :], op=ALU.subtract), ["g2", "g5"], ["g3"])
        A(lambda e: e.activation(out=gsc[:, 3, :], in_=gsc[:, 3, :], func=AF.Exp), ["g3"], ["g3"])
        V(lambda e: e.tensor_tensor(out=gsc[:, 4, :], in0=gsc[:, 1, :], in1=gsc[:, 5, :], op=ALU.add), ["g1", "g5"], ["g4"])
        A(lambda e: e.activation(out=gsc[:, 4, :], in_=gsc[:, 4, :], func=AF.Exp, scale=-1.0), ["g4"], ["g4"])

        if MLEVEL == 1:
            P.emit()
            return nc
        PE = PE_real
        if MSUB == -1:
            V = lambda fn, r=(), w=(): None
        V(lambda e: e.memset(Cst[:, :, :], 0.0), [], ["Cst"])
        V(lambda e: e.memset(nst[:, :], 0.0), [], ["nst"])
        V(lambda e: e.memset(Cwb[:, :, :], 0.0), [], ["Cwb"])
        V(lambda e: e.memset(xpad[:, :, 0:3], 0.0), [], [("xpadh", 0), ("xpadh", 1)])
        V(lambda e: e.memset(vext[:, :, 256:257], 1.0), [], ["vones"])

        V = lambda fn, r=(), w=(): P.add("vector", fn, reads=r, writes=w)
        hv = hT.rearrange("(dt p) t -> p dt t", p=128)
        yv = ymlT_o.rearrange("(e p) t -> p e t", p=128)
        finals = []
        for bi in range(NBLK if MLEVEL >= 5 else (2 if MLEVEL == 4 else 1)):
            c0 = bi * TBS
            s = bi % 2
            P.dma("sync", lambda e, s=s, c0=c0: e.dma_start(out=hblk[:, s, :, :], in_=hv[:, :, c0:c0 + TBS]), ("hblk", s), writes=[("hblk", s)])
            HB = ("hblk", s)
            b0 = bank[0]
            if MSUB == -2:
                continue
            for dt in range(DT):
                PE(lambda e, dt=dt, s=s: e.matmul(b0[:, :], lhsT=wm[:, dt, 0:128], rhs=hblk[:, s, dt, :], start=(dt == 0), stop=(dt == DT - 1)),
                   ["wm", HB], [("bank", 0)])
            if MSUB == -3:
                continue
            A(lambda e: e.activation(out=ubf[:, :], in_=b0[:, :], func=AF.Copy), [("bank", 0)], ["ubf"])
            if MSUB == -4:
                continue
            V(lambda e: e.tensor_scalar(out=du[:, :], in0=b0[:, :], scalar1=vm[:, 6:7], scalar2=None, op0=ALU.mult), [("bank", 0), "vm", "ubf"], ["du"])
            for eo in range(2 if MSUB >= 1 else 0):
                for dt in range(DT):
                    PE(lambda e, dt=dt, s=s, eo=eo: e.matmul(b0[:, :], lhsT=wm[:, dt, 128 + eo * 128:256 + eo * 128], rhs=hblk[:, s, dt, :],
                                                             start=(dt == 0), stop=(dt == DT - 1)), ["wm", HB], [("bank", 0)])
                A(lambda e, eo=eo: e.activation(out=xpad[:, eo, 3:3 + TBS], in_=b0[:, :], func=AF.Copy), [("bank", 0)], [("xpad", eo)])
                XR = [("xpad", eo), ("xpadh", eo)]
                V(lambda e, eo=eo: e.tensor_scalar(out=cacc[:, eo, :], in0=xpad[:, eo, 0:TBS], scalar1=convw[:, eo, 0:1], scalar2=vm[:, eo:eo + 1],
                                                   op0=ALU.mult, op1=ALU.add), XR + ["convw", "vm"], [("cacc", eo)])
                for j in range(1, 4):
                    V(lambda e, eo=eo, j=j: e.scalar_tensor_tensor(out=cacc[:, eo, :], in0=xpad[:, eo, j:j + TBS], scalar=convw[:, eo, j:j + 1],
                                                                   in1=cacc[:, eo, :], op0=ALU.mult, op1=ALU.add), XR + [("cacc", eo), "convw"], [("cacc", eo)])
                G(lambda e, eo=eo: e.tensor_copy(out=xpad[:, eo, 0:3], in_=xpad[:, eo, TBS:TBS + 3]), XR, [("xpadh", eo)])
                A(lambda e, eo=eo: e.activation(out=xc32[:, eo, :], in_=cacc[:, eo, :], func=AF.Sigmoid), [("cacc", eo)], [("xc32", eo)])
                G(lambda e, eo=eo: e.tensor_tensor(out=xc32[:, eo, :], in0=xc32[:, eo, :], in1=cacc[:, eo, :], op=ALU.mult), [("xc32", eo), ("cacc", eo)], [("xc32", eo)])
                G(lambda e, eo=eo: e.tensor_copy(out=xcb[:, eo, :], in_=xc32[:, eo, :]), [("xc32", eo)], [("xcb", eo)])
                A(lambda e, eo=eo: e.activation(out=skx[:, eo, :], in_=xc32[:, eo, :], func=AF.Copy, scale=vm[:, 4 + eo:5 + eo]), [("xc32", eo), "vm"], [("skx", eo)])
            for eo in range(2 if MSUB >= 2 else 0):
                for dd in range(2):
                    PE(lambda e, eo=eo, dd=dd: e.matmul(b0[:, :], lhsT=wq[:, dd, eo * 128:(eo + 1) * 128], rhs=xcb[:, dd, :], start=(dd == 0), stop=(dd == 1)),
                       ["wq", ("xcb", dd)], [("bank", 0)])
                A(lambda e, eo=eo: e.activation(out=qT[:, eo, :], in_=b0[:, :], func=AF.Copy, scale=1.0 / 16.0), [("bank", 0)], [("qT", eo)])
                for dd in range(2):
                    PE(lambda e, eo=eo, dd=dd: e.matmul(b0[:, :], lhsT=wk[:, dd, eo * 128:(eo + 1) * 128], rhs=xcb[:, dd, :], start=(dd == 0), stop=(dd == 1)),
                       ["wk", ("xcb", dd)], [("bank", 0)])
                V(lambda e, eo=eo: e.tensor_copy(out=kT[:, eo, :], in_=b0[:, :]), [("bank", 0)], [("kT", eo)])
            for i in range(4):
                j = i
                ch = i
                gch = bi * 4 + ch
                cc = ch * 128
                ss = gch % 2
                ZR, ZI = ("zr", j), ("zi", j)
                sj = j % 2
                PE(lambda e, j=j: e.matmul(bank[2][:, :], lhsT=BbT[:, 0, j * 128:(j + 1) * 128], rhs=ubf[:, :], start=True, stop=True), ["BbT0", "ubf"], [("bank", 2)])
                PE(lambda e, j=j: e.matmul(bank[3][:, :], lhsT=BbT[:, 1, j * 128:(j + 1) * 128], rhs=ubf[:, :], start=True, stop=True), ["BbT1", "ubf"], [("bank", 3)])
                for dd in range(2):
                    PE(lambda e, dd=dd, cc=cc: e.matmul(bank[1][:, 256:512], lhsT=xcb[:, dd, cc:cc + 128], rhs=wk[:, dd, :], start=(dd == 0), stop=(dd == 1)),
                       ["wk", ("xcb", dd)], [("bank", 1, "k")])
                for dt in range(DT):
                    PE(lambda e, dt=dt, s=s, cc=cc: e.matmul(b0[:, :], lhsT=hblk[:, s, dt, cc:cc + 128], rhs=wm[:, dt, 384:896], start=(dt == 0), stop=(dt == DT - 1)),
                       ["wm", HB], [("bank", 0)])
                for dd in range(2):
                    PE(lambda e, dd=dd, cc=cc: e.matmul(bank[5][:, 0:128], lhsT=kT[:, dd, cc:cc + 128], rhs=qT[:, dd, cc:cc + 128], start=(dd == 0), stop=(dd == 1)),
                       [("kT", dd), ("qT", dd)], [("bank", 5, "s")])
                V(lambda e, j=j: e.tensor_tensor(out=t1[:, 0, :], in0=bank[2][:, :], in1=cosT[:, j, :], op=ALU.mult), [("bank", 2), ("cosT", j)], [("t1", 0)])
                V(lambda e, j=j: e.tensor_tensor(out=t2[:, 0, :], in0=bank[3][:, :], in1=sinT[:, j, :], op=ALU.mult), [("bank", 3), ("sinT", j)], [("t2", 0)])
                G(lambda e: e.tensor_tensor(out=zin[:, 0, :], in0=t1[:, 0, :], in1=t2[:, 0, :], op=ALU.add), [("t1", 0), ("t2", 0)], [("zin", 0)])
                V(lambda e, j=j: e.tensor_tensor(out=t1[:, 1, :], in0=bank[3][:, :], in1=cosT[:, j, :], op=ALU.mult), [("bank", 3), ("cosT", j)], [("t1", 1)])
                V(lambda e, j=j: e.tensor_tensor(out=t2[:, 1, :], in0=bank[2][:, :], in1=sinT[:, j, :], op=ALU.mult), [("bank", 2), ("sinT", j)], [("t2", 1)])
                G(lambda e: e.tensor_tensor(out=zin[:, 1, :], in0=t1[:, 1, :], in1=t2[:, 1, :], op=ALU.subtract), [("t1", 1), ("t2", 1)], [("zin", 1)])
                V(lambda e, ch=ch, gch=gch: e.tensor_scalar(out=kp[:, ch, :], in0=bank[1][:, 256:512], scalar1=gsc[:, 3, gch:gch + 1], scalar2=None, op0=ALU.mult),
                  [("bank", 1, "k"), "g3"], [("kp", ch)])
                V(lambda e, ch=ch: e.tensor_copy(out=vext[:, ch, 0:256], in_=b0[:, 0:256]), [("bank", 0)], [("vext", ch)])
                A(lambda e, ch=ch: e.activation(out=og[:, ch, :], in_=b0[:, 256:512], func=AF.Sigmoid), [("bank", 0)], [("og", ch)])
                V(lambda e, ss=ss, gch=gch: e.scalar_tensor_tensor(out=Sm[:, ss, :], in0=bank[5][:, 0:128], scalar=gsc[:, 3, gch:gch + 1], in1=tri,
                                                                   op0=ALU.mult, op1=ALU.mult), [("bank", 5, "s"), "g3", "cf"], [("Sm", ss)])
                PE(lambda e, ss=ss, ch=ch: e.matmul(bank[6][:, 0:257], lhsT=Sm[:, ss, :], rhs=vext[:, ch, :], start=True, stop=False),
                   [("Sm", ss), ("vext", ch), "vones"], [("bank", 6)])
                for dd in range(2):
                    PE(lambda e, dd=dd, cc=cc: e.matmul(bank[6][:, 0:257], lhsT=qT[:, dd, cc:cc + 128], rhs=Cwb[:, dd, :], start=False, stop=(dd == 1)),
                       [("qT", dd), "Cwb"], [("bank", 6)])
                if bi == 0:
                    if j == 0:
                        V(lambda e: e.memset(init[:, :, :], 0.0), [], [("init", jj) for jj in range(4)])
                else:
                    V(lambda e, j=j: e.tensor_scalar(out=init[:, j, 2:3], in0=zr[:, j, TBS - 1:TBS], scalar1=cs[:, 16 + j:17 + j], scalar2=None, op0=ALU.mult),
                      [ZR, "cs_Ec"], [("initt", j)])
                    V(lambda e, j=j: e.scalar_tensor_tensor(out=init[:, j, 0:1], in0=zi[:, j, TBS - 1:TBS], scalar=cs[:, 24 + j:25 + j], in1=init[:, j, 2:3],
                                                            op0=ALU.mult, op1=ALU.add), [ZI, "cs_nEs", ("initt", j)], [("init", j)])
                    V(lambda e, j=j: e.tensor_scalar(out=init[:, j, 3:4], in0=zi[:, j, TBS - 1:TBS], scalar1=cs[:, 16 + j:17 + j], scalar2=None, op0=ALU.mult),
                      [ZI, "cs_Ec"], [("initu", j)])
                    V(lambda e, j=j: e.scalar_tensor_tensor(out=init[:, j, 1:2], in0=zr[:, j, TBS - 1:TBS], scalar=cs[:, 20 + j:21 + j], in1=init[:, j, 3:4],
                                                            op0=ALU.mult, op1=ALU.add), [ZR, "cs_Es", ("initu", j)], [("init", j)])
                V(lambda e, j=j: e.tensor_tensor_scan(out=zr[:, j, :], data0=rT[:, j, :], data1=zin[:, 0, :], initial=init[:, j, 0:1], op0=ALU.mult, op1=ALU.add),
                  [("rT", j), ("zin", 0), ("init", j), "zr_all"], [ZR])
                V(lambda e, j=j: e.tensor_tensor_scan(out=zi[:, j, :], data0=rT[:, j, :], data1=zin[:, 1, :], initial=init[:, j, 1:2], op0=ALU.mult, op1=ALU.add),
                  [("rT", j), ("zin", 1), ("init", j)], [ZI])
                G(lambda e, j=j: e.tensor_tensor(out=pt[:, 0, :], in0=zr[:, j, :], in1=cosT[:, j, :], op=ALU.mult), [ZR, ("cosT", j)], [("pt", 0)])
                G(lambda e, j=j: e.tensor_tensor(out=pt[:, 1, :], in0=zi[:, j, :], in1=sinT[:, j, :], op=ALU.mult), [ZI, ("sinT", j)], [("pt", 1)])
                G(lambda e, sj=sj: e.tensor_tensor(out=sre[:, sj, :], in0=pt[:, 0, :], in1=pt[:, 1, :], op=ALU.subtract), [("pt", 0), ("pt", 1)], [("sre", sj)])
                G(lambda e, j=j: e.tensor_tensor(out=pt[:, 2, :], in0=zi[:, j, :], in1=cosT[:, j, :], op=ALU.mult), [ZI, ("cosT", j)], [("pt", 2)])
                G(lambda e, j=j: e.tensor_tensor(out=pt[:, 3, :], in0=zr[:, j, :], in1=sinT[:, j, :], op=ALU.mult), [ZR, ("sinT", j)], [("pt", 3)])
                G(lambda e, sj=sj: e.tensor_tensor(out=sim_[:, sj, :], in0=pt[:, 2, :], in1=pt[:, 3, :], op=ALU.add), [("pt", 2), ("pt", 3)], [("sim", sj)])
                A(lambda e: e.activation(out=sm[:, 5:6], in_=bank[6][:, 256:257], func=AF.Abs), [("bank", 6)], ["sm5"])
                V(lambda e, gch=gch: e.tensor_scalar(out=sm[:, 0:1], in0=sm[:, 5:6], scalar1=gsc[:, 4, gch:gch + 1], scalar2=None, op0=ALU.max),
                  ["sm5", "g4"], ["sm0"])
                V(lambda e: e.reciprocal(out=sm[:, 1:2], in_=sm[:, 0:1]), ["sm0"], ["sm1"])
                V(lambda e, ch=ch: e.scalar_tensor_tensor(out=hc4[:, ch, :], in0=bank[6][:, 0:256], scalar=sm[:, 1:2], in1=og[:, ch, :], op0=ALU.mult, op1=ALU.mult),
                  [("bank", 6), "sm1", ("og", ch)], [("hc", ch)])
                PE(lambda e, ch=ch: e.matmul(bank[1][:, 0:256], lhsT=kp[:, ch, 0:128], rhs=vext[:, ch, 0:256], start=True, stop=True),
                   [("kp", ch), ("vext", ch)], [("bank", 1, "kv")])
                PE(lambda e, ch=ch: e.matmul(bank[5][:, 128:384], lhsT=kp[:, ch, 128:256], rhs=vext[:, ch, 0:256], start=True, stop=True),
                   [("kp", ch), ("vext", ch)], [("bank", 5, "kv")])
                PE(lambda e, ch=ch: e.matmul(bank[5][:, 384:385], lhsT=kp[:, ch, 0:128], rhs=vext[:, ch, 256:257], start=True, stop=True),
                   [("kp", ch), "vones"], [("bank", 5, "n0")])
                PE(lambda e, ch=ch: e.matmul(bank[5][:, 385:386], lhsT=kp[:, ch, 128:256], rhs=vext[:, ch, 256:257], start=True, stop=True),
                   [("kp", ch), "vones"], [("bank", 5, "n1")])
                wcol = gsc[:, 6, gch:gch + 1]
                V(lambda e, wcol=wcol: e.scalar_tensor_tensor(out=Cst[:, 0, :], in0=Cst[:, 0, :], scalar=wcol, in1=bank[1][:, 0:256], op0=ALU.mult, op1=ALU.add),
                  ["Cst", "g6", ("bank", 1, "kv")], ["Cst"])
                V(lambda e, wcol=wcol: e.scalar_tensor_tensor(out=Cst[:, 1, :], in0=Cst[:, 1, :], scalar=wcol, in1=bank[5][:, 128:384], op0=ALU.mult, op1=ALU.add),
                  ["Cst", "g6", ("bank", 5, "kv")], ["Cst"])
                V(lambda e, wcol=wcol: e.scalar_tensor_tensor(out=nst[:, :], in0=nst[:, :], scalar=wcol, in1=bank[5][:, 384:386], op0=ALU.mult, op1=ALU.add),
                  ["nst", "g6", ("bank", 5, "n0"), ("bank", 5, "n1")], ["nst"])
                if gch + 1 < NCH:
                    wn = gsc[:, 6, gch + 1:gch + 2]
                    A(lambda e, wn=wn: e.activation(out=Cwb[:, :, 0:256], in_=Cst[:, :, :], func=AF.Copy, scale=wn), ["Cst", "g6"], ["Cwb"])
                    A(lambda e, wn=wn: e.activation(out=Cwb[:, :, 256], in_=nst[:, :], func=AF.Copy, scale=wn), ["nst", "g6"], ["Cwb"])
                PE(lambda e, j=j, sj=sj: e.matmul(bank[4][:, :], lhsT=CTb[:, 0, j, :], rhs=sre[:, sj, :], start=(j == 0), stop=False), ["CTb0", ("sre", sj)], [("bank", 4)])
                PE(lambda e, j=j, sj=sj: e.matmul(bank[4][:, :], lhsT=CTb[:, 1, j, :], rhs=sim_[:, sj, :], start=False, stop=(j == 3)), ["CTb1", ("sim", sj)], [("bank", 4)])
                A(lambda e, ch=ch: e.activation(out=hsq[:, :], in_=hc4[:, ch, :], func=AF.Square), [("hc", ch)], ["hsq"])
                V(lambda e, ch=ch: e.tensor_reduce(out=ssq[:, ch:ch + 1], in_=hsq[:, :], axis=AX.X, op=ALU.add), ["hsq"], [("ssq", ch)])
            V(lambda e: e.tensor_tensor(out=yy[:, :], in0=bank[4][:, :], in1=du[:, :], op=ALU.add), [("bank", 4), "du"], ["yy"])
            G(lambda e: e.tensor_tensor(out=g1[:, :], in0=yy[:, :], in1=yy[:, :], op=ALU.mult), ["yy"], ["g1t"])
            G(lambda e: e.tensor_scalar(out=g1[:, :], in0=g1[:, :], scalar1=0.044715, scalar2=1.0, op0=ALU.mult, op1=ALU.add), ["g1t"], ["g1t"])
            G(lambda e: e.tensor_tensor(out=g1[:, :], in0=g1[:, :], in1=yy[:, :], op=ALU.mult), ["g1t", "yy"], ["g1t"])
            A(lambda e: e.activation(out=g2[:, :], in_=g1[:, :], func=AF.Sigmoid, scale=1.5957691216057308), ["g1t"], ["g2t"])
            G(lambda e, s=s: e.tensor_tensor(out=zout[:, s, :], in0=yy[:, :], in1=g2[:, :], op=ALU.mult), ["yy", "g2t"], [("zout", s)])
            finals.append(P.dma("sync", lambda e, s=s, c0=c0: e.dma_start(out=zT_o[:, c0:c0 + TBS], in_=zout[:, s, :]), ("zst", s), reads=[("zout", s)]))
            V(lambda e: e.tensor_scalar(out=ssq[:, 4:8], in0=ssq[:, 0:4], scalar1=1.0 / 256.0, scalar2=EPS, op0=ALU.mult, op1=ALU.add),
              [("ssq", c) for c in range(4)], ["rs"])
            A(lambda e: e.activation(out=ssq[:, 4:8], in_=ssq[:, 4:8], func=AF.Sqrt), ["rs"], ["rs"])
            V(lambda e: e.reciprocal(out=ssq[:, 4:8], in_=ssq[:, 4:8]), ["rs"], ["rs"])
            for ch in range(4):
                cc = ch * 128
                hs = ch % 2
                A(lambda e, ch=ch, hs=hs: e.activation(out=hn[:, hs, :], in_=hc4[:, ch, :], func=AF.Copy, scale=ssq[:, 4 + ch:5 + ch]), [("hc", ch), "rs"], [("hn", hs)])
                for eo in range(2):
                    PE(lambda e, eo=eo, hs=hs: e.transpose(out=bankT[:, eo * 128:(eo + 1) * 128], in_=hn[:, hs, eo * 128:(eo + 1) * 128], identity=cb[:, :]),
                       [("hn", hs), "cb"], [("bankT", eo)])
                    V(lambda e, eo=eo, s=s, cc=cc: e.scalar_tensor_tensor(out=ymlo[:, s, eo, cc:cc + 128], in0=bankT[:, eo * 128:(eo + 1) * 128], scalar=vm[:, 2 + eo:3 + eo],
                                                                          in1=skx[:, eo, cc:cc + 128], op0=ALU.mult, op1=ALU.add),
                      [("bankT", eo), "vm", ("skx", eo)], [("ymlo", s)])
            finals.append(P.dma("sync", lambda e, s=s, c0=c0: e.dma_start(out=yv[:, :, c0:c0 + TBS], in_=ymlo[:, s, :, :]), ("yst", s), reads=[("ymlo", s)]))
        P.emit(final_waits=finals)
    return nc


def m_inputs(I, l, c, hT_full, ifp_full):
    b, r = c // 4, c % 4
    f32 = np.float32
    w_in = I["w_in"][l]
    w_m = np.concatenate([w_in[:, 128 * r:128 * r + 128], w_in[:, 512 + 256 * r:512 + 256 * r + 256],
                          w_in[:, 1536 + 256 * r:1536 + 256 * r + 256], w_in[:, 2560 + 256 * r:2560 + 256 * r + 256]], axis=1)
    ifg = np.stack([ifp_full[b][:, r], ifp_full[b][:, 4 + r]], axis=-1).reshape(NCH, 128, 2).transpose(1, 0, 2)
    convw = I["ml_conv_w"][l][:, 256 * r:256 * r + 256].reshape(4, 2, 128).transpose(2, 1, 0)
    vm = np.zeros((128, 16), f32)
    col2 = lambda v: v[256 * r:256 * r + 256].reshape(2, 128).T
    vm[:, 0:2] = col2(I["ml_conv_b"][l])
    vm[:, 2:4] = col2(I["ml_norm"][l])
    vm[:, 4:6] = col2(I["ml_skip"][l])
    vm[:, 6] = I["s5_d"][l][128 * r:128 * r + 128]
    vm[:, 8] = I["b_if"][l][r]
    vm[:, 9] = I["b_if"][l][4 + r]
    gs = slice(8 * r, 8 * r + 8)
    lam_re = I["s5_lam_re"][l][gs].reshape(512)
    lam_im = I["s5_lam_im"][l][gs].reshape(512)
    ldt = np.repeat(I["s5_log_dt"][l][gs], 64)
    lamC = np.zeros((128, 12), f32)
    lamC[:, 0:4] = lam_re.reshape(4, 128).T
    lamC[:, 4:8] = lam_im.reshape(4, 128).T
    lamC[:, 8:12] = ldt.reshape(4, 128).T
    lamR = np.broadcast_to(np.stack([lam_re, lam_im, ldt])[None], (128, 3, 512))
    BT = np.zeros((128, 2, 512), f32)
    CT = np.zeros((128, 2, 4, 128), f32)
    for gl in range(8):
        g = 8 * r + gl
        BT[16 * gl:16 * gl + 16, 0, gl * 64:(gl + 1) * 64] = I["s5_b_re"][l][g].T
        BT[16 * gl:16 * gl + 16, 1, gl * 64:(gl + 1) * 64] = I["s5_b_im"][l][g].T
        j, half = gl // 2, gl % 2
        CT[half * 64:(half + 1) * 64, 0, j, 16 * gl:16 * gl + 16] = I["s5_c_re"][l][g].T
        CT[half * 64:(half + 1) * 64, 1, j, 16 * gl:16 * gl + 16] = I["s5_c_im"][l][g].T
    return {"hT": hT_full[b], "ifg": np.ascontiguousarray(ifg, f32), "w_m": np.ascontiguousarray(w_m),
            "wq": I["ml_wq"][l][r], "wk": I["ml_wk"][l][r], "convw": np.ascontiguousarray(convw, f32), "vm": vm,
            "lamC": lamC, "lamR": np.ascontiguousarray(lamR, f32), "BT": BT, "CT": CT}


def m_consts():
    f32 = np.float32
    cf = np.zeros((128, 4, 128), f32)
    cf[:, 0, :] = np.eye(128)
    cf[:, 1, :] = np.triu(np.ones((128, 128)))
    cf[:, 2, :] = 1.0
    cf[127, 3, :] = 1.0
    iota = np.broadcast_to(np.arange(512, dtype=f32)[None], (128, 512))
    return {"cf32": cf, "cb16": np.eye(128).astype(ml_dtypes.bfloat16), "iota": np.ascontiguousarray(iota)}


def run_M(I, l, hT_sh, ifp_sh):
    hT_full = [np.concatenate([hT_sh[4 * b + r] for r in range(4)], axis=1) for b in range(2)]
    ifp_full = [np.concatenate([ifp_sh[4 * b + r] for r in range(4)], axis=0) for b in range(2)]
    cst = m_consts()
    in_maps = []
    for c in range(8):
        m = m_inputs(I, l, c, hT_full, ifp_full)
        m.update(cst)
        in_maps.append(m)
    nc = build_M()
    return _run(nc, in_maps)
```

```python
import contextlib
import math
import os
import numpy as np
import ml_dtypes
import concourse.bass as bass
import concourse.mybir as mybir
from concourse.bass_utils import run_bass_kernel_spmd

F32 = mybir.dt.float32
BF16 = mybir.dt.bfloat16
AF = mybir.ActivationFunctionType
ALU = mybir.AluOpType
AX = mybir.AxisListType

D = 1024
DT = 8
SEQ = 8192
NTOK = 2048
DFF = 2816
FT = 22
DEPTH = 4
EPS = 1e-6
SBT = 1024
NSB = NTOK // SBT
TBS = 512
NTB = SBT // TBS
ENGS = ("tensor", "vector", "scalar", "gpsimd", "sync")
TWO_PI = 2.0 * math.pi


class Op:
    def __init__(self, eng, fn):
        self.eng = eng
        self.fn = fn
        self.dma_sem = None
        self.dma_val = 0
        self.has_dep = False
        self.deps = []
        self.cval = 0


class Prog:
    def __init__(self, nc):
        self.nc = nc
        self.ops = {e: [] for e in ENGS}
        self.last_writer = {}
        self.readers = {}
        self.dma_counts = {}
        self.dma_last = {}
        self.pending_barrier = {e: [] for e in ENGS}

    def add(self, eng, fn, reads=(), writes=(), extra_deps=()):
        rd, wr = [], []
        for t in reads:
            if isinstance(t, tuple) and t[0] in ("bank", "bankT"):
                wr.append(("bank", t[1]) if t[0] == "bank" else ("bankT", 0))
            else:
                rd.append(t)
        for t in writes:
            if isinstance(t, tuple) and t[0] in ("bank", "bankT"):
                wr.append(("bank", t[1]) if t[0] == "bank" else ("bankT", 0))
            else:
                wr.append(t)
        reads, writes = rd, wr
        op = Op(eng, fn)
        deps = []
        for t in reads:
            w = self.last_writer.get(t)
            if w is not None:
                deps.append(w)
        for t in writes:
            w = self.last_writer.get(t)
            if w is not None:
                deps.append(w)
            deps.extend(self.readers.get(t, ()))
        for t in reads:
            self.readers.setdefault(t, []).append(op)
        for t in writes:
            self.last_writer[t] = op
            self.readers[t] = []
        deps.extend(extra_deps)
        if self.pending_barrier[eng]:
            deps.extend(self.pending_barrier[eng])
            self.pending_barrier[eng] = []
        op.deps = [d for d in deps if d is not op and not (eng == "tensor" and d.eng == "tensor" and d.dma_sem is None)]
        for d in op.deps:
            d.has_dep = True
        self.ops[eng].append(op)
        return op

    def dma(self, eng, fn, semkey, reads=(), writes=(), extra_deps=()):
        op = self.add(eng, fn, reads, writes, extra_deps)
        op.dma_sem = semkey
        self.dma_counts[semkey] = self.dma_counts.get(semkey, 0) + 1
        op.dma_val = 16 * self.dma_counts[semkey]
        self.dma_last[semkey] = op
        return op

    def barrier(self):
        lasts = []
        for e in ENGS:
            for op in reversed(self.ops[e]):
                if op.dma_sem is None:
                    lasts.append(op)
                    break
        lasts.extend(self.dma_last.values())
        for e in ENGS:
            self.pending_barrier[e] = list(lasts)

    def emit(self, final_waits=()):
        nc = self.nc
        cnt = {e: 0 for e in ENGS}
        for e in ENGS:
            for op in self.ops[e]:
                if op.dma_sem is None and op.has_dep:
                    cnt[e] += 1
                    op.cval = cnt[e]
        with contextlib.ExitStack() as st:
            sems = {}
            for e in ENGS:
                if cnt[e] > 0:
                    sems[("eng", e)] = st.enter_context(nc.semaphore("s_" + e))
            for i, k in enumerate(sorted(self.dma_counts.keys(), key=str)):
                sems[("dma", k)] = st.enter_context(nc.semaphore("d%d" % i))
            block = st.enter_context(nc.Block())

            def body_for(e):
                def body(engine):
                    waited = {}
                    for op in self.ops[e]:
                        need = {}
                        for d in op.deps:
                            if d.dma_sem is not None:
                                key, val = ("dma", d.dma_sem), d.dma_val
                            else:
                                key, val = ("eng", d.eng), d.cval
                            if need.get(key, 0) < val:
                                need[key] = val
                        for key, val in need.items():
                            if waited.get(key, 0) >= val:
                                continue
                            engine.wait_ge(sems[key], val)
                            waited[key] = val
                        ins = op.fn(engine)
                        if op.dma_sem is not None:
                            ins.then_inc(sems[("dma", op.dma_sem)], 16)
                        elif op.has_dep:
                            ins.then_inc(sems[("eng", e)], 1)
                    if e == "sync":
                        for op in final_waits:
                            engine.wait_ge(sems[("dma", op.dma_sem)], op.dma_val)
                return body

            for e in ENGS:
                if self.ops[e] or (e == "sync" and final_waits):
                    getattr(block, e)(body_for(e))


class Ctx:
    pass


def emit_norm(K, sb, gcol, tag):
    P = K.P
    for tb in range(NTB):
        c0 = sb * SBT + tb * TBS
        lc = tb * TBS
        psn = K.bank[6]
        for dt in range(DT):
            sq = K.sq[:, dt % 2, :]
            P.add("scalar", lambda e, sq=sq, dt=dt, c0=c0: e.activation(out=sq, in_=K.x[:, dt, c0:c0 + TBS], func=AF.Square),
                  reads=[("x", dt, sb, tb)], writes=[("sq", dt % 2)])
            P.add("tensor", lambda e, sq=sq, dt=dt, psn=psn: e.matmul(psn[:, :], lhsT=K.ones32[:, :], rhs=sq, start=(dt == 0), stop=(dt == DT - 1)),
                  reads=[("sq", dt % 2), "consts"], writes=[("bank", 6)])
        P.add("vector", lambda e, psn=psn: e.tensor_scalar(out=K.rstd[:, :], in0=psn[:, :], scalar1=1.0 / D, scalar2=EPS, op0=ALU.mult, op1=ALU.add),
              reads=[("bank", 6)], writes=["rstd"])
        P.add("scalar", lambda e: e.activation(out=K.rstd[:, :], in_=K.rstd[:, :], func=AF.Sqrt), reads=["rstd"], writes=["rstd"])
        P.add("vector", lambda e: e.reciprocal(out=K.rstd[:, :], in_=K.rstd[:, :]), reads=["rstd"], writes=["rstd"])
        for dt in range(DT):
            eng = "vector"
            P.add(eng, lambda e, dt=dt, c0=c0, lc=lc: e.scalar_tensor_tensor(out=K.h[:, dt, lc:lc + TBS], in0=K.x[:, dt, c0:c0 + TBS], scalar=gcol(dt),
                                                                           in1=K.rstd[:, :], op0=ALU.mult, op1=ALU.mult),
                  reads=[("x", dt, sb, tb), "rstd", "vecs"], writes=[("h", dt, tb)])


def emit_ffn(K, sb, wg, wu, wd, gcol, tag):
    P = K.P
    emit_norm(K, sb, gcol, tag)
    wg_v = wg.rearrange("(dt p) f -> p dt f", p=128)
    wu_v = wu.rearrange("(dt p) f -> p dt f", p=128)
    wd_v = wd.rearrange("(ft p) d -> p ft d", p=128)
    for fg in range(FT // 2):
        s = K.cnt_w % 2
        K.cnt_w += 1
        P.dma("gpsimd", lambda e, s=s, fg=fg: e.dma_start(out=K.wgu[:, s, 0, :, :], in_=wg_v[:, :, fg * 256:(fg + 1) * 256]),
              ("wgu", s), writes=[("wg", s)])
        P.dma("gpsimd", lambda e, s=s, fg=fg: e.dma_start(out=K.wgu[:, s, 1, :, :], in_=wu_v[:, :, fg * 256:(fg + 1) * 256]),
              ("wgu", s), writes=[("wu", s)])
        for fi in range(2):
            ft = fg * 2 + fi
            for tb in range(NTB):
                lc = tb * TBS
                pb = K.cnt_p % 2
                K.cnt_p += 1
                pg, pu = K.bank[pb], K.bank[2 + pb]
                for dt in range(DT):
                    P.add("tensor", lambda e, dt=dt, s=s, lc=lc, pg=pg, fi=fi: e.matmul(pg[:, :], lhsT=K.wgu[:, s, 0, dt, fi * 128:(fi + 1) * 128], rhs=K.h[:, dt, lc:lc + TBS],
                                                                                      start=(dt == 0), stop=(dt == DT - 1)),
                          reads=[("wg", s), ("h", dt, tb)], writes=[("bank", pb)])
                for dt in range(DT):
                    P.add("tensor", lambda e, dt=dt, s=s, lc=lc, pu=pu, fi=fi: e.matmul(pu[:, :], lhsT=K.wgu[:, s, 1, dt, fi * 128:(fi + 1) * 128], rhs=K.h[:, dt, lc:lc + TBS],
                                                                                      start=(dt == 0), stop=(dt == DT - 1)),
                          reads=[("wu", s), ("h", dt, tb)], writes=[("bank", 2 + pb)])
                sg = K.tmp[:, pb, :]
                P.add("scalar", lambda e, sg=sg, pg=pg: e.activation(out=sg, in_=pg[:, :], func=AF.Silu),
                      reads=[("bank", pb)], writes=[("tmp", pb)])
                P.add("vector", lambda e, sg=sg, pu=pu, ft=ft, lc=lc: e.tensor_tensor(out=K.act[:, ft, lc:lc + TBS], in0=pu[:, :], in1=sg, op=ALU.mult),
                      reads=[("bank", 2 + pb), ("tmp", pb)], writes=[("act", ft, tb)])
    for dg in range(DT // 2):
        s = K.cnt_d % 2
        K.cnt_d += 1
        P.dma("gpsimd", lambda e, s=s, dg=dg: e.dma_start(out=K.wd[:, s, :, :], in_=wd_v[:, :, dg * 256:(dg + 1) * 256]),
              ("wd", s), writes=[("wd", s)])
        for di in range(2):
            dt = dg * 2 + di
            for tb in range(NTB):
                lc = tb * TBS
                c0 = sb * SBT + lc
                pb = K.cnt_q % 2
                K.cnt_q += 1
                pd = K.bank[4 + pb]
                for ft in range(FT):
                    P.add("tensor", lambda e, ft=ft, s=s, lc=lc, pd=pd, di=di: e.matmul(pd[:, :], lhsT=K.wd[:, s, ft, di * 128:(di + 1) * 128], rhs=K.act[:, ft, lc:lc + TBS],
                                                                                      start=(ft == 0), stop=(ft == FT - 1)),
                          reads=[("wd", s), ("act", ft, tb)], writes=[("bank", 4 + pb)])
                P.add("vector", lambda e, dt=dt, c0=c0, pd=pd: e.scalar_tensor_tensor(out=K.x[:, dt, c0:c0 + TBS], in0=pd[:, :], scalar=0.5,
                                                                                     in1=K.x[:, dt, c0:c0 + TBS], op0=ALU.mult, op1=ALU.add),
                      reads=[("bank", 4 + pb), ("x", dt, sb, tb)], writes=[("x", dt, sb, tb)])


def emit_ta_tail(K, sb, gcol, w_if, hT_out, ifp_out):
    P = K.P
    emit_norm(K, sb, gcol, "mix")
    hv = hT_out.rearrange("(dt p) t -> p dt t", p=128)
    st_ops = []
    st_ops.append(P.dma("sync", lambda e: e.dma_start(out=hv[:, :, sb * SBT:(sb + 1) * SBT], in_=K.h[:, :, :]), ("hst", 0),
                        reads=[("h", dt, tb) for dt in range(DT) for tb in range(NTB)]))
    P.dma("gpsimd", lambda e: e.dma_start(out=K.wif[:, :, :], in_=w_if.rearrange("(dt p) c -> p dt c", p=128)), ("wif", 0), writes=["wif"])
    pif = K.bank[7]
    ntt = SBT // 128
    for tt in range(ntt):
        for dt in range(DT):
            P.add("tensor", lambda e, tt=tt, dt=dt: e.matmul(pif[:, tt * 8:(tt + 1) * 8], lhsT=K.h[:, dt, tt * 128:(tt + 1) * 128], rhs=K.wif[:, dt, :],
                                                             start=(dt == 0), stop=(dt == DT - 1)),
                  reads=[("h", dt, tt // 4), "wif"], writes=[("bank", 7)])
    P.add("vector", lambda e: e.tensor_copy(out=K.ifs[:, :], in_=pif[:, 0:ntt * 8]), reads=[("bank", 7)], writes=["ifs"])
    iv = ifp_out.rearrange("(tt p) c -> p tt c", p=128)
    st_ops.append(P.dma("sync", lambda e: e.dma_start(out=iv[:, sb * ntt:(sb + 1) * ntt, :], in_=K.ifs[:, :].rearrange("p (tt c) -> p tt c", c=8)),
                        ("ifst", 0), reads=["ifs"]))
    return st_ops


def emit_tb(K, sb, W):
    P = K.P
    c0s = sb * SBT
    P.dma("sync", lambda e: e.dma_start(out=K.h[:, :, :], in_=W.hT.rearrange("(dt p) t -> p dt t", p=128)[:, :, c0s:c0s + SBT]), ("hld", 0),
          writes=[("h", dt, tb) for dt in range(DT) for tb in range(NTB)])
    P.dma("sync", lambda e: e.dma_start(out=K.z[:, :, :], in_=W.zT.rearrange("(ct p) t -> p ct t", p=128)[:, :, c0s:c0s + SBT]), ("zld", 0),
          writes=["z"])
    P.dma("sync", lambda e: e.dma_start(out=K.yml[:, :, :], in_=W.ymlT.rearrange("(ct p) t -> p ct t", p=128)[:, :, c0s:c0s + SBT]), ("yld", 0),
          writes=["yml"])
    P.dma("gpsimd", lambda e: e.dma_start(out=K.wglu[:, 0, :, :], in_=W.glu_v.rearrange("(ct p) c -> p ct c", p=128)), ("wglu", 0), writes=["wglu0"])
    P.dma("gpsimd", lambda e: e.dma_start(out=K.wglu[:, 1, :, :], in_=W.glu_g.rearrange("(ct p) c -> p ct c", p=128)), ("wglu", 0), writes=["wglu1"])
    for co in range(4):
        for tb in range(NTB):
            lc = tb * TBS
            pb = K.cnt_p % 2
            K.cnt_p += 1
            pv, pg = K.bank[pb], K.bank[2 + pb]
            for ci in range(4):
                P.add("tensor", lambda e, ci=ci, co=co, lc=lc, pv=pv: e.matmul(pv[:, :], lhsT=K.wglu[:, 0, ci, co * 128:(co + 1) * 128], rhs=K.z[:, ci, lc:lc + TBS],
                                                                             start=(ci == 0), stop=(ci == 3)),
                      reads=["wglu0", "z"], writes=[("bank", pb)])
            for ci in range(4):
                P.add("tensor", lambda e, ci=ci, co=co, lc=lc, pg=pg: e.matmul(pg[:, :], lhsT=K.wglu[:, 1, ci, co * 128:(co + 1) * 128], rhs=K.z[:, ci, lc:lc + TBS],
                                                                             start=(ci == 0), stop=(ci == 3)),
                      reads=["wglu1", "z"], writes=[("bank", 2 + pb)])
            sg = K.tmp[:, pb, :]
            P.add("scalar", lambda e, sg=sg, pg=pg: e.activation(out=sg, in_=pg[:, :], func=AF.Sigmoid), reads=[("bank", 2 + pb)], writes=[("tmp", pb)])
            P.add("vector", lambda e, sg=sg, pv=pv, co=co, lc=lc: e.tensor_tensor(out=K.ys5[:, co, lc:lc + TBS], in0=pv[:, :], in1=sg, op=ALU.mult),
                  reads=[("bank", pb), ("tmp", pb)], writes=[("ys5", co, tb)])
    wbs_v = W.w_br_s5.rearrange("(ct p) d -> p ct d", p=128)
    wbm_v = W.w_br_ml.rearrange("(ct p) d -> p ct d", p=128)
    wg_v = W.w_g.rearrange("(ct p) d -> p ct d", p=128)
    for dt in range(DT):
        s = K.cnt_b % 2
        K.cnt_b += 1
        P.dma("gpsimd", lambda e, s=s, dt=dt: e.dma_start(out=K.wbr[:, s, 0:4, :], in_=wbs_v[:, :, dt * 128:(dt + 1) * 128]), ("wbr", s), writes=[("wbr0", s)])
        P.dma("gpsimd", lambda e, s=s, dt=dt: e.dma_start(out=K.wbr[:, s, 4:12, :], in_=wbm_v[:, :, dt * 128:(dt + 1) * 128]), ("wbr", s), writes=[("wbr1", s)])
        P.dma("gpsimd", lambda e, s=s, dt=dt: e.dma_start(out=K.wbr[:, s, 12:20, :], in_=wg_v[:, :, dt * 128:(dt + 1) * 128]), ("wbr", s), writes=[("wbr2", s)])
        P.dma("gpsimd", lambda e, s=s, dt=dt: e.dma_start(out=K.wbr[:, s, 20:28, :], in_=wg_v[:, :, D + dt * 128:D + (dt + 1) * 128]), ("wbr", s), writes=[("wbr3", s)])
        for tb in range(NTB):
            lc = tb * TBS
            b0, b1, b2, b3 = K.bank[0], K.bank[1], K.bank[2], K.bank[3]
            for ci in range(4):
                P.add("tensor", lambda e, ci=ci, s=s, lc=lc: e.matmul(b0[:, :], lhsT=K.wbr[:, s, ci, :], rhs=K.ys5[:, ci, lc:lc + TBS], start=(ci == 0), stop=(ci == 3)),
                      reads=[("wbr0", s), ("ys5", ci, tb)], writes=[("bank", 0)])
            for ci in range(8):
                P.add("tensor", lambda e, ci=ci, s=s, lc=lc: e.matmul(b1[:, :], lhsT=K.wbr[:, s, 4 + ci, :], rhs=K.yml[:, ci, lc:lc + TBS], start=(ci == 0), stop=(ci == 7)),
                      reads=[("wbr1", s), "yml"], writes=[("bank", 1)])
            for ci in range(8):
                P.add("tensor", lambda e, ci=ci, s=s, lc=lc: e.matmul(b2[:, :], lhsT=K.wbr[:, s, 12 + ci, :], rhs=K.h[:, ci, lc:lc + TBS], start=(ci == 0), stop=(ci == 7)),
                      reads=[("wbr2", s), ("h", ci, tb)], writes=[("bank", 2)])
            for ci in range(8):
                P.add("tensor", lambda e, ci=ci, s=s, lc=lc: e.matmul(b3[:, :], lhsT=K.wbr[:, s, 20 + ci, :], rhs=K.h[:, ci, lc:lc + TBS], start=(ci == 0), stop=(ci == 7)),
                      reads=[("wbr3", s), ("h", ci, tb)], writes=[("bank", 3)])
            P.add("scalar", lambda e: e.activation(out=K.tmp[:, 0, :], in_=b2[:, :], func=AF.Sigmoid), reads=[("bank", 2)], writes=[("tmp", 0)])
            P.add("scalar", lambda e: e.activation(out=K.tmp[:, 1, :], in_=b3[:, :], func=AF.Sigmoid), reads=[("bank", 3)], writes=[("tmp", 1)])
            P.add("vector", lambda e: e.tensor_tensor(out=K.tmp[:, 0, :], in0=b0[:, :], in1=K.tmp[:, 0, :], op=ALU.mult), reads=[("bank", 0), ("tmp", 0)], writes=[("tmp", 0)])
            P.add("vector", lambda e: e.tensor_tensor(out=K.tmp[:, 1, :], in0=b1[:, :], in1=K.tmp[:, 1, :], op=ALU.mult), reads=[("bank", 1), ("tmp", 1)], writes=[("tmp", 1)])
            P.add("vector", lambda e, dt=dt, lc=lc: e.tensor_tensor(out=K.mix[:, dt, lc:lc + TBS], in0=K.tmp[:, 0, :], in1=K.tmp[:, 1, :], op=ALU.add),
                  reads=[("tmp", 0), ("tmp", 1)], writes=[("mix", dt, tb)])
    wo_v = W.w_out.rearrange("(ct p) d -> p ct d", p=128)
    for dt in range(DT):
        s = K.cnt_b % 2
        K.cnt_b += 1
        P.dma("gpsimd", lambda e, s=s, dt=dt: e.dma_start(out=K.wbr[:, s, 0:8, :], in_=wo_v[:, :, dt * 128:(dt + 1) * 128]), ("wbr", s),
              writes=[("wbr0", s), ("wbr1", s)])
        for tb in range(NTB):
            lc = tb * TBS
            c0 = c0s + lc
            pb = K.cnt_q % 2
            K.cnt_q += 1
            pd = K.bank[4 + pb]
            for ci in range(8):
                P.add("tensor", lambda e, ci=ci, s=s, lc=lc, pd=pd: e.matmul(pd[:, :], lhsT=K.wbr[:, s, ci, :], rhs=K.mix[:, ci, lc:lc + TBS], start=(ci == 0), stop=(ci == 7)),
                      reads=[("wbr0", s), ("wbr1", s), ("mix", ci, tb)], writes=[("bank", 4 + pb)])
            P.add("vector", lambda e, dt=dt, c0=c0, pd=pd: e.tensor_tensor(out=K.x[:, dt, c0:c0 + TBS], in0=pd[:, :], in1=K.x[:, dt, c0:c0 + TBS], op=ALU.add),
                  reads=[("bank", 4 + pb), ("x", dt, sb, tb)], writes=[("x", dt, sb, tb)])


def emit_final(K, sb, gcol, outT):
    P = K.P
    emit_norm_f32(K, sb, gcol)
    ov = outT.rearrange("(dt p) t -> p dt t", p=128)
    return [P.dma("sync", lambda e: e.dma_start(out=ov[:, :, sb * SBT:(sb + 1) * SBT], in_=K.x[:, :, sb * SBT:(sb + 1) * SBT]), ("ost", 0),
                  reads=[("x", dt, sb, tb) for dt in range(DT) for tb in range(NTB)])]


def emit_norm_f32(K, sb, gcol):
    P = K.P
    for tb in range(NTB):
        c0 = sb * SBT + tb * TBS
        psn = K.bank[6]
        for dt in range(DT):
            sq = K.sq[:, dt % 2, :]
            P.add("scalar", lambda e, sq=sq, dt=dt, c0=c0: e.activation(out=sq, in_=K.x[:, dt, c0:c0 + TBS], func=AF.Square),
                  reads=[("x", dt, sb, tb)], writes=[("sq", dt % 2)])
            P.add("tensor", lambda e, sq=sq, dt=dt: e.matmul(psn[:, :], lhsT=K.ones32[:, :], rhs=sq, start=(dt == 0), stop=(dt == DT - 1)),
                  reads=[("sq", dt % 2), "consts"], writes=[("bank", 6)])
        P.add("vector", lambda e: e.tensor_scalar(out=K.rstd[:, :], in0=psn[:, :], scalar1=1.0 / D, scalar2=EPS, op0=ALU.mult, op1=ALU.add),
              reads=[("bank", 6)], writes=["rstd"])
        P.add("scalar", lambda e: e.activation(out=K.rstd[:, :], in_=K.rstd[:, :], func=AF.Sqrt), reads=["rstd"], writes=["rstd"])
        P.add("vector", lambda e: e.reciprocal(out=K.rstd[:, :], in_=K.rstd[:, :]), reads=["rstd"], writes=["rstd"])
        for dt in range(DT):
            eng = "vector"
            P.add(eng, lambda e, dt=dt, c0=c0: e.scalar_tensor_tensor(out=K.x[:, dt, c0:c0 + TBS], in0=K.x[:, dt, c0:c0 + TBS], scalar=gcol(dt),
                                                                     in1=K.rstd[:, :], op0=ALU.mult, op1=ALU.mult),
                  reads=[("x", dt, sb, tb), "rstd", "vecs"], writes=[("x", dt, sb, tb)])


def build_T(do_tb, do_ta, do_final):
    nc = bass.Bass("TRN2", target_bir_lowering=False)
    dr = lambda name, shape, dt, kind="ExternalInput": nc.dram_tensor(name, shape, dt, kind=kind).ap()
    W = Ctx()
    xin = dr("xin", [D, NTOK], F32)
    vecs_d = dr("vecs", [128, 40], F32)
    ones_d = dr("ones32", [128, 128], F32)
    if do_tb:
        W.hT = dr("hT_in", [D, NTOK], BF16)
        W.zT = dr("zT_in", [512, NTOK], BF16)
        W.ymlT = dr("ymlT_in", [D, NTOK], BF16)
        W.glu_v = dr("glu_v", [512, 512], F32)
        W.glu_g = dr("glu_g", [512, 512], F32)
        W.w_br_s5 = dr("w_br_s5", [512, D], F32)
        W.w_br_ml = dr("w_br_ml", [D, D], F32)
        W.w_g = dr("w_g", [D, 2 * D], F32)
        W.w_out = dr("w_out", [D, D], F32)
        f2 = (dr("f2_wg", [D, DFF], F32), dr("f2_wu", [D, DFF], F32), dr("f2_wd", [DFF, D], F32))
    if do_ta:
        f1 = (dr("f1_wg", [D, DFF], F32), dr("f1_wu", [D, DFF], F32), dr("f1_wd", [DFF, D], F32))
        w_if = dr("w_if", [D, 8], F32)
        hT_out = dr("hT_out", [D, NTOK], BF16, "ExternalOutput")
        ifp_out = dr("ifp_out", [NTOK, 8], F32, "ExternalOutput")
    xout = dr("xout", [D, NTOK], F32, "ExternalOutput")

    with contextlib.ExitStack() as st:
        K = Ctx()
        K.nc = nc
        K.P = P = Prog(nc)
        sb_t = lambda name, shape, dt: st.enter_context(nc.sbuf_tensor(name, shape, dt))
        K.x = sb_t("x", [128, DT, NTOK], F32)
        K.h = sb_t("h", [128, DT, SBT], BF16)
        K.act = sb_t("act", [128, 24, SBT], BF16)
        K.z = K.act[:, 0:4, :]
        K.ys5 = K.act[:, 4:8, :]
        K.yml = K.act[:, 8:16, :]
        K.mix = K.act[:, 16:24, :]
        K.sq = sb_t("sq", [128, 2, TBS], F32)
        K.tmp = sb_t("tmp", [128, 2, TBS], F32)
        K.rstd = sb_t("rstd", [128, TBS], F32)
        K.wgu = sb_t("wgu", [128, 2, 2, DT, 256], BF16)
        K.wd = sb_t("wd", [128, 2, FT, 256], BF16)
        K.wbr = sb_t("wbr", [128, 2, 28, 128], BF16)
        K.wglu = sb_t("wglu", [128, 2, 4, 512], BF16)
        K.wif = sb_t("wif", [128, DT, 8], BF16)
        K.ifs = sb_t("ifs", [128, (SBT // 128) * 8], F32)
        K.vecs = sb_t("vecs_s", [128, 40], F32)
        K.ones32 = sb_t("ones_s", [128, 128], F32)
        K.bank = [st.enter_context(nc.psum_tensor("bank%d" % i, [128, 512], F32)) for i in range(8)]
        K.cnt_w = K.cnt_p = K.cnt_d = K.cnt_q = K.cnt_b = 0

        P.dma("sync", lambda e: e.dma_start(out=K.vecs[:, :], in_=vecs_d), ("c", 0), writes=["vecs"])
        P.dma("sync", lambda e: e.dma_start(out=K.ones32[:, :], in_=ones_d), ("c", 1), writes=["consts"])
        xv = xin.rearrange("(dt p) t -> p dt t", p=128)
        for sb in range(NSB):
            P.dma("sync", lambda e, sb=sb: e.dma_start(out=K.x[:, :, sb * SBT:(sb + 1) * SBT], in_=xv[:, :, sb * SBT:(sb + 1) * SBT]), ("xld", sb),
                  writes=[("x", dt, sb, tb) for dt in range(DT) for tb in range(NTB)])
        finals = []
        for sb in range(NSB):
            if do_tb:
                P.barrier()
                emit_tb(K, sb, W)
                P.barrier()
                emit_ffn(K, sb, f2[0], f2[1], f2[2], lambda dt: K.vecs[:, dt:dt + 1], "f2")
            if do_ta:
                emit_ffn(K, sb, f1[0], f1[1], f1[2], lambda dt: K.vecs[:, 8 + dt:9 + dt], "f1")
                finals += emit_ta_tail(K, sb, lambda dt: K.vecs[:, 16 + dt:17 + dt], w_if, hT_out, ifp_out)
                xo = xout.rearrange("(dt p) t -> p dt t", p=128)
                finals.append(P.dma("sync", lambda e, sb=sb, xo=xo: e.dma_start(out=xo[:, :, sb * SBT:(sb + 1) * SBT], in_=K.x[:, :, sb * SBT:(sb + 1) * SBT]),
                                    ("ost", 0), reads=[("x", dt, sb, tb) for dt in range(DT) for tb in range(NTB)]))
            if do_final:
                finals += emit_final(K, sb, lambda dt: K.vecs[:, 24 + dt:25 + dt], xout)
        P.emit(final_waits=finals)
    return nc


def _cols(v):
    return np.ascontiguousarray(np.asarray(v, np.float32).reshape(8, 128).T)


_NC_CACHE = {}


def _get_T(do_tb, do_ta, do_final):
    return build_T(do_tb, do_ta, do_final)


def _run(nc, in_maps):
    res = run_bass_kernel_spmd(nc, in_maps, core_ids=list(range(8)))
    return res.results


def run_T(I, l, x_sh, mixer_out, do_tb, do_ta, do_final):
    ones = np.ones((128, 128), np.float32)
    in_maps = []
    lta = l + 1 if do_tb else 0
    for c in range(8):
        vecs = np.zeros((128, 40), np.float32)
        m = {"xin": x_sh[c], "ones32": ones}
        if do_tb:
            vecs[:, 0:8] = _cols(I["ffn2_norm"][l])
            hT, zT, ymlT = mixer_out
            m.update({"hT_in": hT[c], "zT_in": zT[c], "ymlT_in": ymlT[c],
                      "glu_v": I["s5_glu_v"][l], "glu_g": I["s5_glu_g"][l], "w_br_s5": I["w_br_s5"][l], "w_br_ml": I["w_br_ml"][l],
                      "w_g": np.ascontiguousarray(I["w_in"][l][:, 3592:]), "w_out": I["w_out"][l],
                      "f2_wg": I["ffn2_wg"][l], "f2_wu": I["ffn2_wu"][l], "f2_wd": I["ffn2_wd"][l]})
        if do_ta:
            vecs[:, 8:16] = _cols(I["ffn1_norm"][lta])
            vecs[:, 16:24] = _cols(I["mix_norm"][lta])
            m.update({"f1_wg": I["ffn1_wg"][lta], "f1_wu": I["ffn1_wu"][lta], "f1_wd": I["ffn1_wd"][lta],
                      "w_if": np.ascontiguousarray(I["w_in"][lta][:, 3584:3592])})
        if do_final:
            vecs[:, 24:32] = _cols(I["final_norm"])
        m["vecs"] = vecs
        in_maps.append(m)
    nc = _get_T(do_tb, do_ta, do_final)
    return _run(nc, in_maps)


def kernel(**inputs):
    I = {k: np.asarray(v) for k, v in inputs.items()}
    x = I["x"]
    x_sh = [np.ascontiguousarray(x[c // 4, (c % 4) * NTOK:(c % 4 + 1) * NTOK, :].T) for c in range(8)]
    res = run_T(I, 0, x_sh, None, False, True, False)
    for l in range(DEPTH):
        x_sh = [res[c]["xout"] for c in range(8)]
        hT_sh = [res[c]["hT_out"] for c in range(8)]
        ifp_sh = [res[c]["ifp_out"] for c in range(8)]
        mres = run_M(I, l, hT_sh, ifp_sh)
        if os.environ.get('KSTOP') == 'M0':
            return np.zeros((2, SEQ, D), np.float32)
        zT_in, ymlT_in = [], []
        for c in range(8):
            b, r = c // 4, c % 4
            sl = slice(r * NTOK, (r + 1) * NTOK)
            zT_in.append(np.ascontiguousarray(np.concatenate([mres[4 * b + q]["zT"][:, sl] for q in range(4)], axis=0)))
            ymlT_in.append(np.ascontiguousarray(np.concatenate([mres[4 * b + q]["ymlT"][:, sl] for q in range(4)], axis=0)))
        last = (l == DEPTH - 1)
        res = run_T(I, l, x_sh, (hT_sh, zT_in, ymlT_in), True, not last, last)
    out = np.zeros((2, SEQ, D), np.float32)
    for c in range(8):
        out[c // 4, (c % 4) * NTOK:(c % 4 + 1) * NTOK, :] = res[c]["xout"].T
    return out


NBLK = SEQ // TBS
NCH = SEQ // 128
import os
MLEVEL = int(os.environ.get('MLEVEL', '9'))
MSUB = int(os.environ.get('MSUB', '9'))


def build_M():
    nc = bass.Bass("TRN2", target_bir_lowering=False)
    dr = lambda name, shape, dt, kind="ExternalInput": nc.dram_tensor(name, shape, dt, kind=kind).ap()
    hT = dr("hT", [D, SEQ], BF16)
    ifg_d = dr("ifg", [128, NCH, 2], F32)
    w_m = dr("w_m", [D, 896], F32)
    wq_d = dr("wq", [256, 256], F32)
    wk_d = dr("wk", [256, 256], F32)
    convw_d = dr("convw", [128, 2, 4], F32)
    vm_d = dr("vm", [128, 16], F32)
    lamC_d = dr("lamC", [128, 12], F32)
    lamR_d = dr("lamR", [128, 3, 512], F32)
    BT_d = dr("BT", [128, 2, 512], F32)
    CT_d = dr("CT", [128, 2, 4, 128], F32)
    cf_d = dr("cf32", [128, 4, 128], F32)
    cb_d = dr("cb16", [128, 128], BF16)
    iota_d = dr("iota", [128, 512], F32)
    zT_o = dr("zT", [128, SEQ], BF16, "ExternalOutput")
    ymlT_o = dr("ymlT", [256, SEQ], BF16, "ExternalOutput")

    with contextlib.ExitStack() as st:
        P = Prog(nc)
        T = lambda name, shape, dt: st.enter_context(nc.sbuf_tensor(name, shape, dt))
        wm = T("wm", [128, DT, 896], BF16)
        wq = T("wq_s", [128, 2, 256], BF16)
        wk = T("wk_s", [128, 2, 256], BF16)
        hblk = T("hblk", [128, 2, DT, TBS], BF16)
        convw = T("convw_s", [128, 2, 4], F32)
        vm = T("vm_s", [128, 16], F32)
        lamC = T("lamC_s", [128, 12], F32)
        cf = T("cf_s", [128, 4, 128], F32)
        cb = T("cb_s", [128, 128], BF16)
        iota = T("iota_s", [128, 512], F32)
        R = T("R", [128, 12, 512], F32)
        CTf = T("CTf", [128, 2, 4, 128], F32)
        CTb = T("CTb", [128, 2, 4, 128], BF16)
        BbT = T("BbT", [128, 2, 512], BF16)
        cosT = T("cosT", [128, 4, 512], F32)
        sinT = T("sinT", [128, 4, 512], F32)
        rT = T("rT", [128, 4, 512], F32)
        cs = T("cs", [128, 32], F32)
        ubf = T("ubf", [128, 2, TBS], BF16)
        du = T("du", [128, 2, TBS], F32)
        t1 = T("t1", [128, 2, TBS], F32)
        t2 = T("t2", [128, 2, TBS], F32)
        zin = T("zin", [128, 2, TBS], F32)
        zr = T("zr", [128, 4, TBS], F32)
        zi = T("zi", [128, 4, TBS], F32)
        init = T("init", [128, 4, 4], F32)
        pt = T("pt", [128, 4, TBS], F32)
        sre = T("sre", [128, 2, TBS], BF16)
        sim_ = T("sim", [128, 2, TBS], BF16)
        yy = T("yy", [128, TBS], F32)
        g1 = T("g1", [128, TBS], F32)
        g2 = g1
        zout = T("zout", [128, 2, TBS], BF16)
        xpad = T("xpad", [128, 2, TBS + 3], F32)
        cacc = T("cacc", [128, 2, TBS], F32)
        xc32 = T("xc32", [128, 2, TBS], F32)
        xcb = T("xcb", [128, 2, 2, TBS], BF16)
        skx = T("skx", [128, 2, 2, TBS], F32)
        qT = T("qT", [128, 2, 2, TBS], BF16)
        kT = T("kT", [128, 2, 2, TBS], BF16)
        kp = T("kp", [128, 4, 256], BF16)
        vext = T("vext", [128, 4, 257], BF16)
        og = T("og", [128, 4, 256], F32)
        Sm = T("Sm", [128, 2, 128], BF16)
        hc4 = T("hc4", [128, 4, 256], F32)
        ssq = T("ssq", [128, 8], F32)
        hsq = T("hsq", [128, 256], F32)
        hn = T("hn", [128, 2, 256], BF16)
        sm = T("sm", [128, 8], F32)
        Cst = T("Cst", [128, 2, 256], F32)
        nst = T("nst", [128, 2], F32)
        Cwb = T("Cwb", [128, 2, 257], BF16)
        ymlo = T("ymlo", [128, 2, 2, TBS], BF16)
        ifs = T("ifs", [128, NCH, 2], F32)
        gsc = T("gsc", [128, 8, NCH], F32)
        grow = T("grow", [128, 4, NCH + 1], F32)
        abc = T("abc", [64, 128], F32)
        acol = T("acol", [64, 1], F32)
        bank = [st.enter_context(nc.psum_tensor("bank%d" % i, [128, 512], F32)) for i in range(7)]
        bankT = st.enter_context(nc.psum_tensor("bankT", [128, 1024], BF16))

        ident = cf[:, 0, :]
        tri = cf[:, 1, :]
        ones = cf[:, 2, :]

        ld = lambda eng, out, in_, key, tok: P.dma(eng, lambda e: e.dma_start(out=out, in_=in_), key, writes=[tok])
        wmv = w_m.rearrange("(dt p) c -> p dt c", p=128)
        for k in range(7):
            P.dma("gpsimd", lambda e, k=k: e.dma_start(out=wm[:, :, k * 128:(k + 1) * 128], in_=wmv[:, :, k * 128:(k + 1) * 128]), ("w", 0), writes=["wm"])
        for k in range(2):
            P.dma("gpsimd", lambda e, k=k: e.dma_start(out=wq[:, :, k * 128:(k + 1) * 128], in_=wq_d.rearrange("(dt p) c -> p dt c", p=128)[:, :, k * 128:(k + 1) * 128]),
                  ("w", 1), writes=["wq"])
            P.dma("gpsimd", lambda e, k=k: e.dma_start(out=wk[:, :, k * 128:(k + 1) * 128], in_=wk_d.rearrange("(dt p) c -> p dt c", p=128)[:, :, k * 128:(k + 1) * 128]),
                  ("w", 2), writes=["wk"])
        for k in range(4):
            P.dma("gpsimd", lambda e, k=k: e.dma_start(out=CTb[:, 0, k, :], in_=CT_d[:, 0, k, :]), ("w", 3), writes=["CTb0"])
        ld("sync", CTf[:, :, :, :], CT_d, ("w", 4), "CTf")
        ld("sync", convw[:, :, :], convw_d, ("w", 5), "convw")
        ld("sync", vm[:, :], vm_d, ("w", 6), "vm")
        ld("sync", lamC[:, :], lamC_d, ("w", 7), "lamC")
        ld("sync", cf[:, :, :], cf_d, ("w", 8), "cf")
        ld("sync", cb[:, :], cb_d, ("w", 9), "cb")
        ld("sync", iota[:, :], iota_d, ("w", 10), "iota")
        ld("sync", R[:, 0:3, :], lamR_d, ("w", 11), "R012")
        ld("sync", R[:, 8:10, :], BT_d, ("w", 12), "R89")
        ld("sync", ifs[:, :, :], ifg_d, ("w", 13), "ifs")

        V = lambda fn, r=(), w=(): P.add("vector", fn, reads=r, writes=w)
        A = lambda fn, r=(), w=(): P.add("scalar", fn, reads=r, writes=w)
        G = lambda fn, r=(), w=(): P.add("vector" if os.environ.get("POOL2V") else "gpsimd", fn, reads=r, writes=w)
        PE = lambda fn, r=(), w=(): P.add("tensor", fn, reads=r, writes=w)

        RI = T("RI", [128, 512], mybir.dt.int32)

        def sincos(out_s, out_c, ang, tmpa, tmpb, rtoks, wtoks, ttoks):
            n = tmpa.shape[-1]
            it = RI[:, 0:n]
            ta, tb = ttoks
            V(lambda e: e.tensor_scalar(out=tmpa, in0=ang, scalar1=1.0 / TWO_PI, scalar2=None, op0=ALU.mult), rtoks, [ta])
            for k, (o_ap, wt) in enumerate(((out_s, wtoks[0]), (out_c, wtoks[1]))):
                if k == 1:
                    V(lambda e: e.tensor_scalar(out=tmpa, in0=tmpa, scalar1=0.25, scalar2=None, op0=ALU.add), [ta], [ta])
                V(lambda e: e.tensor_copy(out=it, in_=tmpa), [ta], ["RI"])
                V(lambda e: e.tensor_copy(out=tmpb, in_=it), ["RI"], [tb])
                V(lambda e: e.tensor_tensor(out=tmpb, in0=tmpa, in1=tmpb, op=ALU.subtract), [ta, tb], [tb])
                A(lambda e, o_ap=o_ap: e.activation(out=o_ap, in_=tmpb, func=AF.Sin, scale=TWO_PI), [tb], [wt])

        V(lambda e: e.memset(zr[:, :, :], 0.0), [], ["zr_all"])
        A(lambda e: e.activation(out=cs[:, 0:4], in_=lamC[:, 8:12], func=AF.Exp), ["lamC"], ["cs_dt"])
        V(lambda e: e.tensor_scalar(out=cs[:, 4:8], in0=lamC[:, 0:4], scalar1=-1e-4, scalar2=None, op0=ALU.min), ["lamC"], ["cs_lr"])
        V(lambda e: e.tensor_tensor(out=cs[:, 28:32], in0=cs[:, 4:8], in1=cs[:, 0:4], op=ALU.mult), ["cs_lr", "cs_dt"], ["cs_tmp"])
        A(lambda e: e.activation(out=cs[:, 8:12], in_=cs[:, 28:32], func=AF.Exp), ["cs_tmp"], ["cs_mag"])
        V(lambda e: e.tensor_tensor(out=cs[:, 12:16], in0=lamC[:, 4:8], in1=cs[:, 0:4], op=ALU.mult), ["lamC", "cs_dt"], ["cs_th"])
        for j in range(4):
            V(lambda e, j=j: e.tensor_scalar(out=R[:, 10, :], in0=iota[:, :], scalar1=cs[:, 12 + j:13 + j], scalar2=None, op0=ALU.mult), ["iota", "cs_th"], ["R10"])
            sincos(sinT[:, j, :], cosT[:, j, :], R[:, 10, :], R[:, 11, :], R[:, 3, :], ["R10"], [("sinT", j), ("cosT", j)], ["R11", "R3"])
            V(lambda e, j=j: e.memset(rT[:, j, :], 1.0), [], [("rT", j)])
            V(lambda e, j=j: e.tensor_scalar(out=rT[:, j, :], in0=rT[:, j, :], scalar1=cs[:, 8 + j:9 + j], scalar2=None, op0=ALU.mult),
              [("rT", j), "cs_mag"], [("rT", j)])
        V(lambda e: e.tensor_scalar(out=cs[:, 28:32], in0=cs[:, 12:16], scalar1=float(TBS), scalar2=None, op0=ALU.mult), ["cs_th", "cs_tmp"], ["cs_tmp"])
        sincos(cs[:, 20:24], cs[:, 16:20], cs[:, 28:32], R[:, 11, 0:4], R[:, 3, 0:4], ["cs_tmp"], ["cs_Es", "cs_Ec"], ["R11", "R3"])
        V(lambda e: e.tensor_scalar(out=cs[:, 24:28], in0=cs[:, 20:24], scalar1=-1.0, scalar2=None, op0=ALU.mult), ["cs_Es"], ["cs_nEs"])
        A(lambda e: e.activation(out=R[:, 2, :], in_=R[:, 2, :], func=AF.Exp), ["R012"], ["R2"])
        V(lambda e: e.tensor_scalar(out=R[:, 0, :], in0=R[:, 0, :], scalar1=-1e-4, scalar2=None, op0=ALU.min), ["R012"], ["R0"])
        V(lambda e: e.tensor_tensor(out=R[:, 4, :], in0=R[:, 0, :], in1=R[:, 2, :], op=ALU.mult), ["R0", "R2"], ["R4"])
        A(lambda e: e.activation(out=R[:, 4, :], in_=R[:, 4, :], func=AF.Exp), ["R4"], ["R4"])
        V(lambda e: e.tensor_tensor(out=R[:, 5, :], in0=R[:, 1, :], in1=R[:, 2, :], op=ALU.mult), ["R012", "R2"], ["R5"])
        sincos(R[:, 6, :], R[:, 7, :], R[:, 5, :], R[:, 11, :], R[:, 3, :], ["R5"], ["R6", "R7"], ["R11", "R3"])
        V(lambda e: e.tensor_tensor(out=R[:, 6, :], in0=R[:, 6, :], in1=R[:, 4, :], op=ALU.mult), ["R6", "R4"], ["R6"])
        V(lambda e: e.tensor_tensor(out=R[:, 7, :], in0=R[:, 7, :], in1=R[:, 4, :], op=ALU.mult), ["R7", "R4"], ["R7"])
        V(lambda e: e.tensor_scalar(out=R[:, 7, :], in0=R[:, 7, :], scalar1=-1.0, scalar2=None, op0=ALU.add), ["R7"], ["R7"])
        V(lambda e: e.tensor_tensor(out=R[:, 4, :], in0=R[:, 0, :], in1=R[:, 0, :], op=ALU.mult), ["R0", "R4"], ["R4"])
        V(lambda e: e.tensor_tensor(out=R[:, 5, :], in0=R[:, 1, :], in1=R[:, 1, :], op=ALU.mult), ["R012", "R5"], ["R5"])
        V(lambda e: e.tensor_tensor(out=R[:, 4, :], in0=R[:, 4, :], in1=R[:, 5, :], op=ALU.add), ["R4", "R5"], ["R4"])
        V(lambda e: e.reciprocal(out=R[:, 4, :], in_=R[:, 4, :]), ["R4"], ["R4"])
        V(lambda e: e.tensor_tensor(out=R[:, 5, :], in0=R[:, 7, :], in1=R[:, 0, :], op=ALU.mult), ["R7", "R0", "R5"], ["R5"])
        V(lambda e: e.tensor_tensor(out=R[:, 11, :], in0=R[:, 6, :], in1=R[:, 1, :], op=ALU.mult), ["R6", "R012", "R11"], ["R11"])
        V(lambda e: e.tensor_tensor(out=R[:, 5, :], in0=R[:, 5, :], in1=R[:, 11, :], op=ALU.add), ["R5", "R11"], ["R5"])
        V(lambda e: e.tensor_tensor(out=R[:, 5, :], in0=R[:, 5, :], in1=R[:, 4, :], op=ALU.mult), ["R5", "R4"], ["R5"])
        V(lambda e: e.tensor_tensor(out=R[:, 11, :], in0=R[:, 6, :], in1=R[:, 0, :], op=ALU.mult), ["R6", "R0", "R11"], ["R11"])
        V(lambda e: e.tensor_tensor(out=R[:, 3, :], in0=R[:, 7, :], in1=R[:, 1, :], op=ALU.mult), ["R7", "R012", "R3"], ["R3"])
        V(lambda e: e.tensor_tensor(out=R[:, 11, :], in0=R[:, 11, :], in1=R[:, 3, :], op=ALU.subtract), ["R11", "R3"], ["R11"])
        V(lambda e: e.tensor_tensor(out=R[:, 11, :], in0=R[:, 11, :], in1=R[:, 4, :], op=ALU.mult), ["R11", "R4"], ["R11"])
        V(lambda e: e.tensor_tensor(out=R[:, 3, :], in0=R[:, 5, :], in1=R[:, 8, :], op=ALU.mult), ["R5", "R89", "R3"], ["R3"])
        V(lambda e: e.tensor_tensor(out=R[:, 6, :], in0=R[:, 11, :], in1=R[:, 9, :], op=ALU.mult), ["R11", "R89", "R6"], ["R6"])
        V(lambda e: e.tensor_tensor(out=BbT[:, 0, :], in0=R[:, 3, :], in1=R[:, 6, :], op=ALU.subtract), ["R3", "R6"], ["BbT0"])
        V(lambda e: e.tensor_tensor(out=R[:, 3, :], in0=R[:, 5, :], in1=R[:, 9, :], op=ALU.mult), ["R5", "R89", "R3"], ["R3"])
        V(lambda e: e.tensor_tensor(out=R[:, 6, :], in0=R[:, 11, :], in1=R[:, 8, :], op=ALU.mult), ["R11", "R89", "R6"], ["R6"])
        V(lambda e: e.tensor_tensor(out=BbT[:, 1, :], in0=R[:, 3, :], in1=R[:, 6, :], op=ALU.add), ["R3", "R6"], ["BbT1"])
        A(lambda e: e.activation(out=CTb[:, 1, :, :], in_=CTf[:, 1, :, :], func=AF.Copy, scale=-1.0), ["CTf"], ["CTb1"])

        if MLEVEL == 0:
            P.emit()
            return nc
        PE_real = PE
        if os.environ.get('MSKIPG'):
            PE = lambda fn, r=(), w=(): None
        V(lambda e: e.tensor_scalar(out=gsc[:, 7, :], in0=ifs[:, :, 1], scalar1=vm[:, 9:10], scalar2=None, op0=ALU.add), ["ifs", "vm"], ["g7"])
        A(lambda e: e.activation(out=gsc[:, 7, :], in_=gsc[:, 7, :], func=AF.Exp, scale=-1.0), ["g7"], ["g7"])
        A(lambda e: e.activation(out=gsc[:, 0, :], in_=gsc[:, 7, :], func=AF.Ln, bias=1.0), ["g7"], ["g0"])
        V(lambda e: e.tensor_scalar(out=gsc[:, 0, :], in0=gsc[:, 0, :], scalar1=-1.0, scalar2=None, op0=ALU.mult), ["g0"], ["g0"])
        PE(lambda e: e.matmul(bank[0][:, 0:NCH], lhsT=tri, rhs=gsc[:, 0, :], start=True, stop=True), ["cf", "g0"], [("bank", 0)])
        V(lambda e: e.tensor_copy(out=gsc[:, 1, :], in_=bank[0][:, 0:NCH]), [("bank", 0)], ["g1"])
        V(lambda e: e.scalar_tensor_tensor(out=gsc[:, 2, :], in0=ifs[:, :, 0], scalar=vm[:, 8:9], in1=gsc[:, 1, :], op0=ALU.add, op1=ALU.subtract),
          ["ifs", "vm", "g1"], ["g2"])
        PE(lambda e: e.transpose(out=bank[1][0:NCH, 0:128], in_=gsc[:, 2, :], identity=ident), ["g2", "cf"], [("bank", 1)])
        V(lambda e: e.tensor_reduce(out=acol[:, :], in_=bank[1][0:NCH, 0:128], axis=AX.X, op=ALU.max), [("bank", 1)], ["acol"])
        V(lambda e: e.tensor_scalar(out=abc[:, :], in0=cf[0:64, 2, :], scalar1=acol[:, 0:1], scalar2=None, op0=ALU.mult), ["acol", "cf"], ["abc"])
        PE(lambda e: e.matmul(bank[2][:, 0:NCH], lhsT=abc[:, :], rhs=cf[0:64, 0, 0:64], start=True, stop=True), ["abc", "cf"], [("bank", 2)])
        PE(lambda e: e.matmul(bank[3][:, 0:NCH], lhsT=cf[:, 3, :], rhs=gsc[:, 1, :], start=True, stop=True), ["g1", "cf"], [("bank", 3)])
        V(lambda e: e.tensor_copy(out=grow[:, 1, 0:NCH], in_=bank[2][:, 0:NCH]), [("bank", 2)], ["gr1"])
        V(lambda e: e.tensor_copy(out=grow[:, 0, 0:NCH], in_=bank[3][:, 0:NCH]), [("bank", 3)], ["gr0"])
        V(lambda e: e.tensor_tensor(out=grow[:, 2, 0:NCH], in0=grow[:, 1, 0:NCH], in1=grow[:, 0, 0:NCH], op=ALU.add), ["gr0", "gr1"], ["gr2"])
        V(lambda e: e.memset(grow[:, 3, 0:1], 0.0), [], ["gr3a"])
        V(lambda e: e.tensor_tensor_scan(out=grow[:, 3, 1:NCH + 1], data0=grow[:, 0, 0:NCH], data1=grow[:, 2, 0:NCH], initial=0.0, op0=ALU.add, op1=ALU.max),
          ["gr0", "gr2", "gr3a"], ["gr3"])
        V(lambda e: e.tensor_tensor(out=gsc[:, 5, :], in0=grow[:, 3, 1:NCH + 1], in1=grow[:, 0, 0:NCH], op=ALU.subtract), ["gr3", "gr0"], ["g5"])
        V(lambda e: e.tensor_tensor(out=gsc[:, 6, :], in0=grow[:, 3, 0:NCH], in1=gsc[:, 5, :], op=ALU.subtract), ["gr3", "gr3a", "g5"], ["g6"])
        A(lambda e: e.activation(out=gsc[:, 6, :], in_=gsc[:, 6, :], func=AF.Exp), ["g6"], ["g6"])
        V(lambda e: e.tensor_tensor(out=gsc[:, 3, :], in0=gsc[:, 2, :], in1=gsc[:, 5, :], op=ALU.subtract), ["g2", "g5"], ["g3"])
        A(lambda e: e.activation(out=gsc[:, 3, :], in_=gsc[:, 3, :], func=AF.Exp), ["g3"], ["g3"])
        V(lambda e: e.tensor_tensor(out=gsc[:, 4, :], in0=gsc[:, 1, :], in1=gsc[:, 5, :], op=ALU.add), ["g1", "g5"], ["g4"])
        A(lambda e: e.activation(out=gsc[:, 4, :], in_=gsc[:, 4, :], func=AF.Exp, scale=-1.0), ["g4"], ["g4"])

        if MLEVEL == 1:
            P.emit()
            return nc
        PE = PE_real
        if MSUB == -1:
            V = lambda fn, r=(), w=(): None
        V(lambda e: e.memset(Cst[:, :, :], 0.0), [], ["Cst"])
        V(lambda e: e.memset(nst[:, :], 0.0), [], ["nst"])
        V(lambda e: e.memset(Cwb[:, :, :], 0.0), [], ["Cwb"])
        V(lambda e: e.memset(xpad[:, :, 0:3], 0.0), [], [("xpadh", 0), ("xpadh", 1)])
        V(lambda e: e.memset(vext[:, :, 256:257], 1.0), [], ["vones"])

        V = lambda fn, r=(), w=(): P.add("vector", fn, reads=r, writes=w)
        hv = hT.rearrange("(dt p) t -> p dt t", p=128)
        yv = ymlT_o.rearrange("(e p) t -> p e t", p=128)
        finals = []
        def front(bi, part):
            c0 = bi * TBS
            s = bi % 2
            HB = ("hblk", s)
            b0 = bank[0]
            if part == 0:
                P.dma("sync", lambda e: e.dma_start(out=hblk[:, s, :, :], in_=hv[:, :, c0:c0 + TBS]), ("hblk", s), writes=[("hblk", s)])
                for dt in range(DT):
                    PE(lambda e, dt=dt: e.matmul(b0[:, :], lhsT=wm[:, dt, 0:128], rhs=hblk[:, s, dt, :], start=(dt == 0), stop=(dt == DT - 1)),
                       ["wm", HB], [("bank", 0)])
                A(lambda e: e.activation(out=ubf[:, s, :], in_=b0[:, :], func=AF.Copy), [("bank", 0)], [("ubf", s)])
                V(lambda e: e.tensor_scalar(out=du[:, s, :], in0=b0[:, :], scalar1=vm[:, 6:7], scalar2=None, op0=ALU.mult), [("bank", 0), "vm"], [("du", s)])
                for eo in range(2):
                    for dt in range(DT):
                        PE(lambda e, dt=dt, eo=eo: e.matmul(b0[:, :], lhsT=wm[:, dt, 128 + eo * 128:256 + eo * 128], rhs=hblk[:, s, dt, :],
                                                            start=(dt == 0), stop=(dt == DT - 1)), ["wm", HB], [("bank", 0)])
                    A(lambda e, eo=eo: e.activation(out=xpad[:, eo, 3:3 + TBS], in_=b0[:, :], func=AF.Copy), [("bank", 0)], [("xpad", eo)])
            elif part == 1:
                for eo in range(2):
                    XR = [("xpad", eo), ("xpadh", eo)]
                    V(lambda e, eo=eo: e.tensor_scalar(out=cacc[:, eo, :], in0=xpad[:, eo, 0:TBS], scalar1=convw[:, eo, 0:1], scalar2=vm[:, eo:eo + 1],
                                                       op0=ALU.mult, op1=ALU.add), XR + ["convw", "vm"], [("cacc", eo)])
                    for j in range(1, 4):
                        V(lambda e, eo=eo, j=j: e.scalar_tensor_tensor(out=cacc[:, eo, :], in0=xpad[:, eo, j:j + TBS], scalar=convw[:, eo, j:j + 1],
                                                                       in1=cacc[:, eo, :], op0=ALU.mult, op1=ALU.add), XR + [("cacc", eo), "convw"], [("cacc", eo)])
                    G(lambda e, eo=eo: e.tensor_copy(out=xpad[:, eo, 0:3], in_=xpad[:, eo, TBS:TBS + 3]), XR, [("xpadh", eo)])
                    A(lambda e, eo=eo: e.activation(out=xc32[:, eo, :], in_=cacc[:, eo, :], func=AF.Sigmoid), [("cacc", eo)], [("xc32", eo)])
                    V(lambda e, eo=eo: e.tensor_tensor(out=xc32[:, eo, :], in0=xc32[:, eo, :], in1=cacc[:, eo, :], op=ALU.mult), [("xc32", eo), ("cacc", eo)], [("xc32", eo)])
                    A(lambda e, eo=eo: e.activation(out=xcb[:, s, eo, :], in_=xc32[:, eo, :], func=AF.Copy), [("xc32", eo)], [("xcb", s, eo)])
                    A(lambda e, eo=eo: e.activation(out=skx[:, s, eo, :], in_=xc32[:, eo, :], func=AF.Copy, scale=vm[:, 4 + eo:5 + eo]), [("xc32", eo), "vm"], [("skx", s, eo)])
            else:
                for eo in range(2):
                    for dd in range(2):
                        PE(lambda e, eo=eo, dd=dd: e.matmul(b0[:, :], lhsT=wq[:, dd, eo * 128:(eo + 1) * 128], rhs=xcb[:, s, dd, :], start=(dd == 0), stop=(dd == 1)),
                           ["wq", ("xcb", s, dd)], [("bank", 0)])
                    A(lambda e, eo=eo: e.activation(out=qT[:, s, eo, :], in_=b0[:, :], func=AF.Copy, scale=1.0 / 16.0), [("bank", 0)], [("qT", s, eo)])
                    for dd in range(2):
                        PE(lambda e, eo=eo, dd=dd: e.matmul(b0[:, :], lhsT=wk[:, dd, eo * 128:(eo + 1) * 128], rhs=xcb[:, s, dd, :], start=(dd == 0), stop=(dd == 1)),
                           ["wk", ("xcb", s, dd)], [("bank", 0)])
                    A(lambda e, eo=eo: e.activation(out=kT[:, s, eo, :], in_=b0[:, :], func=AF.Copy), [("bank", 0)], [("kT", s, eo)])

        nblk = NBLK if MLEVEL >= 5 else (2 if MLEVEL == 4 else 1)
        for part in range(3):
            front(0, part)
        for bi in range(nblk):
            c0 = bi * TBS
            s = bi % 2
            HB = ("hblk", s)
            b0 = bank[0]
            for i in range(4):
                j = i
                ch = i
                gch = bi * 4 + ch
                cc = ch * 128
                ss = gch % 2
                ZR, ZI = ("zr", j), ("zi", j)
                sj = j % 2
                PE(lambda e, j=j, s=s: e.matmul(bank[2][:, :], lhsT=BbT[:, 0, j * 128:(j + 1) * 128], rhs=ubf[:, s, :], start=True, stop=True), ["BbT0", ("ubf", s)], [("bank", 2)])
                PE(lambda e, j=j, s=s: e.matmul(bank[3][:, :], lhsT=BbT[:, 1, j * 128:(j + 1) * 128], rhs=ubf[:, s, :], start=True, stop=True), ["BbT1", ("ubf", s)], [("bank", 3)])
                for dd in range(2):
                    PE(lambda e, dd=dd, cc=cc, s=s: e.matmul(bank[1][:, 256:512], lhsT=xcb[:, s, dd, cc:cc + 128], rhs=wk[:, dd, :], start=(dd == 0), stop=(dd == 1)),
                       ["wk", ("xcb", s, dd)], [("bank", 1, "k")])
                for dt in range(DT):
                    PE(lambda e, dt=dt, s=s, cc=cc: e.matmul(b0[:, :], lhsT=hblk[:, s, dt, cc:cc + 128], rhs=wm[:, dt, 384:896], start=(dt == 0), stop=(dt == DT - 1)),
                       ["wm", HB], [("bank", 0)])
                for dd in range(2):
                    PE(lambda e, dd=dd, cc=cc, s=s: e.matmul(bank[5][:, 0:128], lhsT=kT[:, s, dd, cc:cc + 128], rhs=qT[:, s, dd, cc:cc + 128], start=(dd == 0), stop=(dd == 1)),
                       [("kT", s, dd), ("qT", s, dd)], [("bank", 5, "s")])
                V(lambda e, j=j: e.tensor_tensor(out=t1[:, 0, :], in0=bank[2][:, :], in1=cosT[:, j, :], op=ALU.mult), [("bank", 2), ("cosT", j)], [("t1", 0)])
                V(lambda e, j=j: e.tensor_tensor(out=t2[:, 0, :], in0=bank[3][:, :], in1=sinT[:, j, :], op=ALU.mult), [("bank", 3), ("sinT", j)], [("t2", 0)])
                G(lambda e: e.tensor_tensor(out=zin[:, 0, :], in0=t1[:, 0, :], in1=t2[:, 0, :], op=ALU.add), [("t1", 0), ("t2", 0)], [("zin", 0)])
                V(lambda e, j=j: e.tensor_tensor(out=t1[:, 1, :], in0=bank[3][:, :], in1=cosT[:, j, :], op=ALU.mult), [("bank", 3), ("cosT", j)], [("t1", 1)])
                V(lambda e, j=j: e.tensor_tensor(out=t2[:, 1, :], in0=bank[2][:, :], in1=sinT[:, j, :], op=ALU.mult), [("bank", 2), ("sinT", j)], [("t2", 1)])
                G(lambda e: e.tensor_tensor(out=zin[:, 1, :], in0=t1[:, 1, :], in1=t2[:, 1, :], op=ALU.subtract), [("t1", 1), ("t2", 1)], [("zin", 1)])
                V(lambda e, ch=ch, gch=gch: e.tensor_scalar(out=kp[:, ch, :], in0=bank[1][:, 256:512], scalar1=gsc[:, 3, gch:gch + 1], scalar2=None, op0=ALU.mult),
                  [("bank", 1, "k"), "g3"], [("kp", ch)])
                V(lambda e, ch=ch: e.tensor_copy(out=vext[:, ch, 0:256], in_=b0[:, 0:256]), [("bank", 0)], [("vext", ch)])
                A(lambda e, ch=ch: e.activation(out=og[:, ch, :], in_=b0[:, 256:512], func=AF.Sigmoid), [("bank", 0)], [("og", ch)])
                V(lambda e, ss=ss, gch=gch: e.scalar_tensor_tensor(out=Sm[:, ss, :], in0=bank[5][:, 0:128], scalar=gsc[:, 3, gch:gch + 1], in1=tri,
                                                                   op0=ALU.mult, op1=ALU.mult), [("bank", 5, "s"), "g3", "cf"], [("Sm", ss)])
                PE(lambda e, ss=ss, ch=ch: e.matmul(bank[6][:, 0:257], lhsT=Sm[:, ss, :], rhs=vext[:, ch, :], start=True, stop=False),
                   [("Sm", ss), ("vext", ch), "vones"], [("bank", 6)])
                for dd in range(2):
                    PE(lambda e, dd=dd, cc=cc, s=s: e.matmul(bank[6][:, 0:257], lhsT=qT[:, s, dd, cc:cc + 128], rhs=Cwb[:, dd, :], start=False, stop=(dd == 1)),
                       [("qT", s, dd), "Cwb"], [("bank", 6)])
                if bi == 0:
                    if j == 0:
                        V(lambda e: e.memset(init[:, :, :], 0.0), [], [("init", jj) for jj in range(4)])
                else:
                    V(lambda e, j=j: e.tensor_scalar(out=init[:, j, 2:3], in0=zr[:, j, TBS - 1:TBS], scalar1=cs[:, 16 + j:17 + j], scalar2=None, op0=ALU.mult),
                      [ZR, "cs_Ec"], [("initt", j)])
                    V(lambda e, j=j: e.scalar_tensor_tensor(out=init[:, j, 0:1], in0=zi[:, j, TBS - 1:TBS], scalar=cs[:, 24 + j:25 + j], in1=init[:, j, 2:3],
                                                            op0=ALU.mult, op1=ALU.add), [ZI, "cs_nEs", ("initt", j)], [("init", j)])
                    V(lambda e, j=j: e.tensor_scalar(out=init[:, j, 3:4], in0=zi[:, j, TBS - 1:TBS], scalar1=cs[:, 16 + j:17 + j], scalar2=None, op0=ALU.mult),
                      [ZI, "cs_Ec"], [("initu", j)])
                    V(lambda e, j=j: e.scalar_tensor_tensor(out=init[:, j, 1:2], in0=zr[:, j, TBS - 1:TBS], scalar=cs[:, 20 + j:21 + j], in1=init[:, j, 3:4],
                                                            op0=ALU.mult, op1=ALU.add), [ZR, "cs_Es", ("initu", j)], [("init", j)])
                V(lambda e, j=j: e.tensor_tensor_scan(out=zr[:, j, :], data0=rT[:, j, :], data1=zin[:, 0, :], initial=init[:, j, 0:1], op0=ALU.mult, op1=ALU.add),
                  [("rT", j), ("zin", 0), ("init", j), "zr_all"], [ZR])
                V(lambda e, j=j: e.tensor_tensor_scan(out=zi[:, j, :], data0=rT[:, j, :], data1=zin[:, 1, :], initial=init[:, j, 1:2], op0=ALU.mult, op1=ALU.add),
                  [("rT", j), ("zin", 1), ("init", j)], [ZI])
                G(lambda e, j=j: e.tensor_tensor(out=pt[:, 0, :], in0=zr[:, j, :], in1=cosT[:, j, :], op=ALU.mult), [ZR, ("cosT", j)], [("pt", 0)])
                G(lambda e, j=j: e.tensor_tensor(out=pt[:, 1, :], in0=zi[:, j, :], in1=sinT[:, j, :], op=ALU.mult), [ZI, ("sinT", j)], [("pt", 1)])
                G(lambda e, sj=sj: e.tensor_tensor(out=sre[:, sj, :], in0=pt[:, 0, :], in1=pt[:, 1, :], op=ALU.subtract), [("pt", 0), ("pt", 1)], [("sre", sj)])
                G(lambda e, j=j: e.tensor_tensor(out=pt[:, 2, :], in0=zi[:, j, :], in1=cosT[:, j, :], op=ALU.mult), [ZI, ("cosT", j)], [("pt", 2)])
                G(lambda e, j=j: e.tensor_tensor(out=pt[:, 3, :], in0=zr[:, j, :], in1=sinT[:, j, :], op=ALU.mult), [ZR, ("sinT", j)], [("pt", 3)])
                G(lambda e, sj=sj: e.tensor_tensor(out=sim_[:, sj, :], in0=pt[:, 2, :], in1=pt[:, 3, :], op=ALU.add), [("pt", 2), ("pt", 3)], [("sim", sj)])
                A(lambda e: e.activation(out=sm[:, 5:6], in_=bank[6][:, 256:257], func=AF.Abs), [("bank", 6)], ["sm5"])
                V(lambda e, gch=gch: e.tensor_scalar(out=sm[:, 0:1], in0=sm[:, 5:6], scalar1=gsc[:, 4, gch:gch + 1], scalar2=None, op0=ALU.max),
                  ["sm5", "g4"], ["sm0"])
                V(lambda e: e.reciprocal(out=sm[:, 1:2], in_=sm[:, 0:1]), ["sm0"], ["sm1"])
                V(lambda e, ch=ch: e.scalar_tensor_tensor(out=hc4[:, ch, :], in0=bank[6][:, 0:256], scalar=sm[:, 1:2], in1=og[:, ch, :], op0=ALU.mult, op1=ALU.mult),
                  [("bank", 6), "sm1", ("og", ch)], [("hc", ch)])
                PE(lambda e, ch=ch: e.matmul(bank[1][:, 0:256], lhsT=kp[:, ch, 0:128], rhs=vext[:, ch, 0:256], start=True, stop=True),
                   [("kp", ch), ("vext", ch)], [("bank", 1, "kv")])
                PE(lambda e, ch=ch: e.matmul(bank[5][:, 128:384], lhsT=kp[:, ch, 128:256], rhs=vext[:, ch, 0:256], start=True, stop=True),
                   [("kp", ch), ("vext", ch)], [("bank", 5, "kv")])
                PE(lambda e, ch=ch: e.matmul(bank[5][:, 384:385], lhsT=kp[:, ch, 0:128], rhs=vext[:, ch, 256:257], start=True, stop=True),
                   [("kp", ch), "vones"], [("bank", 5, "n0")])
                PE(lambda e, ch=ch: e.matmul(bank[5][:, 385:386], lhsT=kp[:, ch, 128:256], rhs=vext[:, ch, 256:257], start=True, stop=True),
                   [("kp", ch), "vones"], [("bank", 5, "n1")])
                wcol = gsc[:, 6, gch:gch + 1]
                V(lambda e, wcol=wcol: e.scalar_tensor_tensor(out=Cst[:, 0, :], in0=Cst[:, 0, :], scalar=wcol, in1=bank[1][:, 0:256], op0=ALU.mult, op1=ALU.add),
                  ["Cst", "g6", ("bank", 1, "kv")], ["Cst"])
                V(lambda e, wcol=wcol: e.scalar_tensor_tensor(out=Cst[:, 1, :], in0=Cst[:, 1, :], scalar=wcol, in1=bank[5][:, 128:384], op0=ALU.mult, op1=ALU.add),
                  ["Cst", "g6", ("bank", 5, "kv")], ["Cst"])
                V(lambda e, wcol=wcol: e.scalar_tensor_tensor(out=nst[:, :], in0=nst[:, :], scalar=wcol, in1=bank[5][:, 384:386], op0=ALU.mult, op1=ALU.add),
                  ["nst", "g6", ("bank", 5, "n0"), ("bank", 5, "n1")], ["nst"])
                if gch + 1 < NCH:
                    wn = gsc[:, 6, gch + 1:gch + 2]
                    A(lambda e, wn=wn: e.activation(out=Cwb[:, :, 0:256], in_=Cst[:, :, :], func=AF.Copy, scale=wn), ["Cst", "g6"], ["Cwb"])
                    A(lambda e, wn=wn: e.activation(out=Cwb[:, :, 256], in_=nst[:, :], func=AF.Copy, scale=wn), ["nst", "g6"], ["Cwb"])
                PE(lambda e, j=j, sj=sj: e.matmul(bank[4][:, :], lhsT=CTb[:, 0, j, :], rhs=sre[:, sj, :], start=(j == 0), stop=False), ["CTb0", ("sre", sj)], [("bank", 4)])
                PE(lambda e, j=j, sj=sj: e.matmul(bank[4][:, :], lhsT=CTb[:, 1, j, :], rhs=sim_[:, sj, :], start=False, stop=(j == 3)), ["CTb1", ("sim", sj)], [("bank", 4)])
                A(lambda e, ch=ch: e.activation(out=hsq[:, :], in_=hc4[:, ch, :], func=AF.Square), [("hc", ch)], ["hsq"])
                V(lambda e, ch=ch: e.tensor_reduce(out=ssq[:, ch:ch + 1], in_=hsq[:, :], axis=AX.X, op=ALU.add), ["hsq"], [("ssq", ch)])
                if i < 3 and bi + 1 < nblk:
                    front(bi + 1, i)
            V(lambda e, s=s: e.tensor_tensor(out=yy[:, :], in0=bank[4][:, :], in1=du[:, s, :], op=ALU.add), [("bank", 4), ("du", s)], ["yy"])
            G(lambda e: e.tensor_tensor(out=g1[:, :], in0=yy[:, :], in1=yy[:, :], op=ALU.mult), ["yy"], ["g1t"])
            G(lambda e: e.tensor_scalar(out=g1[:, :], in0=g1[:, :], scalar1=0.044715, scalar2=1.0, op0=ALU.mult, op1=ALU.add), ["g1t"], ["g1t"])
            G(lambda e: e.tensor_tensor(out=g1[:, :], in0=g1[:, :], in1=yy[:, :], op=ALU.mult), ["g1t", "yy"], ["g1t"])
            A(lambda e: e.activation(out=g2[:, :], in_=g1[:, :], func=AF.Sigmoid, scale=1.5957691216057308), ["g1t"], ["g1t"])
            G(lambda e, s=s: e.tensor_tensor(out=zout[:, s, :], in0=yy[:, :], in1=g2[:, :], op=ALU.mult), ["yy", "g1t"], [("zout", s)])
            finals.append(P.dma("sync", lambda e, s=s, c0=c0: e.dma_start(out=zT_o[:, c0:c0 + TBS], in_=zout[:, s, :]), ("zst", s), reads=[("zout", s)]))
            V(lambda e: e.tensor_scalar(out=ssq[:, 4:8], in0=ssq[:, 0:4], scalar1=1.0 / 256.0, scalar2=EPS, op0=ALU.mult, op1=ALU.add),
              [("ssq", c) for c in range(4)], ["rs"])
            A(lambda e: e.activation(out=ssq[:, 4:8], in_=ssq[:, 4:8], func=AF.Sqrt), ["rs"], ["rs"])
            V(lambda e: e.reciprocal(out=ssq[:, 4:8], in_=ssq[:, 4:8]), ["rs"], ["rs"])
            for ch in range(4):
                cc = ch * 128
                hs = ch % 2
                A(lambda e, ch=ch, hs=hs: e.activation(out=hn[:, hs, :], in_=hc4[:, ch, :], func=AF.Copy, scale=ssq[:, 4 + ch:5 + ch]), [("hc", ch), "rs"], [("hn", hs)])
                for eo in range(2):
                    PE(lambda e, eo=eo, hs=hs: e.transpose(out=bankT[:, eo * 128:(eo + 1) * 128], in_=hn[:, hs, eo * 128:(eo + 1) * 128], identity=cb[:, :]),
                       [("hn", hs), "cb"], [("bankT", eo)])
                    V(lambda e, eo=eo, s=s, cc=cc: e.scalar_tensor_tensor(out=ymlo[:, s, eo, cc:cc + 128], in0=bankT[:, eo * 128:(eo + 1) * 128], scalar=vm[:, 2 + eo:3 + eo],
                                                                          in1=skx[:, s, eo, cc:cc + 128], op0=ALU.mult, op1=ALU.add),
                      [("bankT", eo), "vm", ("skx", s, eo)], [("ymlo", s)])
            finals.append(P.dma("sync", lambda e, s=s, c0=c0: e.dma_start(out=yv[:, :, c0:c0 + TBS], in_=ymlo[:, s, :, :]), ("yst", s), reads=[("ymlo", s)]))
        P.emit(final_waits=finals)
    return nc


def m_inputs(I, l, c, hT_full, ifp_full):
    b, r = c // 4, c % 4
    f32 = np.float32
    w_in = I["w_in"][l]
    w_m = np.concatenate([w_in[:, 128 * r:128 * r + 128], w_in[:, 512 + 256 * r:512 + 256 * r + 256],
                          w_in[:, 1536 + 256 * r:1536 + 256 * r + 256], w_in[:, 2560 + 256 * r:2560 + 256 * r + 256]], axis=1)
    ifg = np.stack([ifp_full[b][:, r], ifp_full[b][:, 4 + r]], axis=-1).reshape(NCH, 128, 2).transpose(1, 0, 2)
    convw = I["ml_conv_w"][l][:, 256 * r:256 * r + 256].reshape(4, 2, 128).transpose(2, 1, 0)
    vm = np.zeros((128, 16), f32)
    col2 = lambda v: v[256 * r:256 * r + 256].reshape(2, 128).T
    vm[:, 0:2] = col2(I["ml_conv_b"][l])
    vm[:, 2:4] = col2(I["ml_norm"][l])
    vm[:, 4:6] = col2(I["ml_skip"][l])
    vm[:, 6] = I["s5_d"][l][128 * r:128 * r + 128]
    vm[:, 8] = I["b_if"][l][r]
    vm[:, 9] = I["b_if"][l][4 + r]
    gs = slice(8 * r, 8 * r + 8)
    lam_re = I["s5_lam_re"][l][gs].reshape(512)
    lam_im = I["s5_lam_im"][l][gs].reshape(512)
    ldt = np.repeat(I["s5_log_dt"][l][gs], 64)
    lamC = np.zeros((128, 12), f32)
    lamC[:, 0:4] = lam_re.reshape(4, 128).T
    lamC[:, 4:8] = lam_im.reshape(4, 128).T
    lamC[:, 8:12] = ldt.reshape(4, 128).T
    lamR = np.broadcast_to(np.stack([lam_re, lam_im, ldt])[None], (128, 3, 512))
    BT = np.zeros((128, 2, 512), f32)
    CT = np.zeros((128, 2, 4, 128), f32)
    for gl in range(8):
        g = 8 * r + gl
        BT[16 * gl:16 * gl + 16, 0, gl * 64:(gl + 1) * 64] = I["s5_b_re"][l][g].T
        BT[16 * gl:16 * gl + 16, 1, gl * 64:(gl + 1) * 64] = I["s5_b_im"][l][g].T
        j, half = gl // 2, gl % 2
        CT[half * 64:(half + 1) * 64, 0, j, 16 * gl:16 * gl + 16] = I["s5_c_re"][l][g].T
        CT[half * 64:(half + 1) * 64, 1, j, 16 * gl:16 * gl + 16] = I["s5_c_im"][l][g].T
    return {"hT": hT_full[b], "ifg": np.ascontiguousarray(ifg, f32), "w_m": np.ascontiguousarray(w_m),
            "wq": I["ml_wq"][l][r], "wk": I["ml_wk"][l][r], "convw": np.ascontiguousarray(convw, f32), "vm": vm,
            "lamC": lamC, "lamR": np.ascontiguousarray(lamR, f32), "BT": BT, "CT": CT}


def m_consts():
    f32 = np.float32
    cf = np.zeros((128, 4, 128), f32)
    cf[:, 0, :] = np.eye(128)
    cf[:, 1, :] = np.triu(np.ones((128, 128)))
    cf[:, 2, :] = 1.0
    cf[127, 3, :] = 1.0
    iota = np.broadcast_to(np.arange(512, dtype=f32)[None], (128, 512))
    return {"cf32": cf, "cb16": np.eye(128).astype(ml_dtypes.bfloat16), "iota": np.ascontiguousarray(iota)}


def run_M(I, l, hT_sh, ifp_sh):
    hT_full = [np.concatenate([hT_sh[4 * b + r] for r in range(4)], axis=1) for b in range(2)]
    ifp_full = [np.concatenate([ifp_sh[4 * b + r] for r in range(4)], axis=0) for b in range(2)]
    cst = m_consts()
    in_maps = []
    for c in range(8):
        m = m_inputs(I, l, c, hT_full, ifp_full)
        m.update(cst)
        in_maps.append(m)
    nc = build_M()
    return _run(nc, in_maps)
```
